# Optimizing a Trainium2 kernel written in Bass

```python
import jax
import jax.numpy as jnp
from jax import lax
import numpy as np

D_MODEL = 1024
BATCH = 4
SEQ = 8192
DEPTH = 2

GRID_W = 64
CTX_LEN = 256
D_INNER = 2 * D_MODEL
HG_WIDTH = D_INNER // 2
HG_DK = 128
HG_HEADS = HG_WIDTH // HG_DK
HG_DV = HG_WIDTH // HG_HEADS
HG_CHUNK = 64
POOL_WIDTH = D_INNER - HG_WIDTH
POOL_WINDOWS = (2, 4, 8, 16)
POOL_GROUP = POOL_WIDTH // len(POOL_WINDOWS)
MLA_HEADS = 16
MLA_NOPE = 128
MLA_ROPE = 64
MLA_V = D_INNER // MLA_HEADS
MLA_Q_RANK = D_MODEL // 2
MLA_KV_RANK = D_MODEL // 4
MLA_SCALE = (MLA_NOPE + MLA_ROPE) ** -0.5
ROPE_FREQ = MLA_ROPE // 4
ROPE_BASE = 10000.0
Q_BLOCK = 128
EPS = 1e-6
N_EVEN = (DEPTH + 1) // 2
N_ODD = DEPTH // 2
EVEN_IN = 5 * HG_WIDTH + 2 * POOL_WIDTH
ODD_IN = MLA_Q_RANK + MLA_KV_RANK + MLA_ROPE + D_INNER

kernel_name = 'hybrid_hgrn2_pool_mla_prefix_dit'


def rms_norm(x, g):
    xf = x.astype(jnp.float32)
    y = xf * lax.rsqrt(jnp.mean(xf * xf, axis=-1, keepdims=True) + EPS)
    return (y * g.astype(jnp.float32)).astype(x.dtype)


def ada_modulation(cond, w, b):
    m = (jax.nn.silu(cond) @ w + b)[:, None, :]
    return jnp.split(m, 3, axis=-1)


def axial_rope_tables(n_tokens):
    rows = n_tokens // GRID_W
    pos_r = jnp.repeat(jnp.arange(rows), GRID_W).astype(jnp.float32)
    pos_c = jnp.tile(jnp.arange(GRID_W), rows).astype(jnp.float32)
    inv = ROPE_BASE ** (-2.0 * jnp.arange(ROPE_FREQ, dtype=jnp.float32) / (MLA_ROPE // 2))
    ang = jnp.stack([pos_r[:, None] * inv, pos_c[:, None] * inv], axis=1)
    return jnp.cos(ang), jnp.sin(ang)


def apply_axial_rope(x, cos, sin):
    shp = x.shape
    xr = x.astype(jnp.float32).reshape(shp[:-1] + (2, 2, ROPE_FREQ))
    x1, x2 = xr[..., 0, :], xr[..., 1, :]
    co, si = cos[None, :, None], sin[None, :, None]
    out = jnp.stack([x1 * co - x2 * si, x2 * co + x1 * si], axis=-2)
    return out.reshape(shp).astype(x.dtype)


def hgrn2_scan(q, k, v, logf, s0):
    out_dtype = v.dtype
    Bn, T, H, _ = q.shape
    n = T // HG_CHUNK

    def to_chunks(a):
        a = a.astype(jnp.float32).reshape(Bn, n, HG_CHUNK, H, a.shape[-1])
        return jnp.moveaxis(a, 1, 0)

    lower = jnp.tril(jnp.ones((HG_CHUNK, HG_CHUNK), bool))[None, :, :, None, None]

    def step(s, inp):
        qc, kc, vc, gc = inp
        b = jnp.cumsum(gc, axis=1)
        diff = jnp.where(lower, b[:, :, None] - b[:, None, :], -jnp.inf)
        att = jnp.sum(qc[:, :, None] * kc[:, None] * jnp.exp(diff), axis=-1)
        o = jnp.einsum('btsh,bshv->bthv', att, vc) + jnp.einsum('bthk,bhkv->bthv', qc * jnp.exp(b), s)
        b_last = b[:, -1]
        k_end = kc * jnp.exp(b_last[:, None] - b)
        s_new = jnp.exp(b_last)[..., None] * s + jnp.einsum('bthk,bthv->bhkv', k_end, vc)
        return s_new, o

    s_fin, o = lax.scan(step, s0, (to_chunks(q), to_chunks(k), to_chunks(v), to_chunks(logf)))
    o = jnp.moveaxis(o, 0, 1).reshape(Bn, T, H, v.shape[-1])
    return o.astype(out_dtype), s_fin


def hgrn2_branch(p, lb, s0f, s0b):
    Bn, T, _ = p.shape
    q, ff, fb, i, g = jnp.split(p, 5, axis=-1)
    heads = lambda a: a.reshape(Bn, T, HG_HEADS, HG_DK)
    q = jax.nn.silu(heads(q))
    i = heads(i)

    def decay(f_pre, lbd):
        f = lbd + (1.0 - lbd) * jax.nn.sigmoid(heads(f_pre).astype(jnp.float32))
        return 1.0 - f, jnp.log(f)

    kf, gf = decay(ff, lb[0])
    kb, gb = decay(fb, lb[1])
    o_f, sf = hgrn2_scan(q, kf, i, gf, s0f)
    o_b, sb = hgrn2_scan(q[:, ::-1], kb[:, ::-1], i[:, ::-1], gb[:, ::-1], s0b)
    return o_f + o_b[:, ::-1], g, sf, sb


def multiscale_pool(u):
    T = u.shape[1]
    uf = u.astype(jnp.float32)
    cs = jnp.concatenate([jnp.zeros_like(uf[:, :1]), jnp.cumsum(uf, axis=1)], axis=1)
    t = jnp.arange(T)
    outs = []
    for gi, w in enumerate(POOL_WINDOWS):
        sl = slice(gi * POOL_GROUP, (gi + 1) * POOL_GROUP)
        lo = jnp.clip(t - w // 2, 0, T)
        hi = jnp.clip(t + w // 2, 0, T)
        cnt = (hi - lo).astype(jnp.float32)[None, :, None]
        outs.append((cs[:, hi, sl] - cs[:, lo, sl]) / cnt - uf[..., sl])
    return jnp.concatenate(outs, axis=-1).astype(u.dtype)


def pool_branch(u, g, pool_w, pool_scale):
    Bn, T, _ = u.shape
    y = multiscale_pool(u).reshape(Bn, T, len(POOL_WINDOWS), POOL_GROUP)
    y = jnp.einsum('btgc,gcd->btgd', y, pool_w).reshape(Bn, T, POOL_WIDTH) * pool_scale
    return y * jax.nn.silu(g)


def even_mixer(zl, zc, in_w, lb, hg_norm_g, pool_w, pool_scale, need_ctx):
    def run(z, s0f, s0b, need_out):
        Bn, T, _ = z.shape
        p = z @ in_w
        o, ga, sf, sb = hgrn2_branch(p[..., :5 * HG_WIDTH], lb, s0f, s0b)
        if not need_out:
            return None, sf, sb
        a = rms_norm(o, hg_norm_g.reshape(HG_HEADS, HG_DV)).reshape(Bn, T, HG_WIDTH) * jax.nn.silu(ga)
        u = p[..., 5 * HG_WIDTH:5 * HG_WIDTH + POOL_WIDTH]
        gb = p[..., 5 * HG_WIDTH + POOL_WIDTH:]
        b = pool_branch(u, gb, pool_w, pool_scale)
        return jnp.concatenate([a, b], axis=-1), sf, sb

    s0 = jnp.zeros((zc.shape[0], HG_HEADS, HG_DK, HG_DV), jnp.float32)
    yc, sf, sb = run(zc, s0, s0, need_ctx)
    yl, _, _ = run(zl, sf, sb, True)
    return yl, yc


def mla_attend(qn, qr, kn, kr, v):
    s = jnp.einsum('bqhn,bkhn->bhqk', qn, kn) + jnp.einsum('bqhr,bkr->bhqk', qr, kr)
    p = jax.nn.softmax(s.astype(jnp.float32) * MLA_SCALE, axis=-1).astype(v.dtype)
    return jnp.einsum('bhqk,bkhv->bqhv', p, v)


def odd_mixer(zl, zc, in_w, qa_g, qb_w, kva_g, kvb_w, cos, sin, need_ctx):
    o1 = MLA_Q_RANK
    o2 = o1 + MLA_KV_RANK
    o3 = o2 + MLA_ROPE
    w = kvb_w.reshape(MLA_KV_RANK, MLA_HEADS, MLA_NOPE + MLA_V)
    w_uk, w_uv = w[..., :MLA_NOPE], w[..., MLA_NOPE:]
    Bn, T, _ = zl.shape
    Lc = zc.shape[1]

    pl = zl @ in_w
    cq_l = rms_norm(pl[..., :o1], qa_g)
    ckv_l = rms_norm(pl[..., o1:o2], kva_g)
    kr_l = apply_axial_rope(pl[..., o2:o3][:, :, None], cos, sin)[:, :, 0]
    g_l = pl[..., o3:]
    pc = zc @ in_w[:, o1:o3]
    ckv_c = rms_norm(pc[..., :MLA_KV_RANK], kva_g)
    kr_c = pc[..., MLA_KV_RANK:]

    ckv_all = jnp.concatenate([ckv_c, ckv_l], axis=1)
    kr_all = jnp.concatenate([kr_c, kr_l], axis=1)
    kn_all = jnp.einsum('blc,chn->blhn', ckv_all, w_uk)
    v_all = jnp.einsum('blc,chv->blhv', ckv_all, w_uv)

    def queries(cq):
        q = (cq @ qb_w).reshape(cq.shape[:2] + (MLA_HEADS, MLA_NOPE + MLA_ROPE))
        return q[..., :MLA_NOPE], q[..., MLA_NOPE:]

    nblk = T // Q_BLOCK

    def block(args):
        cq_b, cos_b, sin_b = args
        qn, qr = queries(cq_b)
        return mla_attend(qn, apply_axial_rope(qr, cos_b, sin_b), kn_all, kr_all, v_all)

    xs = (jnp.moveaxis(cq_l.reshape(Bn, nblk, Q_BLOCK, MLA_Q_RANK), 1, 0),
          cos.reshape(nblk, Q_BLOCK, 2, ROPE_FREQ), sin.reshape(nblk, Q_BLOCK, 2, ROPE_FREQ))
    ol = jnp.moveaxis(lax.map(block, xs), 0, 1).reshape(Bn, T, MLA_HEADS * MLA_V)
    yl = ol * jax.nn.silu(g_l)

    yc = None
    if need_ctx:
        pcq = zc @ in_w[:, :o1]
        gc = zc @ in_w[:, o3:]
        qn, qr = queries(rms_norm(pcq, qa_g))
        oc = mla_attend(qn, qr, kn_all[:, :Lc], kr_c, v_all[:, :Lc]).reshape(Bn, Lc, MLA_HEADS * MLA_V)
        yc = oc * jax.nn.silu(gc)
    return yl, yc


def setup_inputs(seed: int = 0) -> dict:
    key = jax.random.key(seed)
    ks = jax.random.split(key, 19)
    nrm = lambda k, shape, scale: jax.random.normal(k, shape, jnp.float32) * scale
    gain = lambda k, shape: 1.0 + 0.02 * jax.random.normal(k, shape, jnp.float32)
    return {
        'x': nrm(ks[0], (BATCH, SEQ, D_MODEL), 1.0),
        'c': nrm(ks[1], (BATCH, D_MODEL), 1.0),
        'ctx': nrm(ks[2], (BATCH, CTX_LEN, D_MODEL), 1.0),
        'c_ctx': nrm(ks[3], (D_MODEL,), 1.0),
        'ada_w': nrm(ks[4], (DEPTH, D_MODEL, 3 * D_MODEL), D_MODEL ** -0.5),
        'ada_b': nrm(ks[5], (DEPTH, 3 * D_MODEL), 0.01),
        'norm_g': gain(ks[6], (DEPTH, D_MODEL)),
        'out_w': nrm(ks[7], (DEPTH, D_INNER, D_MODEL), D_INNER ** -0.5),
        'ev_in_w': nrm(ks[8], (N_EVEN, D_MODEL, EVEN_IN), D_MODEL ** -0.5),
        'hg_lb': nrm(ks[9], (2, DEPTH + 1, HG_WIDTH), 0.1),
        'hg_norm_g': gain(ks[10], (N_EVEN, HG_WIDTH)),
        'pool_w': nrm(ks[11], (N_EVEN, len(POOL_WINDOWS), POOL_GROUP, POOL_GROUP), POOL_GROUP ** -0.5),
        'pool_scale': gain(ks[12], (N_EVEN, POOL_WIDTH)),
        'od_in_w': nrm(ks[13], (N_ODD, D_MODEL, ODD_IN), D_MODEL ** -0.5),
        'qa_norm_g': gain(ks[14], (N_ODD, MLA_Q_RANK)),
        'qb_w': nrm(ks[15], (N_ODD, MLA_Q_RANK, MLA_HEADS * (MLA_NOPE + MLA_ROPE)), MLA_Q_RANK ** -0.5),
        'kva_norm_g': gain(ks[16], (N_ODD, MLA_KV_RANK)),
        'kvb_w': nrm(ks[17], (N_ODD, MLA_KV_RANK, MLA_HEADS * (MLA_NOPE + MLA_V)), MLA_KV_RANK ** -0.5),
        'final_norm_g': gain(ks[18], (D_MODEL,)),
    }


def reference(x, c, ctx, c_ctx, ada_w, ada_b, norm_g, out_w, ev_in_w, hg_lb, hg_norm_g, pool_w,
              pool_scale, od_in_w, qa_norm_g, qb_w, kva_norm_g, kvb_w, final_norm_g):
    cos, sin = axial_rope_tables(x.shape[1])
    lb_all = jnp.cumsum(jax.nn.softmax(hg_lb.astype(jnp.float32), axis=1), axis=1)
    h, hc = x, ctx
    for layer in range(DEPTH):
        need_ctx = layer < DEPTH - 1
        sh_l, sc_l, gt_l = ada_modulation(c, ada_w[layer], ada_b[layer])
        sh_c, sc_c, gt_c = ada_modulation(c_ctx[None, :], ada_w[layer], ada_b[layer])
        zl = rms_norm(h, norm_g[layer]) * (1.0 + sc_l) + sh_l
        zc = rms_norm(hc, norm_g[layer]) * (1.0 + sc_c) + sh_c
        j = layer // 2
        if layer % 2 == 0:
            lb = lb_all[:, layer].reshape(2, HG_HEADS, HG_DK)
            yl, yc = even_mixer(zl, zc, ev_in_w[j], lb, hg_norm_g[j], pool_w[j], pool_scale[j], need_ctx)
        else:
            yl, yc = odd_mixer(zl, zc, od_in_w[j], qa_norm_g[j], qb_w[j], kva_norm_g[j], kvb_w[j],
                               cos, sin, need_ctx)
        h = h + gt_l * (yl @ out_w[layer])
        if need_ctx:
            hc = hc + gt_c * (yc @ out_w[layer])
    return rms_norm(h, final_norm_g)
```

```python
import numpy as np
from contextlib import ExitStack
import concourse.bass as bass
import concourse.mybir as mybir
from concourse.bass_utils import run_bass_kernel_spmd

F32 = mybir.dt.float32
BF16 = mybir.dt.bfloat16
AF = mybir.ActivationFunctionType
ALU = mybir.AluOpType
AX = mybir.AxisListType

D = 1024
T = 8192
TC = 256
NS = 66
NOWN = 32
NQ = 4096
NK = 8448
HEADS = 16
EPS = 1e-6
MLA_SCALE = 192.0 ** -0.5

ENGS = ("pe", "act", "dve", "pool", "sp")
NDMA = 12


class Tr:
    __slots__ = ("w", "r")

    def __init__(self):
        self.w = None
        self.r = {}


class Buf:
    __slots__ = ("t", "tr")

    def __init__(self, t):
        self.t = t
        self.tr = Tr()


class Prog:
    def __init__(self, nc):
        self.nc = nc
        self.es = ExitStack()
        self.ops = {e: [] for e in ENGS}
        self.cnt = {e: 0 for e in ENGS}
        self.waited = {e: {} for e in ENGS}
        self.sems = {}
        for e in ("pe", "act", "dve", "pool"):
            self.sems["c_" + e] = self.es.enter_context(nc.semaphore("c_" + e))
        self.dma_cnt = {}
        self.dma_rr = {}
        for q in ("sp", "act", "pool"):
            self.dma_rr[q] = 0
            for j in range(NDMA):
                k = "d_%s_%d" % (q, j)
                self.sems[k] = self.es.enter_context(nc.semaphore(k))
                self.dma_cnt[k] = 0
        self.final_events = {}
        self.uid = 0

    def buf(self, stack, name, shape, dtype):
        self.uid += 1
        return Buf(stack.enter_context(self.nc.sbuf_tensor("%s_%d" % (name, self.uid), list(shape), dtype)))

    def pbuf(self, stack, name, shape, dtype=F32):
        self.uid += 1
        return Buf(stack.enter_context(self.nc.psum_tensor("%s_%d" % (name, self.uid), list(shape), dtype)))

    def _deps(self, eng, reads, writes, pe_like, own=None):
        need = {}
        if own is None:
            own = "c_" + eng
        for t in reads:
            if t.w is not None:
                if t.w[0] == own and pe_like:
                    continue
                if need.get(t.w[0], 0) < t.w[1]:
                    need[t.w[0]] = t.w[1]
        for t in writes:
            if t.w is not None and t.w[0] != own:
                if need.get(t.w[0], 0) < t.w[1]:
                    need[t.w[0]] = t.w[1]
            for k, v in t.r.items():
                if k != own and need.get(k, 0) < v:
                    need[k] = v
        out = []
        wd = self.waited[eng]
        for k, v in need.items():
            if wd.get(k, 0) >= v:
                continue
            wd[k] = v
            out.append((k, v))
        return out

    def _commit(self, ev, reads, writes):
        k, v = ev
        for t in reads:
            if t.r.get(k, 0) < v:
                t.r[k] = v
        for t in writes:
            t.w = ev
            t.r = {}

    def op(self, eng, fn, reads=(), writes=()):
        waits = self._deps(eng, reads, writes, eng == "pe")
        self.cnt[eng] += 1
        ev = ("c_" + eng, self.cnt[eng])
        self.ops[eng].append((waits, fn, ("c_" + eng, 1)))
        self._commit(ev, reads, writes)

    def dma(self, q, fn, reads=(), writes=()):
        j = self.dma_rr[q]
        self.dma_rr[q] = (j + 1) % NDMA
        k = "d_%s_%d" % (q, j)
        waits = self._deps(q, reads, writes, False, own="__none__")
        prev = self.dma_cnt[k]
        if prev > 0 and self.waited[q].get(k, 0) < prev:
            self.waited[q][k] = prev
            waits.append((k, prev))
        self.dma_cnt[k] = prev + 16
        ev = (k, prev + 16)
        self.ops[q].append((waits, fn, (k, 16)))
        self._commit(ev, reads, writes)
        self.final_events[k] = prev + 16

    def barrier(self):
        allev = []
        for e in ("pe", "act", "dve", "pool"):
            if self.cnt[e] > 0:
                allev.append(("c_" + e, self.cnt[e]))
        for k, v in self.final_events.items():
            allev.append((k, v))
        for e in ENGS:
            w = []
            for k, v in allev:
                if k == "c_" + e:
                    continue
                if self.waited[e].get(k, 0) < v:
                    self.waited[e][k] = v
                    w.append((k, v))
            if w:
                self.ops[e].append((w, None, None))

    def emit_block(self):
        nc = self.nc
        sems = self.sems
        ops = self.ops
        self.ops = {e: [] for e in ENGS}
        with nc.Block() as block:
            def run(engname):
                def body(e):
                    for waits, fn, inc in ops[engname]:
                        for k, v in waits:
                            e.wait_ge(sems[k], v)
                        if fn is not None:
                            fn(e).then_inc(sems[inc[0]], inc[1])
                return body
            block.tensor(run("pe"))
            block.scalar(run("act"))
            block.vector(run("dve"))
            block.gpsimd(run("pool"))
            block.sync(run("sp"))

    def end_phase(self):
        self.barrier()
        self.emit_block()

    def close(self):
        self.es.close()


class K:
    pass


def mm(P, out, lhsT, rhs, start, stop, R, W):
    P.op("pe", lambda e: e.matmul(out, lhsT=lhsT, rhs=rhs, start=start, stop=stop), R, W)


def tp(P, out, in_, ident, R, W):
    P.op("pe", lambda e: e.transpose(out, in_, ident), R, W)


def act(P, out, in_, func, R, W, **kw):
    P.op("act", lambda e: e.activation(out=out, in_=in_, func=func, **kw), R, W)


def tt(P, eng, out, in0, in1, op, R, W):
    P.op(eng, lambda e: e.tensor_tensor(out=out, in0=in0, in1=in1, op=op), R, W)


def ts(P, eng, out, in0, s1, s2, op0, op1, R, W):
    if s2 is None:
        P.op(eng, lambda e: e.tensor_scalar(out=out, in0=in0, scalar1=s1, scalar2=None, op0=op0), R, W)
    else:
        P.op(eng, lambda e: e.tensor_scalar(out=out, in0=in0, scalar1=s1, scalar2=s2, op0=op0, op1=op1), R, W)


def stt(P, eng, out, in0, scalar, in1, op0, op1, R, W):
    P.op(eng, lambda e: e.scalar_tensor_tensor(out=out, in0=in0, scalar=scalar, in1=in1, op0=op0, op1=op1), R, W)


def cp(P, eng, out, in_, R, W):
    if eng == "act":
        P.op("act", lambda e: e.copy(out=out, in_=in_), R, W)
    else:
        P.op(eng, lambda e: e.tensor_copy(out=out, in_=in_), R, W)


def dma(P, q, out, in_, R, W):
    P.dma(q, lambda e: e.dma_start(out=out, in_=in_), R, W)


def rstd_from_ss(P, ss, tmp, n, R_extra=()):
    ts(P, "dve", tmp.t[:], ss.t[:], 1.0 / n, EPS, ALU.mult, ALU.add, [ss.tr], [tmp.tr])
    act(P, tmp.t[:], tmp.t[:], AF.Sqrt, [tmp.tr], [tmp.tr])
    P.op("dve", lambda e: e.reciprocal(out=ss.t[:], in_=tmp.t[:]), [tmp.tr], [ss.tr])


def build_program(debug=False, stop_after=99, lim=None):
    nc = bass.Bass("TRN2", target_bir_lowering=False)
    P = Prog(nc)
    k = K()
    k.nc, k.P = nc, P

    def din(name, shape, dt=F32):
        return nc.dram_tensor(name, list(shape), dt, kind="ExternalInput").ap()

    dbg = set(debug) if debug else set()

    def dscr(name, shape, dt):
        return nc.dram_tensor(name, list(shape), dt, kind=("ExternalOutput" if name in dbg else "Internal")).ap()

    xs = din("xs", [NS, 128, D])
    cvT = din("cvT", [128, 8, 2])
    ada_w = din("ada_w", [2, D, 3 * D])
    ada_b = din("ada_b", [2, 3 * D])
    norm_g = din("norm_g", [2, D])
    final_g = din("final_g", [1, D])
    w0 = din("w0", [D, 7 * D])
    lbraw = din("lbraw", [2, 3, D])
    hgn = din("hgn", [1, D])
    pool_w = din("pool_w", [4, 256, 256])
    pool_scale = din("pool_scale", [1, D])
    out_w = din("out_w", [2, 2 * D, D])
    w1 = din("w1", [D, 2944])
    qa_g = din("qa_g", [1, 512])
    kva_g = din("kva_g", [1, 256])
    qbw = din("qbw", [512, HEADS * 256])
    kvbw = din("kvbw", [256, HEADS * 256])
    ropeK_cos = din("ropeK_cos", [NS, 128, 64])
    ropeK_sin = din("ropeK_sin", [NS, 128, 64])
    ropeQ_cosT = din("ropeQ_cosT", [64, NQ])
    ropeQ_sinT = din("ropeQ_sinT", [64, NQ])
    ident_d = din("ident", [128, 128])
    tris_d = din("tris", [6, 128, 128])
    ci_d = din("ci", [128, 2])
    masks_d = din("masks", [2, 128, 64])
    bands_d = din("bands", [20, 128, 128])
    out_d = nc.dram_tensor("out", [NOWN, 128, D], F32, kind="ExternalOutput").ap()

    MS = dscr("MS", [2, 2, 3 * D], F32)
    ZT = dscr("ZT", [NS, 128, 8, 128], BF16)
    ODN = dscr("ODN", [NS, 128, D], F32)
    ABUF = dscr("ABUF", [NS, 128, D], BF16)
    H1 = dscr("H1", [NS, 128, D], F32)
    SGT = dscr("SGT", [HEADS, 128, NQ], F32)
    YT = dscr("YT", [HEADS, 128, NQ], BF16)
    t_MS, t_SGT = Tr(), Tr()
    t_ZT = [Tr() for _ in range(NS)]
    t_ODN = [Tr() for _ in range(NS)]
    t_ABUF = [Tr() for _ in range(NS)]
    t_H1 = [Tr() for _ in range(NS)]
    t_YT = [Tr() for _ in range(HEADS)]

    G = P.es
    identb = P.buf(G, "identb", [128, 128], BF16)
    identf = P.buf(G, "identf", [128, 128], F32)
    tri = P.buf(G, "tri", [128, 6, 128], F32)
    cib = P.buf(G, "cib", [128, 2], F32)
    maskb = P.buf(G, "maskb", [128, 2, 64], F32)
    dma(P, "pool", identb.t[:], ident_d, [], [identb.tr])
    dma(P, "sp", identf.t[:], ident_d, [], [identf.tr])
    dma(P, "sp", tri.t[:], tris_d.rearrange("m s t -> s m t"), [], [tri.tr])
    dma(P, "sp", cib.t[:], ci_d, [], [cib.tr])
    dma(P, "sp", maskb.t[:], masks_d.rearrange("m s t -> s m t"), [], [maskb.tr])

    with ExitStack() as S:
        cv = P.buf(S, "cv", [128, 8, 2], F32)
        sv = P.buf(S, "sv", [128, 8, 2], F32)
        aw = [P.buf(S, "aw%d" % i, [128, 8, 512], F32) for i in range(2)]
        ab = [P.buf(S, "ab%d" % i, [2, 512], F32) for i in range(2)]
        mrow = [P.buf(S, "mrow%d" % i, [2, 512], F32) for i in range(2)]
        pm = [P.pbuf(S, "pm%d" % i, [128, 512]) for i in range(2)]
        dma(P, "sp", cv.t[:], cvT, [], [cv.tr])
        act(P, sv.t[:], cv.t[:], AF.Silu, [cv.tr], [sv.tr])
        it = 0
        for l in range(2):
            for n in range(6):
                a_, b_, m_, p_ = aw[it % 2], ab[it % 2], mrow[it % 2], pm[it % 2]
                it += 1
                for kk in range(8):
                    dma(P, "sp" if kk % 2 == 0 else "act", a_.t[:, kk, :],
                        ada_w[l, kk * 128:(kk + 1) * 128, n * 512:(n + 1) * 512], [], [a_.tr])
                dma(P, "sp", b_.t[:], ada_b[l:l + 1, n * 512:(n + 1) * 512].partition_broadcast(2), [], [b_.tr])
                for kk in range(8):
                    mm(P, p_.t[0:2, :], sv.t[:, kk, :], a_.t[:, kk, :], kk == 0, kk == 7, [sv.tr, a_.tr], [p_.tr])
                tt(P, "dve", m_.t[:], p_.t[0:2, :], b_.t[:], ALU.add, [p_.tr, b_.tr], [m_.tr])
                dma(P, "sp", MS[l, :, n * 512:(n + 1) * 512], m_.t[:], [m_.tr], [t_MS])
        P.end_phase()
    if stop_after <= 0:
        return finish(k, nc, P)

    def load_bc(S, name, src_row, q="sp"):
        n = src_row.shape[-1]
        b = P.buf(S, name, [128, n], F32)
        dma(P, q, b.t[:], src_row.partition_broadcast(128), [t_MS], [b.tr])
        return b

    def mod_tiles(S, l, want_gate):
        ng = load_bc(S, "ng", norm_g[l:l + 1, :])
        res = []
        for r in range(2):
            sh = load_bc(S, "sh", MS[l, r:r + 1, 0:D])
            sc = load_bc(S, "sc", MS[l, r:r + 1, D:2 * D])
            stt(P, "dve", sc.t[:], sc.t[:], 1.0, ng.t[:], ALU.add, ALU.mult, [sc.tr, ng.tr], [sc.tr])
            gt = None
            if want_gate[r]:
                gt = load_bc(S, "gt", MS[l, r:r + 1, 2 * D:3 * D])
            res.append((sc, sh, gt))
        return res

    def make_z(xt, Gb, SHb, junk, ssb, tmpb, zb):
        act(P, junk.t[:], xt.t[:], AF.Square, [xt.tr], [junk.tr, ssb.tr], accum_out=ssb.t[:, 0:1])
        rstd_from_ss(P, ssb, tmpb, D)
        stt(P, "dve", xt.t[:], xt.t[:], ssb.t[:, 0:1], Gb.t[:], ALU.mult, ALU.mult, [xt.tr, ssb.tr, Gb.tr], [xt.tr])
        tt(P, "pool", zb.t[:], xt.t[:], SHb.t[:], ALU.add, [xt.tr, SHb.tr], [zb.tr])

    def transpose_to(psT, src, ncols, dstT, evac_eng):
        nb = ncols // 128
        for j in range(nb):
            tp(P, psT.t[:, j * 128:(j + 1) * 128], src.t[:, j * 128:(j + 1) * 128], identb.t[:],
               [src.tr, identb.tr], [psT.tr])
        cp(P, evac_eng, dstT.t[:].rearrange("p k t -> p (k t)"), psT.t[:, 0:ncols], [psT.tr], [dstT.tr])

    def load_w_bf16(dst, dcol, src, scol, ncols, nk=8):
        for kk in range(nk):
            for c0 in range(0, ncols, 512):
                c1 = min(ncols, c0 + 512)
                dma(P, "pool", dst.t[:, kk, dcol + c0:dcol + c1],
                    src[kk * 128:(kk + 1) * 128, scol + c0:scol + c1], [], [dst.tr])

    def hgrn_pass(dn):
        with ExitStack() as S:
            ncols = 3 * D if dn else 4 * D
            wres = P.buf(S, "wres", [128, 8, ncols], BF16)
            load_w_bf16(wres, 0, w0, 0, D)
            load_w_bf16(wres, D, w0, (2 * D if dn else D), D)
            load_w_bf16(wres, 2 * D, w0, 3 * D, D)
            if not dn:
                load_w_bf16(wres, 3 * D, w0, 4 * D, D)
            d_idx = 1 if dn else 0
            e3 = P.buf(S, "e3", [128, 3, D], F32)
            omlb = P.buf(S, "omlb", [128, D], F32)
            for j in range(3):
                dma(P, "sp", e3.t[:, j, :], lbraw[d_idx, j:j + 1, :].partition_broadcast(128), [], [e3.tr])
            act(P, e3.t[:], e3.t[:], AF.Exp, [e3.tr], [e3.tr])
            tt(P, "dve", omlb.t[:], e3.t[:, 1, :], e3.t[:, 2, :], ALU.add, [e3.tr], [omlb.tr])
            tt(P, "dve", e3.t[:, 2, :], omlb.t[:], e3.t[:, 0, :], ALU.add, [omlb.tr, e3.tr], [e3.tr])
            P.op("dve", lambda e: e.reciprocal(out=e3.t[:, 1, :], in_=e3.t[:, 2, :]), [e3.tr], [e3.tr])
            tt(P, "dve", omlb.t[:], omlb.t[:], e3.t[:, 1, :], ALU.mult, [omlb.tr, e3.tr], [omlb.tr])
            if dn:
                mods = mod_tiles(S, 0, (False, False))
            else:
                hgb = load_bc(S, "hgb", hgn[0:1, :])
            St = P.buf(S, "St", [128, 8, 128], F32)
            Sbf = P.buf(S, "Sbf", [128, 8, 128], BF16)
            P.op("pool", lambda e: e.memset(St.t[:], 0.0), [], [St.tr])
            P.op("pool", lambda e: e.memset(Sbf.t[:], 0.0), [], [Sbf.tr])
            xt = [P.buf(S, "xt", [128, D], F32) for _ in range(2)]
            junk = P.buf(S, "junk", [128, D], BF16)
            ssb = P.buf(S, "ssb", [128, 8], F32)
            tmpb = P.buf(S, "tmpb", [128, 8], F32)
            zb = P.buf(S, "zb", [128, D], BF16)
            zT = [P.buf(S, "zT", [128, 8, 128], BF16) for _ in range(2)]
            Qf = P.buf(S, "Qf", [128, D], F32)
            S2 = P.buf(S, "S2", [128, D], F32)
            Kf = P.buf(S, "Kf", [128, D], F32)
            Gl = P.buf(S, "Gl", [128, D], F32)
            V = P.buf(S, "V", [128, D], BF16)
            E = [P.buf(S, "E", [128, D], F32) for _ in range(2)]
            Qe = P.buf(S, "Qe", [128, D], BF16)
            Ke = P.buf(S, "Ke", [128, D], BF16)
            Qb = P.buf(S, "Qb", [128, D], BF16)
            Kend = P.buf(S, "Kend", [128, D], BF16)
            QeT = P.buf(S, "QeT", [128, 8, 128], BF16)
            KeT = P.buf(S, "KeT", [128, 8, 128], BF16)
            QbT = P.buf(S, "QbT", [128, 8, 128], BF16)
            Asb = P.buf(S, "Asb", [128, 8, 64], BF16)
            eb = P.buf(S, "eb", [128, 8, 2], F32)
            osb = [P.buf(S, "osb", [128, D], F32) for _ in range(2)]
            if not dn:
                SGA = P.buf(S, "SGA", [128, D], F32)
                odn = [P.buf(S, "odn", [128, D], F32) for _ in range(2)]
                abf = [P.buf(S, "abf", [128, D], BF16) for _ in range(2)]
            psA = P.pbuf(S, "psA", [128, D])
            psB = P.pbuf(S, "psB", [128, D])
            psC = P.pbuf(S, "psC", [128, D])
            psT = [P.pbuf(S, "psT", [128, D], BF16) for _ in range(2)]
            tb = 3 if dn else 0
            order = [1, 0] + list(range(NS - 1, 1, -1)) if dn else list(range(NS))
            corder = (1, 0) if dn else (0, 1)
            if lim is not None:
                order = order[:lim]
            for it, s in enumerate(order):
                z_ = zT[it % 2]
                if dn:
                    x_ = xt[it % 2]
                    r = 1 if s < 2 else 0
                    dma(P, "sp", x_.t[:], xs[s], [], [x_.tr])
                    make_z(x_, mods[r][0], mods[r][1], junk, ssb, tmpb, zb)
                    transpose_to(psT[0], zb, D, z_, "act")
                    dma(P, "sp", ZT[s], z_.t[:], [z_.tr], [t_ZT[s]])
                else:
                    dma(P, "sp", z_.t[:], ZT[s], [t_ZT[s]], [z_.tr])
                    o_ = odn[it % 2]
                    dma(P, "act", o_.t[:], ODN[s], [t_ODN[s]], [o_.tr])
                for ps, col in ((psA, 0), (psB, D), (psC, 2 * D)):
                    for n in range(2):
                        for kk in range(8):
                            mm(P, ps.t[:, n * 512:(n + 1) * 512], z_.t[:, kk, :],
                               wres.t[:, kk, col + n * 512:col + (n + 1) * 512], kk == 0, kk == 7,
                               [z_.tr, wres.tr], [ps.tr])
                act(P, Qf.t[:], psA.t[:], AF.Silu, [psA.tr], [Qf.tr])
                act(P, S2.t[:], psB.t[:], AF.Sigmoid, [psB.tr], [S2.tr], scale=-1.0)
                cp(P, "dve", V.t[:], psC.t[:], [psC.tr], [V.tr])
                tt(P, "dve", Kf.t[:], S2.t[:], omlb.t[:], ALU.mult, [S2.tr, omlb.tr], [Kf.tr])
                act(P, Gl.t[:], Kf.t[:], AF.Ln, [Kf.tr], [Gl.tr], scale=-1.0, bias=1.0)
                for ps, ti in ((psA, tb + 1), (psB, tb + 0), (psC, tb + 2)):
                    for n in range(2):
                        mm(P, ps.t[:, n * 512:(n + 1) * 512], tri.t[:, ti, :], Gl.t[:, n * 512:(n + 1) * 512],
                           True, True, [tri.tr, Gl.tr], [ps.tr])
                act(P, E[0].t[:], psA.t[:], AF.Exp, [psA.tr], [E[0].tr])
                act(P, E[1].t[:], psA.t[:], AF.Exp, [psA.tr], [E[1].tr], scale=-1.0)
                tt(P, "dve", Qe.t[:], Qf.t[:], E[0].t[:], ALU.mult, [Qf.tr, E[0].tr], [Qe.tr])
                tt(P, "pool", Ke.t[:], Kf.t[:], E[1].t[:], ALU.mult, [Kf.tr, E[1].tr], [Ke.tr])
                act(P, E[0].t[:], psB.t[:], AF.Exp, [psB.tr], [E[0].tr])
                act(P, E[1].t[:], psC.t[:], AF.Exp, [psC.tr], [E[1].tr])
                tt(P, "dve", Qb.t[:], Qf.t[:], E[0].t[:], ALU.mult, [Qf.tr, E[0].tr], [Qb.tr])
                tt(P, "pool", Kend.t[:], Kf.t[:], E[1].t[:], ALU.mult, [Kf.tr, E[1].tr], [Kend.tr])
                if not dn:
                    for n in range(2):
                        for kk in range(8):
                            mm(P, psC.t[:, n * 512:(n + 1) * 512], z_.t[:, kk, :],
                               wres.t[:, kk, 3 * D + n * 512:3 * D + (n + 1) * 512], kk == 0, kk == 7,
                               [z_.tr, wres.tr], [psC.tr])
                    act(P, SGA.t[:], psC.t[:], AF.Silu, [psC.tr], [SGA.tr])
                transpose_to(psT[0], Qe, D, QeT, "act")
                transpose_to(psT[1], Ke, D, KeT, "dve")
                transpose_to(psT[0], Qb, D, QbT, "act")
                for h in range(8):
                    mm(P, psA.t[:, 512 + 2 * h:514 + 2 * h], Gl.t[:, h * 128:(h + 1) * 128], cib.t[:], True, True,
                       [Gl.tr, cib.tr], [psA.tr])
                for h in range(8):
                    for c in range(2):
                        mm(P, psA.t[c * 64:(c + 1) * 64, h * 64:(h + 1) * 64], KeT.t[:, h, c * 64:(c + 1) * 64],
                           QeT.t[:, h, c * 64:(c + 1) * 64], True, True, [KeT.tr, QeT.tr], [psA.tr])
                act(P, eb.t[:].rearrange("p h c -> p (h c)"), psA.t[:, 512:528], AF.Exp, [psA.tr], [eb.tr])
                tt(P, "dve", Asb.t[:], psA.t[:, 0:512].rearrange("p (h t) -> p h t", h=8),
                   maskb.t[:, (1 if dn else 0), :].unsqueeze(1).to_broadcast([128, 8, 64]), ALU.mult,
                   [psA.tr, maskb.tr], [Asb.tr])
                for c in corder:
                    r0, r1 = c * 64, (c + 1) * 64
                    for h in range(8):
                        hs = slice(h * 128, (h + 1) * 128)
                        mm(P, psB.t[r0:r1, hs], Asb.t[r0:r1, h, :], V.t[r0:r1, hs], True, False,
                           [Asb.tr, V.tr], [psB.tr])
                        mm(P, psB.t[r0:r1, hs], QbT.t[:, h, r0:r1], Sbf.t[:, h, :], False, True,
                           [QbT.tr, Sbf.tr], [psB.tr])
                    for h in range(8):
                        hs = slice(h * 128, (h + 1) * 128)
                        mm(P, psC.t[:, hs], Kend.t[r0:r1, hs], V.t[r0:r1, hs], True, True,
                           [Kend.tr, V.tr], [psC.tr])
                    tt(P, "dve", St.t[:], St.t[:], eb.t[:, :, c:c + 1].to_broadcast([128, 8, 128]), ALU.mult,
                       [St.tr, eb.tr], [St.tr])
                    tt(P, "dve", St.t[:].rearrange("p h v -> p (h v)"), St.t[:].rearrange("p h v -> p (h v)"),
                       psC.t[:], ALU.add, [St.tr, psC.tr], [St.tr])
                    cp(P, "act", Sbf.t[:], St.t[:], [St.tr], [Sbf.tr])
                if dn:
                    ob = osb[it % 2]
                    cp(P, "act", ob.t[:], psB.t[:], [psB.tr], [ob.tr])
                    dma(P, "sp", ODN[s], ob.t[:], [ob.tr], [t_ODN[s]])
                else:
                    ob = osb[it % 2]
                    a_ = abf[it % 2]
                    tt(P, "dve", ob.t[:], psB.t[:], o_.t[:], ALU.add, [psB.tr, o_.tr], [ob.tr])
                    tt(P, "pool", o_.t[:], ob.t[:], ob.t[:], ALU.mult, [ob.tr], [o_.tr])
                    P.op("dve", lambda e, o_=o_: e.tensor_reduce(out=ssb.t[:], in_=o_.t[:].rearrange("p (h v) -> p h v", h=8),
                                                         axis=AX.X, op=ALU.add), [o_.tr], [ssb.tr])
                    rstd_from_ss(P, ssb, tmpb, 128)
                    tt(P, "dve", ob.t[:].rearrange("p (h v) -> p h v", h=8), ob.t[:].rearrange("p (h v) -> p h v", h=8),
                       ssb.t[:].unsqueeze(2).to_broadcast([128, 8, 128]), ALU.mult, [ob.tr, ssb.tr], [ob.tr])
                    tt(P, "pool", ob.t[:], ob.t[:], hgb.t[:], ALU.mult, [ob.tr, hgb.tr], [ob.tr])
                    tt(P, "dve", a_.t[:], ob.t[:], SGA.t[:], ALU.mult, [ob.tr, SGA.tr], [a_.tr])
                    dma(P, "sp", ABUF[s], a_.t[:], [a_.tr], [t_ABUF[s]])
            P.end_phase()

    hgrn_pass(True)
    if stop_after <= 1:
        return finish(k, nc, P)
    hgrn_pass(False)
    if stop_after <= 2:
        return finish(k, nc, P)

    with ExitStack() as S:
        wres = P.buf(S, "wres3", [128, 8, 2 * D], BF16)
        load_w_bf16(wres, 0, w0, 5 * D, 2 * D)
        ow = P.buf(S, "ow0", [128, 16, D], BF16)
        load_w_bf16(ow, 0, out_w[0], 0, D, nk=16)
        pw = P.buf(S, "pw", [128, 4, 2, 256], BF16)
        for g in range(4):
            for j in range(2):
                dma(P, "pool", pw.t[:, g, j, :], pool_w[g, j * 128:(j + 1) * 128, :], [], [pw.tr])
        bands = P.buf(S, "bands", [128, 20, 128], BF16)
        for j in range(20):
            dma(P, "pool", bands.t[:, j, :], bands_d[j], [], [bands.tr])
        psc = load_bc(S, "psc", pool_scale[0:1, :])
        gts = []
        for r in range(2):
            gts.append(load_bc(S, "gt0", MS[0, r:r + 1, 2 * D:3 * D]))
        zT = [P.buf(S, "zT3", [128, 8, 128], BF16) for _ in range(2)]
        U = [P.buf(S, "U", [128, D], BF16) for _ in range(3)]
        SGB = [P.buf(S, "SGB", [128, D], F32) for _ in range(2)]
        ppT = P.buf(S, "ppT", [128, 8, 128], BF16)
        yb = P.buf(S, "yb", [128, 2 * D], BF16)
        yT = P.buf(S, "yT", [128, 16, 128], BF16)
        bt = P.buf(S, "bt", [128, D], F32)
        xt = [P.buf(S, "xt3", [128, D], F32) for _ in range(2)]
        psA = P.pbuf(S, "ps3A", [128, D])
        psB = P.pbuf(S, "ps3B", [128, D])
        psC = P.pbuf(S, "ps3C", [128, D])
        psT = [P.pbuf(S, "ps3T", [128, D], BF16) for _ in range(2)]

        def seq_info(s):
            return (0, 1) if s < 2 else (2, NS - 1)

        def post(s, itp):
            lo, hi = seq_info(s)
            var = 0 if s == lo else (2 if s == hi else 1)
            for cb in range(8):
                g = cb // 2
                terms = []
                if s > lo:
                    terms.append((U[(s - 1) % 3], g * 5 + 3))
                terms.append((U[s % 3], g * 5 + var))
                if s < hi:
                    terms.append((U[(s + 1) % 3], g * 5 + 4))
                for i, (ub, bi) in enumerate(terms):
                    mm(P, psC.t[:, cb * 128:(cb + 1) * 128], ub.t[:, cb * 128:(cb + 1) * 128], bands.t[:, bi, :],
                       i == 0, i == len(terms) - 1, [ub.tr, bands.tr], [psC.tr])
            cp(P, "act", ppT.t[:].rearrange("p k t -> p (k t)"), psC.t[:], [psC.tr], [ppT.tr])
            for g in range(4):
                for j in range(2):
                    mm(P, psA.t[:, g * 256:(g + 1) * 256], ppT.t[:, 2 * g + j, :], pw.t[:, g, j, :], j == 0, j == 1,
                       [ppT.tr, pw.tr], [psA.tr])
            sg = SGB[s % 2]
            tt(P, "dve", bt.t[:], psA.t[:], psc.t[:], ALU.mult, [psA.tr, psc.tr], [bt.tr])
            tt(P, "pool", yb.t[:, D:2 * D], bt.t[:], sg.t[:], ALU.mult, [bt.tr, sg.tr], [yb.tr])
            dma(P, "act", yb.t[:, 0:D], ABUF[s], [t_ABUF[s]], [yb.tr])
            for j in range(16):
                tp(P, psT[j // 8].t[:, (j % 8) * 128:(j % 8 + 1) * 128], yb.t[:, j * 128:(j + 1) * 128], identb.t[:],
                   [yb.tr, identb.tr], [psT[j // 8].tr])
            cp(P, "act", yT.t[:, 0:8, :].rearrange("p k t -> p (k t)"), psT[0].t[:], [psT[0].tr], [yT.tr])
            cp(P, "dve", yT.t[:, 8:16, :].rearrange("p k t -> p (k t)"), psT[1].t[:], [psT[1].tr], [yT.tr])
            for n in range(2):
                for kk in range(16):
                    mm(P, psB.t[:, n * 512:(n + 1) * 512], yT.t[:, kk, :], ow.t[:, kk, n * 512:(n + 1) * 512],
                       kk == 0, kk == 15, [yT.tr, ow.tr], [psB.tr])
            x_ = xt[itp % 2]
            dma(P, "sp", x_.t[:], xs[s], [], [x_.tr])
            gt = gts[1 if s < 2 else 0]
            tt(P, "dve", bt.t[:], psB.t[:], gt.t[:], ALU.mult, [psB.tr, gt.tr], [bt.tr])
            tt(P, "pool", x_.t[:], x_.t[:], bt.t[:], ALU.add, [x_.tr, bt.tr], [x_.tr])
            dma(P, "sp", H1[s], x_.t[:], [x_.tr], [t_H1[s]])

        itp = 0
        for s in range(NS):
            z_ = zT[s % 2]
            dma(P, "sp", z_.t[:], ZT[s], [t_ZT[s]], [z_.tr])
            for ps, col in ((psA, 0), (psB, D)):
                for n in range(2):
                    for kk in range(8):
                        mm(P, ps.t[:, n * 512:(n + 1) * 512], z_.t[:, kk, :],
                           wres.t[:, kk, col + n * 512:col + (n + 1) * 512], kk == 0, kk == 7,
                           [z_.tr, wres.tr], [ps.tr])
            cp(P, "dve", U[s % 3].t[:], psA.t[:], [psA.tr], [U[s % 3].tr])
            act(P, SGB[s % 2].t[:], psB.t[:], AF.Silu, [psB.tr], [SGB[s % 2].tr])
            lo, hi = seq_info(s)
            if s > lo:
                post(s - 1, itp)
                itp += 1
            if s == hi:
                post(s, itp)
                itp += 1
        P.end_phase()
    if stop_after <= 3:
        return finish(k, nc, P)

    cqT = P.buf(G, "cqT", [128, 4, NQ], BF16)
    ckvT = P.buf(G, "ckvT", [128, 2, NK], BF16)
    krT = P.buf(G, "krT", [64, NK], BF16)
    t_cq = [Tr() for _ in range(8)]
    t_kv = [Tr() for _ in range(17)]

    with ExitStack() as S:
        wres = P.buf(S, "wres4", [128, 8, 2944], BF16)
        load_w_bf16(wres, 0, w1, 0, 2944)
        mods = mod_tiles(S, 1, (False, False))
        qag = load_bc(S, "qag", qa_g[0:1, :])
        kvag = load_bc(S, "kvag", kva_g[0:1, :])
        xt = [P.buf(S, "xt4", [128, D], F32) for _ in range(2)]
        junk = P.buf(S, "junk4", [128, D], BF16)
        ssb = P.buf(S, "ssb4", [128, 8], F32)
        tmpb = P.buf(S, "tmpb4", [128, 8], F32)
        zb = P.buf(S, "zb4", [128, D], BF16)
        zT = [P.buf(S, "zT4", [128, 8, 128], BF16) for _ in range(2)]
        kvb = P.buf(S, "kvb", [128, 256], BF16)
        rc = [P.buf(S, "rc", [128, 64], F32) for _ in range(2)]
        rs = [P.buf(S, "rs", [128, 64], F32) for _ in range(2)]
        krf = P.buf(S, "krf", [128, 64], F32)
        krf2 = P.buf(S, "krf2", [128, 64], F32)
        krb = P.buf(S, "krb", [128, 64], BF16)
        cqb = P.buf(S, "cqb", [128, 512], BF16)
        sgf = P.buf(S, "sgf", [128, 2 * D], F32)
        sgT = [P.buf(S, "sgT", [128, 16, 128], F32) for _ in range(1)]
        psK = P.pbuf(S, "ps4K", [128, 512])
        psQ = P.pbuf(S, "ps4Q", [128, 512])
        psG = [P.pbuf(S, "ps4G", [128, D]) for _ in range(2)]
        psT = P.pbuf(S, "ps4T", [128, D], BF16)
        psF = P.pbuf(S, "ps4F", [128, 512])
        for s in range(NS):
            x_ = xt[s % 2]
            z_ = zT[s % 2]
            own = 2 <= s < 2 + NOWN
            r = 1 if s < 2 else 0
            dma(P, "sp", x_.t[:], H1[s], [t_H1[s]], [x_.tr])
            dma(P, "act", rc[s % 2].t[:], ropeK_cos[s], [], [rc[s % 2].tr])
            dma(P, "act", rs[s % 2].t[:], ropeK_sin[s], [], [rs[s % 2].tr])
            make_z(x_, mods[r][0], mods[r][1], junk, ssb, tmpb, zb)
            transpose_to(psT, zb, D, z_, "act")
            for kk in range(8):
                mm(P, psK.t[:, 0:384], z_.t[:, kk, :], wres.t[:, kk, 512:896], kk == 0, kk == 7,
                   [z_.tr, wres.tr], [psK.tr])
            act(P, junk.t[:, 0:256], psK.t[:, 0:256], AF.Square, [psK.tr], [junk.tr, ssb.tr], accum_out=ssb.t[:, 0:1])
            rstd_from_ss(P, ssb, tmpb, 256)
            stt(P, "dve", kvb.t[:], psK.t[:, 0:256], ssb.t[:, 0:1], kvag.t[:], ALU.mult, ALU.mult,
                [psK.tr, ssb.tr, kvag.tr], [kvb.tr])
            tt(P, "dve", krf.t[:], psK.t[:, 256:320], rc[s % 2].t[:], ALU.mult, [psK.tr, rc[s % 2].tr], [krf.tr])
            tt(P, "dve", krf2.t[:], psK.t[:, 320:384], rs[s % 2].t[:], ALU.mult, [psK.tr, rs[s % 2].tr], [krf2.tr])
            tt(P, "pool", krb.t[:], krf.t[:], krf2.t[:], ALU.add, [krf.tr, krf2.tr], [krb.tr])
            kc_tr = t_kv[s // 4]
            for j in range(2):
                tp(P, psT.t[:, j * 128:(j + 1) * 128], kvb.t[:, j * 128:(j + 1) * 128], identb.t[:],
                   [kvb.tr, identb.tr], [psT.tr])
            tp(P, psT.t[0:64, 256:384], krb.t[:], identb.t[:], [krb.tr, identb.tr], [psT.tr])
            for j in range(2):
                cp(P, "act", ckvT.t[:, j, s * 128:(s + 1) * 128], psT.t[:, j * 128:(j + 1) * 128], [psT.tr], [kc_tr])
            cp(P, "dve", krT.t[:, s * 128:(s + 1) * 128], psT.t[0:64, 256:384], [psT.tr], [kc_tr])
            if own:
                q0 = (s - 2) * 128
                for kk in range(8):
                    mm(P, psQ.t[:], z_.t[:, kk, :], wres.t[:, kk, 0:512], kk == 0, kk == 7, [z_.tr, wres.tr], [psQ.tr])
                act(P, junk.t[:, 0:512], psQ.t[:], AF.Square, [psQ.tr], [junk.tr, ssb.tr], accum_out=ssb.t[:, 1:2])
                ts(P, "dve", tmpb.t[:, 1:2], ssb.t[:, 1:2], 1.0 / 512, EPS, ALU.mult, ALU.add, [ssb.tr], [tmpb.tr])
                act(P, tmpb.t[:, 1:2], tmpb.t[:, 1:2], AF.Sqrt, [tmpb.tr], [tmpb.tr])
                P.op("dve", lambda e: e.reciprocal(out=ssb.t[:, 1:2], in_=tmpb.t[:, 1:2]), [tmpb.tr], [ssb.tr])
                stt(P, "dve", cqb.t[:], psQ.t[:], ssb.t[:, 1:2], qag.t[:], ALU.mult, ALU.mult,
                    [psQ.tr, ssb.tr, qag.tr], [cqb.tr])
                for j in range(4):
                    tp(P, psT.t[:, 512 + j * 128:512 + (j + 1) * 128], cqb.t[:, j * 128:(j + 1) * 128], identb.t[:],
                       [cqb.tr, identb.tr], [psT.tr])
                for j in range(4):
                    cp(P, "act" if j % 2 == 0 else "dve", cqT.t[:, j, q0:q0 + 128],
                       psT.t[:, 512 + j * 128:512 + (j + 1) * 128], [psT.tr], [t_cq[(s - 2) // 4]])
                for half in range(2):
                    pg = psG[half]
                    for n in range(2):
                        c0 = 896 + half * D + n * 512
                        for kk in range(8):
                            mm(P, pg.t[:, n * 512:(n + 1) * 512], z_.t[:, kk, :], wres.t[:, kk, c0:c0 + 512],
                               kk == 0, kk == 7, [z_.tr, wres.tr], [pg.tr])
                    act(P, sgf.t[:, half * D:(half + 1) * D], pg.t[:], AF.Silu, [pg.tr], [sgf.tr])
                st_ = sgT[0]
                for hq in range(4):
                    for j in range(4):
                        h = hq * 4 + j
                        tp(P, psF.t[:, j * 128:(j + 1) * 128], sgf.t[:, h * 128:(h + 1) * 128], identf.t[:],
                           [sgf.tr, identf.tr], [psF.tr])
                    cp(P, "dve" if hq % 2 == 0 else "act", st_.t[:, hq * 4:(hq + 1) * 4, :].rearrange("p h t -> p (h t)"),
                       psF.t[:], [psF.tr], [st_.tr])
                dma(P, "sp", SGT[:, :, q0:q0 + 128].rearrange("h v q -> v h q"), st_.t[:], [st_.tr], [t_SGT])
        P.end_phase()
    if stop_after <= 4:
        return finish(k, nc, P)

    with ExitStack() as S:
        qw = [P.buf(S, "qw", [128, 4, 256], BF16) for _ in range(2)]
        kw = [P.buf(S, "kw", [128, 2, 256], BF16) for _ in range(2)]
        KnT = P.buf(S, "KnT", [128, NK], BF16)
        Vh = P.buf(S, "Vh", [128, NS, 128], BF16)
        QnT = P.buf(S, "QnT", [128, NQ], BF16)
        QrT = P.buf(S, "QrT", [64, NQ], BF16)
        sgt = P.buf(S, "sgt", [128, NQ], F32)
        rq = [[P.buf(S, "rq", [64, 512], F32) for _ in range(2)] for _ in range(2)]
        q1 = P.buf(S, "q1", [64, 512], F32)
        q2 = P.buf(S, "q2", [64, 512], F32)
        PT = [P.buf(S, "PT", [128, 1024], BF16) for _ in range(3)]
        laccA = P.buf(S, "laccA", [128, 1024], F32)
        laccB = P.buf(S, "laccB", [128, 1024], F32)
        onesf = P.buf(S, "onesf", [128, 128], F32)
        rl = P.buf(S, "rl", [128, 512], F32)
        yo = [P.buf(S, "yo", [128, 512], BF16) for _ in range(2)]
        yf = P.buf(S, "yf", [128, 512], F32)
        P.op("pool", lambda e: e.memset(onesf.t[:], 1.0), [], [onesf.tr])
        psS = [P.pbuf(S, "ps5S", [128, 1024]) for _ in range(3)]
        psO = P.pbuf(S, "ps5O", [128, 512])
        psL = P.pbuf(S, "ps5L", [128, 512])

        class _PX:
            def __init__(self, b, c0):
                self.tr = b.tr
                self.t = b.t[:, c0:c0 + 512]
        psX = [_PX(psS[i], j * 512) for j in range(2) for i in range(3)]
        t_kn = [Tr() for _ in range(17)]
        t_v = [Tr() for _ in range(17)]
        t_qn = [Tr() for _ in range(8)]
        ix = 0
        ip = 0
        for h in range(HEADS):
            qw_, kw_ = qw[h % 2], kw[h % 2]
            for kk in range(4):
                dma(P, "pool", qw_.t[:, kk, :], qbw[kk * 128:(kk + 1) * 128, h * 256:(h + 1) * 256], [], [qw_.tr])
            for kk in range(2):
                dma(P, "pool", kw_.t[:, kk, :], kvbw[kk * 128:(kk + 1) * 128, h * 256:(h + 1) * 256], [], [kw_.tr])
            dma(P, "sp", sgt.t[:], SGT[h], [t_SGT], [sgt.tr])
            for kc in range(17):
                c0 = kc * 512
                cn = min(512, NK - c0)
                px = psX[ix % 6]
                ix += 1
                for kk in range(2):
                    mm(P, px.t[:, 0:cn], kw_.t[:, kk, 0:128], ckvT.t[:, kk, c0:c0 + cn], kk == 0, kk == 1,
                       [kw_.tr, t_kv[kc]], [px.tr])
                cp(P, "dve" if kc % 2 == 0 else "pool" if False else "dve", KnT.t[:, c0:c0 + cn], px.t[:, 0:cn],
                   [px.tr], [t_kn[kc]])
                px = psX[ix % 6]
                ix += 1
                nt = cn // 128
                for j in range(nt):
                    s = kc * 4 + j
                    for kk in range(2):
                        mm(P, px.t[:, j * 128:(j + 1) * 128], ckvT.t[:, kk, s * 128:(s + 1) * 128], kw_.t[:, kk, 128:256],
                           kk == 0, kk == 1, [kw_.tr, t_kv[kc]], [px.tr])
                cp(P, "act", Vh.t[:, kc * 4:kc * 4 + nt, :].rearrange("p s v -> p (s v)"), px.t[:, 0:cn],
                   [px.tr], [t_v[kc]])
            for qc in range(8):
                c0 = qc * 512
                px = psX[ix % 6]
                ix += 1
                for kk in range(4):
                    mm(P, px.t[:], qw_.t[:, kk, 0:128], cqT.t[:, kk, c0:c0 + 512], kk == 0, kk == 3,
                       [qw_.tr, t_cq[qc]], [px.tr])
                cp(P, "dve", QnT.t[:, c0:c0 + 512], px.t[:], [px.tr], [t_qn[qc]])
                rq_ = rq[qc % 2]
                dma(P, "sp", rq_[0].t[:], ropeQ_cosT[:, c0:c0 + 512], [], [rq_[0].tr])
                dma(P, "sp", rq_[1].t[:], ropeQ_sinT[:, c0:c0 + 512], [], [rq_[1].tr])
                px = psX[ix % 6]
                ix += 1
                for kk in range(4):
                    mm(P, px.t[0:64, :], qw_.t[:, kk, 128:192], cqT.t[:, kk, c0:c0 + 512], kk == 0, kk == 3,
                       [qw_.tr, t_cq[qc]], [px.tr])
                tt(P, "dve", q1.t[:], px.t[0:64, :], rq_[0].t[:], ALU.mult, [px.tr, rq_[0].tr], [q1.tr])
                px = psX[ix % 6]
                ix += 1
                for kk in range(4):
                    mm(P, px.t[0:64, :], qw_.t[:, kk, 192:256], cqT.t[:, kk, c0:c0 + 512], kk == 0, kk == 3,
                       [qw_.tr, t_cq[qc]], [px.tr])
                tt(P, "dve", q2.t[:], px.t[0:64, :], rq_[1].t[:], ALU.mult, [px.tr, rq_[1].tr], [q2.tr])
                tt(P, "pool", QrT.t[:, c0:c0 + 512], q1.t[:], q2.t[:], ALU.add, [q1.tr, q2.tr], [t_qn[qc]])
            for qc in range(8):
                c0 = qc * 512
                NP = NS // 2
                for p in range(NP + 2):
                    if p < NP:
                        ps = psS[p % 3]
                        pt = PT[p % 3]
                        for j in range(2):
                            s = 2 * p + j
                            kc = s // 4
                            mm(P, ps.t[:, j * 512:(j + 1) * 512], KnT.t[:, s * 128:(s + 1) * 128], QnT.t[:, c0:c0 + 512],
                               True, False, [t_kn[kc], t_qn[qc]], [ps.tr])
                            mm(P, ps.t[:, j * 512:(j + 1) * 512], krT.t[:, s * 128:(s + 1) * 128], QrT.t[:, c0:c0 + 512],
                               False, True, [t_kv[kc], t_qn[qc]], [ps.tr])
                        act(P, pt.t[:], ps.t[:], AF.Exp, [ps.tr], [pt.tr], scale=MLA_SCALE)
                    if p >= 2:
                        q = p - 2
                        pt0 = PT[q % 3]
                        for j in range(2):
                            s0 = 2 * q + j
                            mm(P, psO.t[:], Vh.t[:, s0, :], pt0.t[:, j * 512:(j + 1) * 512], s0 == 0, s0 == NS - 1,
                               [t_v[s0 // 4], pt0.tr], [psO.tr])
                        la, eng = (laccA, "dve") if q % 3 != 2 else (laccB, "pool")
                        if q == 0 or q == 2:
                            cp(P, eng, la.t[:], pt0.t[:], [pt0.tr], [la.tr])
                        else:
                            tt(P, eng, la.t[:], la.t[:], pt0.t[:], ALU.add, [la.tr, pt0.tr], [la.tr])
                for j, (la, c1) in enumerate(((laccA, 0), (laccA, 512), (laccB, 0), (laccB, 512))):
                    mm(P, psL.t[:], onesf.t[:], la.t[:, c1:c1 + 512], j == 0, j == 3, [onesf.tr, la.tr], [psL.tr])
                P.op("dve", lambda e: e.reciprocal(out=rl.t[:], in_=psL.t[:]), [psL.tr], [rl.tr])
                tt(P, "dve", yf.t[:], psO.t[:], rl.t[:], ALU.mult, [psO.tr, rl.tr], [yf.tr])
                yo_ = yo[qc % 2]
                tt(P, "pool", yo_.t[:], yf.t[:], sgt.t[:, c0:c0 + 512], ALU.mult, [yf.tr, sgt.tr], [yo_.tr])
                dma(P, "sp", YT[h, :, c0:c0 + 512], yo_.t[:], [yo_.tr], [t_YT[h]])
        P.end_phase()
    if stop_after <= 5:
        return finish(k, nc, P)

    with ExitStack() as S:
        ow = P.buf(S, "ow1", [128, 16, D], BF16)
        load_w_bf16(ow, 0, out_w[1], 0, D, nk=16)
        gt1 = load_bc(S, "gt1", MS[1, 0:1, 2 * D:3 * D])
        fgb = load_bc(S, "fgb", final_g[0:1, :])
        yT = [P.buf(S, "yT6", [128, 16, 128], BF16) for _ in range(2)]
        xt = [P.buf(S, "xt6", [128, D], F32) for _ in range(2)]
        bt = P.buf(S, "bt6", [128, D], F32)
        junk = P.buf(S, "junk6", [128, D], BF16)
        ssb = P.buf(S, "ssb6", [128, 8], F32)
        tmpb = P.buf(S, "tmpb6", [128, 8], F32)
        psB = [P.pbuf(S, "ps6B", [128, D]) for _ in range(2)]
        for i in range(NOWN):
            s = i + 2
            q0 = i * 128
            y_ = yT[i % 2]
            x_ = xt[i % 2]
            pb = psB[i % 2]
            dma(P, "sp", y_.t[:], YT[:, :, q0:q0 + 128].rearrange("h v q -> v h q"), t_YT, [y_.tr])
            dma(P, "act", x_.t[:], H1[s], [t_H1[s]], [x_.tr])
            for n in range(2):
                for kk in range(16):
                    mm(P, pb.t[:, n * 512:(n + 1) * 512], y_.t[:, kk, :], ow.t[:, kk, n * 512:(n + 1) * 512],
                       kk == 0, kk == 15, [y_.tr, ow.tr], [pb.tr])
            tt(P, "dve", bt.t[:], pb.t[:], gt1.t[:], ALU.mult, [pb.tr, gt1.tr], [bt.tr])
            tt(P, "pool", x_.t[:], x_.t[:], bt.t[:], ALU.add, [x_.tr, bt.tr], [x_.tr])
            act(P, junk.t[:], x_.t[:], AF.Square, [x_.tr], [junk.tr, ssb.tr], accum_out=ssb.t[:, 0:1])
            rstd_from_ss(P, ssb, tmpb, D)
            stt(P, "dve", x_.t[:], x_.t[:], ssb.t[:, 0:1], fgb.t[:], ALU.mult, ALU.mult, [x_.tr, ssb.tr, fgb.tr], [x_.tr])
            dma(P, "sp", out_d[i], x_.t[:], [x_.tr], [])
        P.end_phase()
    return finish(k, nc, P)


def finish(k, nc, P):
    if any(len(v) for v in P.ops.values()):
        P.end_phase()
    P.close()
    return nc


def _consts(mir):
    ident = np.eye(128, dtype=np.float32)
    s = np.arange(128)[:, None]
    t = np.arange(128)[None, :]
    same = (s // 64) == (t // 64)
    tb_up = (same & (s <= t)).astype(np.float32)
    tb_dn = (same & (s >= t)).astype(np.float32)
    mid = (t // 64) * 64 + 32
    td_up = tb_up - tb_up[:, mid[0]]
    td_dn = tb_dn - tb_dn[:, mid[0]]
    tr_up = (same & (s > t)).astype(np.float32)
    tr_dn = (same & (s < t)).astype(np.float32)
    tris = np.stack([tb_up, td_up, tr_up, tb_dn, td_dn, tr_dn]).astype(np.float32)
    ci = np.zeros((128, 2), np.float32)
    ci[:64, 0] = 1
    ci[64:, 1] = 1
    p = np.arange(128)[:, None] % 64
    j = np.arange(64)[None, :]
    masks = np.stack([(p <= j), (p >= j)]).astype(np.float32)
    L = 384
    bands = np.zeros((20, 128, 128), np.float32)
    for g, w in enumerate((2, 4, 8, 16)):
        Bm = np.zeros((L, L), np.float64)
        for jl in range(L):
            to = (L - 1 - jl) if mir else jl
            lo = max(0, to - w // 2)
            hi = min(L, to + w // 2)
            cnt = hi - lo
            for so in range(lo, hi):
                sl = (L - 1 - so) if mir else so
                Bm[sl, jl] += 1.0 / cnt
            Bm[jl, jl] -= 1.0
        bands[g * 5 + 0] = Bm[0:128, 0:128]
        bands[g * 5 + 1] = Bm[128:256, 128:256]
        bands[g * 5 + 2] = Bm[256:384, 256:384]
        bands[g * 5 + 3] = Bm[0:128, 128:256]
        bands[g * 5 + 4] = Bm[256:384, 128:256]
    return ident, tris, ci, masks, bands


def _rope_tables(mir):
    inv = (10000.0 ** (-2.0 * np.arange(16, dtype=np.float32) / 32.0)).astype(np.float32)
    jl = np.arange(T)
    to = (T - 1 - jl) if mir else jl
    pr = (to // 64).astype(np.float32)
    pc = (to % 64).astype(np.float32)
    ang_r = pr[:, None] * inv[None, :]
    ang_c = pc[:, None] * inv[None, :]
    cr, sr = np.cos(ang_r).astype(np.float32), np.sin(ang_r).astype(np.float32)
    cc, sc = np.cos(ang_c).astype(np.float32), np.sin(ang_c).astype(np.float32)
    cos = np.concatenate([cr, cr, cc, cc], axis=1)
    sin = np.concatenate([-sr, sr, -sc, sc], axis=1)
    cosK = np.concatenate([np.ones((TC, 64), np.float32), cos], axis=0).reshape(NS, 128, 64)
    sinK = np.concatenate([np.zeros((TC, 64), np.float32), sin], axis=0).reshape(NS, 128, 64)
    cosQ = np.ascontiguousarray(cos[:NQ].T)
    sinQ = np.ascontiguousarray(sin[:NQ].T)
    return cosK, sinK, cosQ, sinQ


_PERM64 = np.concatenate([np.arange(16, 32), np.arange(0, 16), np.arange(48, 64), np.arange(32, 48)])

_NC_CACHE = {}


def _prep_inputs(x, c, ctx, c_ctx, ada_w, ada_b, norm_g, out_w, ev_in_w, hg_lb, hg_norm_g, pool_w,
                 pool_scale, od_in_w, qa_norm_g, qb_w, kva_norm_g, kvb_w, final_norm_g):
    f = lambda a: np.ascontiguousarray(np.asarray(a, dtype=np.float32))
    x, c, ctx, c_ctx = f(x), f(c), f(ctx), f(c_ctx)
    ev = f(ev_in_w)[0]
    od = f(od_in_w)[0]
    qb = f(qb_w)[0].reshape(512, HEADS, 192)
    qb_ext = np.concatenate([qb, qb[:, :, 128:][:, :, _PERM64]], axis=2).reshape(512, HEADS * 256)
    kr = od[:, 768:832]
    w1 = np.concatenate([od[:, 0:768], kr, kr[:, _PERM64], od[:, 832:]], axis=1)
    lb = f(hg_lb)
    shared = dict(ada_w=f(ada_w), ada_b=f(ada_b), norm_g=f(norm_g), final_g=f(final_norm_g)[None, :],
                  hgn=f(hg_norm_g), pool_w=f(pool_w)[0], pool_scale=f(pool_scale), out_w=f(out_w),
                  w1=np.ascontiguousarray(w1), qa_g=f(qa_norm_g), kva_g=f(kva_norm_g),
                  qbw=np.ascontiguousarray(qb_ext), kvbw=f(kvb_w)[0])
    per_mir = {}
    for mir in (False, True):
        ident, tris, ci, masks, bands = _consts(mir)
        cosK, sinK, cosQ, sinQ = _rope_tables(mir)
        q_, ff, fb, i_, ga, u_, gb = [ev[:, j * D:(j + 1) * D] for j in range(7)]
        w0 = np.concatenate([q_, fb, ff, i_, ga, u_, gb] if mir else [q_, ff, fb, i_, ga, u_, gb], axis=1)
        lbr = lb[::-1] if mir else lb
        per_mir[mir] = dict(ident=ident, tris=tris, ci=ci, masks=masks, bands=bands, ropeK_cos=cosK, ropeK_sin=sinK,
                            ropeQ_cosT=cosQ, ropeQ_sinT=sinQ, w0=np.ascontiguousarray(w0),
                            lbraw=np.ascontiguousarray(lbr))
    in_maps = []
    for core in range(8):
        b, half = core // 2, core % 2
        mir = half == 1
        xl = x[b][::-1] if mir else x[b]
        cl = ctx[b][::-1] if mir else ctx[b]
        xs = np.concatenate([cl, xl], axis=0).reshape(NS, 128, D)
        cvec = np.stack([c[b], c_ctx], axis=0)
        cvT = np.ascontiguousarray(cvec.reshape(2, 8, 128).transpose(2, 1, 0))
        m = dict(xs=np.ascontiguousarray(xs), cvT=cvT)
        m.update(shared)
        m.update(per_mir[mir])
        in_maps.append(m)
    return in_maps


def kernel(x, c, ctx, c_ctx, ada_w, ada_b, norm_g, out_w, ev_in_w, hg_lb, hg_norm_g, pool_w,
           pool_scale, od_in_w, qa_norm_g, qb_w, kva_norm_g, kvb_w, final_norm_g):
    in_maps = _prep_inputs(x, c, ctx, c_ctx, ada_w, ada_b, norm_g, out_w, ev_in_w, hg_lb, hg_norm_g, pool_w,
                           pool_scale, od_in_w, qa_norm_g, qb_w, kva_norm_g, kvb_w, final_norm_g)
    if "nc" not in _NC_CACHE:
        _NC_CACHE["nc"] = build_program()
    nc = _NC_CACHE["nc"]
    res = run_bass_kernel_spmd(nc, in_maps, core_ids=list(range(8)))
    out = np.zeros((4, T, D), np.float32)
    for core in range(8):
        b, half = core // 2, core % 2
        o = np.asarray(res.results[core]["out"]).reshape(NQ, D)
        if half == 1:
            out[b, NQ:] = o[::-1]
        else:
            out[b, :NQ] = o
    return out
```

```python
import numpy as np
from contextlib import ExitStack
import concourse.bass as bass
import concourse.mybir as mybir
from concourse.bass_utils import run_bass_kernel_spmd

F32 = mybir.dt.float32
BF16 = mybir.dt.bfloat16
AF = mybir.ActivationFunctionType
ALU = mybir.AluOpType
AX = mybir.AxisListType

D = 1024
T = 8192
TC = 256
NS = 66
NOWN = 32
NQ = 4096
NK = 8448
HEADS = 16
EPS = 1e-6
MLA_SCALE = 192.0 ** -0.5

ENGS = ("pe", "act", "dve", "pool", "sp")
NDMA = 12


class Tr:
    __slots__ = ("w", "r")

    def __init__(self):
        self.w = None
        self.r = {}


class Buf:
    __slots__ = ("t", "tr")

    def __init__(self, t):
        self.t = t
        self.tr = Tr()


class Prog:
    def __init__(self, nc):
        self.nc = nc
        self.es = ExitStack()
        self.ops = {e: [] for e in ENGS}
        self.cnt = {e: 0 for e in ENGS}
        self.waited = {e: {} for e in ENGS}
        self.sems = {}
        for e in ("pe", "act", "dve", "pool"):
            self.sems["c_" + e] = self.es.enter_context(nc.semaphore("c_" + e))
        self.dma_cnt = {}
        self.dma_rr = {}
        for q in ("sp", "act", "pool"):
            self.dma_rr[q] = 0
            for j in range(NDMA):
                k = "d_%s_%d" % (q, j)
                self.sems[k] = self.es.enter_context(nc.semaphore(k))
                self.dma_cnt[k] = 0
        self.final_events = {}
        self.uid = 0

    def buf(self, stack, name, shape, dtype):
        self.uid += 1
        return Buf(stack.enter_context(self.nc.sbuf_tensor("%s_%d" % (name, self.uid), list(shape), dtype)))

    def pbuf(self, stack, name, shape, dtype=F32):
        self.uid += 1
        return Buf(stack.enter_context(self.nc.psum_tensor("%s_%d" % (name, self.uid), list(shape), dtype)))

    def _deps(self, eng, reads, writes, pe_like, own=None):
        need = {}
        if own is None:
            own = "c_" + eng
        for t in reads:
            if t.w is not None:
                if t.w[0] == own and pe_like:
                    continue
                if need.get(t.w[0], 0) < t.w[1]:
                    need[t.w[0]] = t.w[1]
        for t in writes:
            if t.w is not None and t.w[0] != own:
                if need.get(t.w[0], 0) < t.w[1]:
                    need[t.w[0]] = t.w[1]
            for k, v in t.r.items():
                if k != own and need.get(k, 0) < v:
                    need[k] = v
        out = []
        wd = self.waited[eng]
        for k, v in need.items():
            if wd.get(k, 0) >= v:
                continue
            wd[k] = v
            out.append((k, v))
        return out

    def _commit(self, ev, reads, writes):
        k, v = ev
        for t in reads:
            if t.r.get(k, 0) < v:
                t.r[k] = v
        for t in writes:
            t.w = ev
            t.r = {}

    def op(self, eng, fn, reads=(), writes=()):
        waits = self._deps(eng, reads, writes, eng == "pe")
        self.cnt[eng] += 1
        ev = ("c_" + eng, self.cnt[eng])
        self.ops[eng].append((waits, fn, ("c_" + eng, 1)))
        self._commit(ev, reads, writes)

    def dma(self, q, fn, reads=(), writes=()):
        j = self.dma_rr[q]
        self.dma_rr[q] = (j + 1) % NDMA
        k = "d_%s_%d" % (q, j)
        waits = self._deps(q, reads, writes, False, own="__none__")
        prev = self.dma_cnt[k]
        if prev > 0 and self.waited[q].get(k, 0) < prev:
            self.waited[q][k] = prev
            waits.append((k, prev))
        self.dma_cnt[k] = prev + 16
        ev = (k, prev + 16)
        self.ops[q].append((waits, fn, (k, 16)))
        self._commit(ev, reads, writes)
        self.final_events[k] = prev + 16

    def barrier(self):
        allev = []
        for e in ("pe", "act", "dve", "pool"):
            if self.cnt[e] > 0:
                allev.append(("c_" + e, self.cnt[e]))
        for k, v in self.final_events.items():
            allev.append((k, v))
        for e in ENGS:
            w = []
            for k, v in allev:
                if k == "c_" + e:
                    continue
                if self.waited[e].get(k, 0) < v:
                    self.waited[e][k] = v
                    w.append((k, v))
            if w:
                self.ops[e].append((w, None, None))

    def emit_block(self):
        nc = self.nc
        sems = self.sems
        ops = self.ops
        self.ops = {e: [] for e in ENGS}
        with nc.Block() as block:
            def run(engname):
                def body(e):
                    for waits, fn, inc in ops[engname]:
                        for k, v in waits:
                            e.wait_ge(sems[k], v)
                        if fn is not None:
                            fn(e).then_inc(sems[inc[0]], inc[1])
                return body
            block.tensor(run("pe"))
            block.scalar(run("act"))
            block.vector(run("dve"))
            block.gpsimd(run("pool"))
            block.sync(run("sp"))

    def end_phase(self):
        self.barrier()
        self.emit_block()

    def close(self):
        self.es.close()


class K:
    pass


def mm(P, out, lhsT, rhs, start, stop, R, W):
    P.op("pe", lambda e: e.matmul(out, lhsT=lhsT, rhs=rhs, start=start, stop=stop), R, W)


def tp(P, out, in_, ident, R, W):
    P.op("pe", lambda e: e.transpose(out, in_, ident), R, W)


def act(P, out, in_, func, R, W, **kw):
    P.op("act", lambda e: e.activation(out=out, in_=in_, func=func, **kw), R, W)


def tt(P, eng, out, in0, in1, op, R, W):
    P.op(eng, lambda e: e.tensor_tensor(out=out, in0=in0, in1=in1, op=op), R, W)


def ts(P, eng, out, in0, s1, s2, op0, op1, R, W):
    if s2 is None:
        P.op(eng, lambda e: e.tensor_scalar(out=out, in0=in0, scalar1=s1, scalar2=None, op0=op0), R, W)
    else:
        P.op(eng, lambda e: e.tensor_scalar(out=out, in0=in0, scalar1=s1, scalar2=s2, op0=op0, op1=op1), R, W)


def stt(P, eng, out, in0, scalar, in1, op0, op1, R, W):
    P.op(eng, lambda e: e.scalar_tensor_tensor(out=out, in0=in0, scalar=scalar, in1=in1, op0=op0, op1=op1), R, W)


def cp(P, eng, out, in_, R, W):
    if eng == "act":
        P.op("act", lambda e: e.copy(out=out, in_=in_), R, W)
    else:
        P.op(eng, lambda e: e.tensor_copy(out=out, in_=in_), R, W)


def dma(P, q, out, in_, R, W):
    P.dma(q, lambda e: e.dma_start(out=out, in_=in_), R, W)


def rstd_from_ss(P, ss, tmp, n, c0=0, c1=1):
    ts(P, "dve", tmp.t[:, c0:c1], ss.t[:, c0:c1], 1.0 / n, EPS, ALU.mult, ALU.add, [ss.tr], [tmp.tr])
    act(P, tmp.t[:, c0:c1], tmp.t[:, c0:c1], AF.Sqrt, [tmp.tr], [tmp.tr])
    P.op("dve", lambda e: e.reciprocal(out=ss.t[:, c0:c1], in_=tmp.t[:, c0:c1]), [tmp.tr], [ss.tr])


def build_program(debug=False, stop_after=99, lim=None):
    nc = bass.Bass("TRN2", target_bir_lowering=False)
    P = Prog(nc)
    k = K()
    k.nc, k.P = nc, P

    def din(name, shape, dt=F32):
        return nc.dram_tensor(name, list(shape), dt, kind="ExternalInput").ap()

    dbg = set(debug) if debug else set()

    def dscr(name, shape, dt):
        return nc.dram_tensor(name, list(shape), dt, kind=("ExternalOutput" if name in dbg else "Internal")).ap()

    xs = din("xs", [NS, 128, D])
    cvT = din("cvT", [128, 8, 2])
    ada_w = din("ada_w", [2, D, 3 * D])
    ada_b = din("ada_b", [2, 3 * D])
    norm_g = din("norm_g", [2, D])
    final_g = din("final_g", [1, D])
    w0 = din("w0", [D, 7 * D])
    lbraw = din("lbraw", [2, 3, D])
    hgn = din("hgn", [1, D])
    pool_w = din("pool_w", [4, 256, 256])
    pool_scale = din("pool_scale", [1, D])
    out_w = din("out_w", [2, 2 * D, D])
    w1 = din("w1", [D, 2944])
    qa_g = din("qa_g", [1, 512])
    kva_g = din("kva_g", [1, 256])
    qbw = din("qbw", [512, HEADS * 256])
    kvbw = din("kvbw", [256, HEADS * 256])
    ropeK_cos = din("ropeK_cos", [NS, 128, 64])
    ropeK_sin = din("ropeK_sin", [NS, 128, 64])
    ropeQ_cosT = din("ropeQ_cosT", [64, NQ])
    ropeQ_sinT = din("ropeQ_sinT", [64, NQ])
    ident_d = din("ident", [128, 128])
    tris_d = din("tris", [6, 128, 128])
    ci_d = din("ci", [128, 2])
    masks_d = din("masks", [2, 128, 64])
    bands_d = din("bands", [20, 128, 128])
    out_d = nc.dram_tensor("out", [NOWN, 128, D], F32, kind="ExternalOutput").ap()

    MS = dscr("MS", [2, 2, 3 * D], F32)
    ZT = dscr("ZT", [NS, 128, 8, 128], BF16)
    ODN = dscr("ODN", [NS, 128, D], F32)
    ABUF = dscr("ABUF", [NS, 128, D], BF16)
    H1 = dscr("H1", [NS, 128, D], F32)
    SGT = dscr("SGT", [HEADS, 128, NQ], F32)
    YT = dscr("YT", [HEADS, 128, NQ], BF16)
    t_MS, t_SGT = Tr(), Tr()
    t_ZT = [Tr() for _ in range(NS)]
    t_ODN = [Tr() for _ in range(NS)]
    t_ABUF = [Tr() for _ in range(NS)]
    t_H1 = [Tr() for _ in range(NS)]
    t_YT = [Tr() for _ in range(HEADS)]

    G = P.es
    identb = P.buf(G, "identb", [128, 128], BF16)
    identf = P.buf(G, "identf", [128, 128], F32)
    tri = P.buf(G, "tri", [128, 6, 128], F32)
    cib = P.buf(G, "cib", [128, 2], F32)
    maskb = P.buf(G, "maskb", [128, 2, 64], F32)
    dma(P, "pool", identb.t[:], ident_d, [], [identb.tr])
    dma(P, "sp", identf.t[:], ident_d, [], [identf.tr])
    dma(P, "sp", tri.t[:], tris_d.rearrange("m s t -> s m t"), [], [tri.tr])
    dma(P, "sp", cib.t[:], ci_d, [], [cib.tr])
    dma(P, "sp", maskb.t[:], masks_d.rearrange("m s t -> s m t"), [], [maskb.tr])

    with ExitStack() as S:
        cv = P.buf(S, "cv", [128, 8, 2], F32)
        sv = P.buf(S, "sv", [128, 8, 2], F32)
        aw = [P.buf(S, "aw%d" % i, [128, 8, 512], F32) for i in range(2)]
        ab = [P.buf(S, "ab%d" % i, [2, 512], F32) for i in range(2)]
        mrow = [P.buf(S, "mrow%d" % i, [2, 512], F32) for i in range(2)]
        pm = [P.pbuf(S, "pm%d" % i, [128, 512]) for i in range(2)]
        dma(P, "sp", cv.t[:], cvT, [], [cv.tr])
        act(P, sv.t[:], cv.t[:], AF.Silu, [cv.tr], [sv.tr])
        it = 0
        for l in range(2):
            for n in range(6):
                a_, b_, m_, p_ = aw[it % 2], ab[it % 2], mrow[it % 2], pm[it % 2]
                it += 1
                for kk in range(8):
                    dma(P, "sp" if kk % 2 == 0 else "act", a_.t[:, kk, :],
                        ada_w[l, kk * 128:(kk + 1) * 128, n * 512:(n + 1) * 512], [], [a_.tr])
                dma(P, "sp", b_.t[:], ada_b[l:l + 1, n * 512:(n + 1) * 512].partition_broadcast(2), [], [b_.tr])
                for kk in range(8):
                    mm(P, p_.t[0:2, :], sv.t[:, kk, :], a_.t[:, kk, :], kk == 0, kk == 7, [sv.tr, a_.tr], [p_.tr])
                tt(P, "dve", m_.t[:], p_.t[0:2, :], b_.t[:], ALU.add, [p_.tr, b_.tr], [m_.tr])
                dma(P, "sp", MS[l, :, n * 512:(n + 1) * 512], m_.t[:], [m_.tr], [t_MS])
        P.end_phase()
    if stop_after <= 0:
        return finish(k, nc, P)

    def load_bc(S, name, src_row, q="sp"):
        n = src_row.shape[-1]
        b = P.buf(S, name, [128, n], F32)
        dma(P, q, b.t[:], src_row.partition_broadcast(128), [t_MS], [b.tr])
        return b

    def mod_tiles(S, l, want_gate):
        ng = load_bc(S, "ng", norm_g[l:l + 1, :])
        res = []
        for r in range(2):
            sh = load_bc(S, "sh", MS[l, r:r + 1, 0:D])
            sc = load_bc(S, "sc", MS[l, r:r + 1, D:2 * D])
            stt(P, "dve", sc.t[:], sc.t[:], 1.0, ng.t[:], ALU.add, ALU.mult, [sc.tr, ng.tr], [sc.tr])
            gt = None
            if want_gate[r]:
                gt = load_bc(S, "gt", MS[l, r:r + 1, 2 * D:3 * D])
            res.append((sc, sh, gt))
        return res

    def make_z(xt, Gb, SHb, junk, ssb, tmpb, zb):
        act(P, junk.t[:], xt.t[:], AF.Square, [xt.tr], [junk.tr, ssb.tr], accum_out=ssb.t[:, 0:1])
        rstd_from_ss(P, ssb, tmpb, D)
        stt(P, "dve", xt.t[:], xt.t[:], ssb.t[:, 0:1], Gb.t[:], ALU.mult, ALU.mult, [xt.tr, ssb.tr, Gb.tr], [xt.tr])
        tt(P, "pool", zb.t[:], xt.t[:], SHb.t[:], ALU.add, [xt.tr, SHb.tr], [zb.tr])

    def transpose_to(psT, src, ncols, dstT, evac_eng):
        nb = ncols // 128
        for j in range(nb):
            tp(P, psT.t[:, j * 128:(j + 1) * 128], src.t[:, j * 128:(j + 1) * 128], identb.t[:],
               [src.tr, identb.tr], [psT.tr])
        cp(P, evac_eng, dstT.t[:].rearrange("p k t -> p (k t)"), psT.t[:, 0:ncols], [psT.tr], [dstT.tr])

    def load_w_bf16(dst, dcol, src, scol, ncols, nk=8):
        for kk in range(nk):
            for c0 in range(0, ncols, 512):
                c1 = min(ncols, c0 + 512)
                dma(P, "pool", dst.t[:, kk, dcol + c0:dcol + c1],
                    src[kk * 128:(kk + 1) * 128, scol + c0:scol + c1], [], [dst.tr])

    def hgrn_pass(dn):
        with ExitStack() as S:
            ncols = 3 * D if dn else 4 * D
            wres = P.buf(S, "wres", [128, 8, ncols], BF16)
            load_w_bf16(wres, 0, w0, 0, D)
            load_w_bf16(wres, D, w0, (2 * D if dn else D), D)
            load_w_bf16(wres, 2 * D, w0, 3 * D, D)
            if not dn:
                load_w_bf16(wres, 3 * D, w0, 4 * D, D)
            d_idx = 1 if dn else 0
            e3 = P.buf(S, "e3", [128, 3, D], F32)
            omlb = P.buf(S, "omlb", [128, D], F32)
            for j in range(3):
                dma(P, "sp", e3.t[:, j, :], lbraw[d_idx, j:j + 1, :].partition_broadcast(128), [], [e3.tr])
            act(P, e3.t[:], e3.t[:], AF.Exp, [e3.tr], [e3.tr])
            tt(P, "dve", omlb.t[:], e3.t[:, 1, :], e3.t[:, 2, :], ALU.add, [e3.tr], [omlb.tr])
            tt(P, "dve", e3.t[:, 2, :], omlb.t[:], e3.t[:, 0, :], ALU.add, [omlb.tr, e3.tr], [e3.tr])
            P.op("dve", lambda e: e.reciprocal(out=e3.t[:, 1, :], in_=e3.t[:, 2, :]), [e3.tr], [e3.tr])
            tt(P, "dve", omlb.t[:], omlb.t[:], e3.t[:, 1, :], ALU.mult, [omlb.tr, e3.tr], [omlb.tr])
            if dn:
                mods = mod_tiles(S, 0, (False, False))
            else:
                hgb = load_bc(S, "hgb", hgn[0:1, :])
            St = P.buf(S, "St", [128, 8, 128], F32)
            Sbf = P.buf(S, "Sbf", [128, 8, 128], BF16)
            P.op("pool", lambda e: e.memset(St.t[:], 0.0), [], [St.tr])
            P.op("pool", lambda e: e.memset(Sbf.t[:], 0.0), [], [Sbf.tr])
            xt = [P.buf(S, "xt", [128, D], F32) for _ in range(2)]
            junk = P.buf(S, "junk", [128, D], BF16)
            ssb = P.buf(S, "ssb", [128, 8], F32)
            tmpb = P.buf(S, "tmpb", [128, 8], F32)
            zb = P.buf(S, "zb", [128, D], BF16)
            zT = [P.buf(S, "zT", [128, 8, 128], BF16) for _ in range(2)]
            Qf = P.buf(S, "Qf", [128, D], F32)
            S2 = P.buf(S, "S2", [128, D], F32)
            Kf = P.buf(S, "Kf", [128, D], F32)
            Gl = P.buf(S, "Gl", [128, D], F32)
            V = P.buf(S, "V", [128, D], BF16)
            E = [P.buf(S, "E", [128, D], F32) for _ in range(2)]
            Qe = P.buf(S, "Qe", [128, D], BF16)
            Ke = P.buf(S, "Ke", [128, D], BF16)
            Qb = P.buf(S, "Qb", [128, D], BF16)
            Kend = P.buf(S, "Kend", [128, D], BF16)
            QeT = P.buf(S, "QeT", [128, 8, 128], BF16)
            KeT = P.buf(S, "KeT", [128, 8, 128], BF16)
            QbT = P.buf(S, "QbT", [128, 8, 128], BF16)
            Asb = P.buf(S, "Asb", [128, 8, 64], BF16)
            eb = P.buf(S, "eb", [128, 8, 2], F32)
            osb = [P.buf(S, "osb", [128, D], F32) for _ in range(2)]
            if not dn:
                SGA = P.buf(S, "SGA", [128, D], F32)
                odn = [P.buf(S, "odn", [128, D], F32) for _ in range(2)]
                abf = [P.buf(S, "abf", [128, D], BF16) for _ in range(2)]
            psA = P.pbuf(S, "psA", [128, D])
            psB = P.pbuf(S, "psB", [128, D])
            psC = P.pbuf(S, "psC", [128, D])
            psT = [P.pbuf(S, "psT", [128, D], BF16) for _ in range(2)]
            tb = 3 if dn else 0
            order = [1, 0] + list(range(NS - 1, 1, -1)) if dn else list(range(NS))
            corder = (1, 0) if dn else (0, 1)
            if lim is not None:
                order = order[:lim]
            for it, s in enumerate(order):
                z_ = zT[it % 2]
                if dn:
                    x_ = xt[it % 2]
                    r = 1 if s < 2 else 0
                    dma(P, "sp", x_.t[:], xs[s], [], [x_.tr])
                    make_z(x_, mods[r][0], mods[r][1], junk, ssb, tmpb, zb)
                    transpose_to(psT[0], zb, D, z_, "act")
                    dma(P, "sp", ZT[s], z_.t[:], [z_.tr], [t_ZT[s]])
                else:
                    dma(P, "sp", z_.t[:], ZT[s], [t_ZT[s]], [z_.tr])
                    o_ = odn[it % 2]
                    dma(P, "act", o_.t[:], ODN[s], [t_ODN[s]], [o_.tr])
                for ps, col in ((psA, 0), (psB, D), (psC, 2 * D)):
                    for n in range(2):
                        for kk in range(8):
                            mm(P, ps.t[:, n * 512:(n + 1) * 512], z_.t[:, kk, :],
                               wres.t[:, kk, col + n * 512:col + (n + 1) * 512], kk == 0, kk == 7,
                               [z_.tr, wres.tr], [ps.tr])
                act(P, Qf.t[:], psA.t[:], AF.Silu, [psA.tr], [Qf.tr])
                act(P, S2.t[:], psB.t[:], AF.Sigmoid, [psB.tr], [S2.tr], scale=-1.0)
                cp(P, "dve", V.t[:], psC.t[:], [psC.tr], [V.tr])
                tt(P, "dve", Kf.t[:], S2.t[:], omlb.t[:], ALU.mult, [S2.tr, omlb.tr], [Kf.tr])
                act(P, Gl.t[:], Kf.t[:], AF.Ln, [Kf.tr], [Gl.tr], scale=-1.0, bias=1.0)
                for ps, ti in ((psA, tb + 1), (psB, tb + 0), (psC, tb + 2)):
                    for n in range(2):
                        mm(P, ps.t[:, n * 512:(n + 1) * 512], tri.t[:, ti, :], Gl.t[:, n * 512:(n + 1) * 512],
                           True, True, [tri.tr, Gl.tr], [ps.tr])
                act(P, E[0].t[:], psA.t[:], AF.Exp, [psA.tr], [E[0].tr])
                act(P, E[1].t[:], psA.t[:], AF.Exp, [psA.tr], [E[1].tr], scale=-1.0)
                tt(P, "dve", Qe.t[:], Qf.t[:], E[0].t[:], ALU.mult, [Qf.tr, E[0].tr], [Qe.tr])
                tt(P, "pool", Ke.t[:], Kf.t[:], E[1].t[:], ALU.mult, [Kf.tr, E[1].tr], [Ke.tr])
                act(P, E[0].t[:], psB.t[:], AF.Exp, [psB.tr], [E[0].tr])
                act(P, E[1].t[:], psC.t[:], AF.Exp, [psC.tr], [E[1].tr])
                tt(P, "dve", Qb.t[:], Qf.t[:], E[0].t[:], ALU.mult, [Qf.tr, E[0].tr], [Qb.tr])
                tt(P, "pool", Kend.t[:], Kf.t[:], E[1].t[:], ALU.mult, [Kf.tr, E[1].tr], [Kend.tr])
                if not dn:
                    for n in range(2):
                        for kk in range(8):
                            mm(P, psC.t[:, n * 512:(n + 1) * 512], z_.t[:, kk, :],
                               wres.t[:, kk, 3 * D + n * 512:3 * D + (n + 1) * 512], kk == 0, kk == 7,
                               [z_.tr, wres.tr], [psC.tr])
                    act(P, SGA.t[:], psC.t[:], AF.Silu, [psC.tr], [SGA.tr])
                transpose_to(psT[0], Qe, D, QeT, "act")
                transpose_to(psT[1], Ke, D, KeT, "dve")
                transpose_to(psT[0], Qb, D, QbT, "act")
                for h in range(8):
                    mm(P, psA.t[:, 512 + 2 * h:514 + 2 * h], Gl.t[:, h * 128:(h + 1) * 128], cib.t[:], True, True,
                       [Gl.tr, cib.tr], [psA.tr])
                for h in range(8):
                    for c in range(2):
                        mm(P, psA.t[c * 64:(c + 1) * 64, h * 64:(h + 1) * 64], KeT.t[:, h, c * 64:(c + 1) * 64],
                           QeT.t[:, h, c * 64:(c + 1) * 64], True, True, [KeT.tr, QeT.tr], [psA.tr])
                act(P, eb.t[:].rearrange("p h c -> p (h c)"), psA.t[:, 512:528], AF.Exp, [psA.tr], [eb.tr])
                tt(P, "dve", Asb.t[:], psA.t[:, 0:512].rearrange("p (h t) -> p h t", h=8),
                   maskb.t[:, (1 if dn else 0), :].unsqueeze(1).to_broadcast([128, 8, 64]), ALU.mult,
                   [psA.tr, maskb.tr], [Asb.tr])
                for c in corder:
                    r0, r1 = c * 64, (c + 1) * 64
                    for h in range(8):
                        hs = slice(h * 128, (h + 1) * 128)
                        mm(P, psB.t[r0:r1, hs], Asb.t[r0:r1, h, :], V.t[r0:r1, hs], True, False,
                           [Asb.tr, V.tr], [psB.tr])
                        mm(P, psB.t[r0:r1, hs], QbT.t[:, h, r0:r1], Sbf.t[:, h, :], False, True,
                           [QbT.tr, Sbf.tr], [psB.tr])
                    for h in range(8):
                        hs = slice(h * 128, (h + 1) * 128)
                        mm(P, psC.t[:, hs], Kend.t[r0:r1, hs], V.t[r0:r1, hs], True, True,
                           [Kend.tr, V.tr], [psC.tr])
                    tt(P, "dve", St.t[:], St.t[:], eb.t[:, :, c:c + 1].to_broadcast([128, 8, 128]), ALU.mult,
                       [St.tr, eb.tr], [St.tr])
                    tt(P, "dve", St.t[:].rearrange("p h v -> p (h v)"), St.t[:].rearrange("p h v -> p (h v)"),
                       psC.t[:], ALU.add, [St.tr, psC.tr], [St.tr])
                    cp(P, "act", Sbf.t[:], St.t[:], [St.tr], [Sbf.tr])
                if dn:
                    ob = osb[it % 2]
                    cp(P, "act", ob.t[:], psB.t[:], [psB.tr], [ob.tr])
                    dma(P, "sp", ODN[s], ob.t[:], [ob.tr], [t_ODN[s]])
                else:
                    ob = osb[it % 2]
                    a_ = abf[it % 2]
                    tt(P, "dve", ob.t[:], psB.t[:], o_.t[:], ALU.add, [psB.tr, o_.tr], [ob.tr])
                    tt(P, "pool", o_.t[:], ob.t[:], ob.t[:], ALU.mult, [ob.tr], [o_.tr])
                    P.op("dve", lambda e, o_=o_: e.tensor_reduce(out=ssb.t[:], in_=o_.t[:].rearrange("p (h v) -> p h v", h=8),
                                                         axis=AX.X, op=ALU.add), [o_.tr], [ssb.tr])
                    rstd_from_ss(P, ssb, tmpb, 128, 0, 8)
                    tt(P, "dve", ob.t[:].rearrange("p (h v) -> p h v", h=8), ob.t[:].rearrange("p (h v) -> p h v", h=8),
                       ssb.t[:].unsqueeze(2).to_broadcast([128, 8, 128]), ALU.mult, [ob.tr, ssb.tr], [ob.tr])
                    tt(P, "pool", ob.t[:], ob.t[:], hgb.t[:], ALU.mult, [ob.tr, hgb.tr], [ob.tr])
                    tt(P, "dve", a_.t[:], ob.t[:], SGA.t[:], ALU.mult, [ob.tr, SGA.tr], [a_.tr])
                    dma(P, "sp", ABUF[s], a_.t[:], [a_.tr], [t_ABUF[s]])
            P.end_phase()

    hgrn_pass(True)
    if stop_after <= 1:
        return finish(k, nc, P)
    hgrn_pass(False)
    if stop_after <= 2:
        return finish(k, nc, P)

    with ExitStack() as S:
        wres = P.buf(S, "wres3", [128, 8, 2 * D], BF16)
        load_w_bf16(wres, 0, w0, 5 * D, 2 * D)
        ow = P.buf(S, "ow0", [128, 16, D], BF16)
        load_w_bf16(ow, 0, out_w[0], 0, D, nk=16)
        pw = P.buf(S, "pw", [128, 4, 2, 256], BF16)
        for g in range(4):
            for j in range(2):
                dma(P, "pool", pw.t[:, g, j, :], pool_w[g, j * 128:(j + 1) * 128, :], [], [pw.tr])
        bands = P.buf(S, "bands", [128, 20, 128], BF16)
        for j in range(20):
            dma(P, "pool", bands.t[:, j, :], bands_d[j], [], [bands.tr])
        psc = load_bc(S, "psc", pool_scale[0:1, :])
        gts = []
        for r in range(2):
            gts.append(load_bc(S, "gt0", MS[0, r:r + 1, 2 * D:3 * D]))
        zT = [P.buf(S, "zT3", [128, 8, 128], BF16) for _ in range(2)]
        U = [P.buf(S, "U", [128, D], BF16) for _ in range(3)]
        SGB = [P.buf(S, "SGB", [128, D], F32) for _ in range(2)]
        ppT = P.buf(S, "ppT", [128, 8, 128], BF16)
        yb = P.buf(S, "yb", [128, 2 * D], BF16)
        yT = P.buf(S, "yT", [128, 16, 128], BF16)
        bt = P.buf(S, "bt", [128, D], F32)
        xt = [P.buf(S, "xt3", [128, D], F32) for _ in range(2)]
        psA = P.pbuf(S, "ps3A", [128, D])
        psB = P.pbuf(S, "ps3B", [128, D])
        psC = P.pbuf(S, "ps3C", [128, D])
        psT = [P.pbuf(S, "ps3T", [128, D], BF16) for _ in range(2)]

        def seq_info(s):
            return (0, 1) if s < 2 else (2, NS - 1)

        def post(s, itp):
            lo, hi = seq_info(s)
            var = 0 if s == lo else (2 if s == hi else 1)
            for cb in range(8):
                g = cb // 2
                terms = []
                if s > lo:
                    terms.append((U[(s - 1) % 3], g * 5 + 3))
                terms.append((U[s % 3], g * 5 + var))
                if s < hi:
                    terms.append((U[(s + 1) % 3], g * 5 + 4))
                for i, (ub, bi) in enumerate(terms):
                    mm(P, psC.t[:, cb * 128:(cb + 1) * 128], ub.t[:, cb * 128:(cb + 1) * 128], bands.t[:, bi, :],
                       i == 0, i == len(terms) - 1, [ub.tr, bands.tr], [psC.tr])
            cp(P, "act", ppT.t[:].rearrange("p k t -> p (k t)"), psC.t[:], [psC.tr], [ppT.tr])
            for g in range(4):
                for j in range(2):
                    mm(P, psA.t[:, g * 256:(g + 1) * 256], ppT.t[:, 2 * g + j, :], pw.t[:, g, j, :], j == 0, j == 1,
                       [ppT.tr, pw.tr], [psA.tr])
            sg = SGB[s % 2]
            tt(P, "dve", bt.t[:], psA.t[:], psc.t[:], ALU.mult, [psA.tr, psc.tr], [bt.tr])
            tt(P, "pool", yb.t[:, D:2 * D], bt.t[:], sg.t[:], ALU.mult, [bt.tr, sg.tr], [yb.tr])
            dma(P, "act", yb.t[:, 0:D], ABUF[s], [t_ABUF[s]], [yb.tr])
            for j in range(16):
                tp(P, psT[j // 8].t[:, (j % 8) * 128:(j % 8 + 1) * 128], yb.t[:, j * 128:(j + 1) * 128], identb.t[:],
                   [yb.tr, identb.tr], [psT[j // 8].tr])
            cp(P, "act", yT.t[:, 0:8, :].rearrange("p k t -> p (k t)"), psT[0].t[:], [psT[0].tr], [yT.tr])
            cp(P, "dve", yT.t[:, 8:16, :].rearrange("p k t -> p (k t)"), psT[1].t[:], [psT[1].tr], [yT.tr])
            for n in range(2):
                for kk in range(16):
                    mm(P, psB.t[:, n * 512:(n + 1) * 512], yT.t[:, kk, :], ow.t[:, kk, n * 512:(n + 1) * 512],
                       kk == 0, kk == 15, [yT.tr, ow.tr], [psB.tr])
            x_ = xt[itp % 2]
            dma(P, "sp", x_.t[:], xs[s], [], [x_.tr])
            gt = gts[1 if s < 2 else 0]
            tt(P, "dve", bt.t[:], psB.t[:], gt.t[:], ALU.mult, [psB.tr, gt.tr], [bt.tr])
            tt(P, "pool", x_.t[:], x_.t[:], bt.t[:], ALU.add, [x_.tr, bt.tr], [x_.tr])
            dma(P, "sp", H1[s], x_.t[:], [x_.tr], [t_H1[s]])

        itp = 0
        for s in range(NS):
            z_ = zT[s % 2]
            dma(P, "sp", z_.t[:], ZT[s], [t_ZT[s]], [z_.tr])
            for ps, col in ((psA, 0), (psB, D)):
                for n in range(2):
                    for kk in range(8):
                        mm(P, ps.t[:, n * 512:(n + 1) * 512], z_.t[:, kk, :],
                           wres.t[:, kk, col + n * 512:col + (n + 1) * 512], kk == 0, kk == 7,
                           [z_.tr, wres.tr], [ps.tr])
            cp(P, "dve", U[s % 3].t[:], psA.t[:], [psA.tr], [U[s % 3].tr])
            act(P, SGB[s % 2].t[:], psB.t[:], AF.Silu, [psB.tr], [SGB[s % 2].tr])
            lo, hi = seq_info(s)
            if s > lo:
                post(s - 1, itp)
                itp += 1
            if s == hi:
                post(s, itp)
                itp += 1
        P.end_phase()
    if stop_after <= 3:
        return finish(k, nc, P)

    cqT = P.buf(G, "cqT", [128, 4, NQ], BF16)
    ckvT = P.buf(G, "ckvT", [128, 2, NK], BF16)
    krT = P.buf(G, "krT2", [128, NK // 2], BF16)
    t_cq = [Tr() for _ in range(8)]
    t_kv = [Tr() for _ in range(17)]

    with ExitStack() as S:
        wres = P.buf(S, "wres4", [128, 8, 2944], BF16)
        load_w_bf16(wres, 0, w1, 0, 2944)
        mods = mod_tiles(S, 1, (False, False))
        qag = load_bc(S, "qag", qa_g[0:1, :])
        kvag = load_bc(S, "kvag", kva_g[0:1, :])
        xt = [P.buf(S, "xt4", [128, D], F32) for _ in range(2)]
        junk = P.buf(S, "junk4", [128, D], BF16)
        ssb = P.buf(S, "ssb4", [128, 8], F32)
        tmpb = P.buf(S, "tmpb4", [128, 8], F32)
        zb = P.buf(S, "zb4", [128, D], BF16)
        zT = [P.buf(S, "zT4", [128, 8, 128], BF16) for _ in range(2)]
        kvb = P.buf(S, "kvb", [128, 256], BF16)
        rc = [P.buf(S, "rc", [128, 64], F32) for _ in range(2)]
        rs = [P.buf(S, "rs", [128, 64], F32) for _ in range(2)]
        krf = P.buf(S, "krf", [128, 64], F32)
        krf2 = P.buf(S, "krf2", [128, 64], F32)
        krb = P.buf(S, "krb", [128, 64], BF16)
        cqb = P.buf(S, "cqb", [128, 512], BF16)
        sgf = P.buf(S, "sgf", [128, 2 * D], F32)
        sgT = [P.buf(S, "sgT", [128, 16, 128], F32) for _ in range(1)]
        psK = P.pbuf(S, "ps4K", [128, 512])
        psQ = P.pbuf(S, "ps4Q", [128, 512])
        psG = [P.pbuf(S, "ps4G", [128, D]) for _ in range(2)]
        psT = P.pbuf(S, "ps4T", [128, D], BF16)
        psF = P.pbuf(S, "ps4F", [128, 512])
        for s in range(NS):
            x_ = xt[s % 2]
            z_ = zT[s % 2]
            own = 2 <= s < 2 + NOWN
            r = 1 if s < 2 else 0
            dma(P, "sp", x_.t[:], H1[s], [t_H1[s]], [x_.tr])
            dma(P, "act", rc[s % 2].t[:], ropeK_cos[s], [], [rc[s % 2].tr])
            dma(P, "act", rs[s % 2].t[:], ropeK_sin[s], [], [rs[s % 2].tr])
            make_z(x_, mods[r][0], mods[r][1], junk, ssb, tmpb, zb)
            transpose_to(psT, zb, D, z_, "act")
            for kk in range(8):
                mm(P, psK.t[:, 0:384], z_.t[:, kk, :], wres.t[:, kk, 512:896], kk == 0, kk == 7,
                   [z_.tr, wres.tr], [psK.tr])
            act(P, junk.t[:, 0:256], psK.t[:, 0:256], AF.Square, [psK.tr], [junk.tr, ssb.tr], accum_out=ssb.t[:, 0:1])
            rstd_from_ss(P, ssb, tmpb, 256)
            stt(P, "dve", kvb.t[:], psK.t[:, 0:256], ssb.t[:, 0:1], kvag.t[:], ALU.mult, ALU.mult,
                [psK.tr, ssb.tr, kvag.tr], [kvb.tr])
            tt(P, "dve", krf.t[:], psK.t[:, 256:320], rc[s % 2].t[:], ALU.mult, [psK.tr, rc[s % 2].tr], [krf.tr])
            tt(P, "dve", krf2.t[:], psK.t[:, 320:384], rs[s % 2].t[:], ALU.mult, [psK.tr, rs[s % 2].tr], [krf2.tr])
            tt(P, "pool", krb.t[:], krf.t[:], krf2.t[:], ALU.add, [krf.tr, krf2.tr], [krb.tr])
            kc_tr = t_kv[s // 4]
            for j in range(2):
                tp(P, psT.t[:, j * 128:(j + 1) * 128], kvb.t[:, j * 128:(j + 1) * 128], identb.t[:],
                   [kvb.tr, identb.tr], [psT.tr])
            hp = s % 2
            tp(P, psT.t[hp * 64:(hp + 1) * 64, 256:384], krb.t[:], identb.t[:], [krb.tr, identb.tr], [psT.tr])
            for j in range(2):
                cp(P, "act", ckvT.t[:, j, s * 128:(s + 1) * 128], psT.t[:, j * 128:(j + 1) * 128], [psT.tr], [kc_tr])
            cp(P, "dve", krT.t[hp * 64:(hp + 1) * 64, (s // 2) * 128:(s // 2 + 1) * 128],
               psT.t[hp * 64:(hp + 1) * 64, 256:384], [psT.tr], [kc_tr])
            if own:
                q0 = (s - 2) * 128
                for kk in range(8):
                    mm(P, psQ.t[:], z_.t[:, kk, :], wres.t[:, kk, 0:512], kk == 0, kk == 7, [z_.tr, wres.tr], [psQ.tr])
                act(P, junk.t[:, 0:512], psQ.t[:], AF.Square, [psQ.tr], [junk.tr, ssb.tr], accum_out=ssb.t[:, 1:2])
                ts(P, "dve", tmpb.t[:, 1:2], ssb.t[:, 1:2], 1.0 / 512, EPS, ALU.mult, ALU.add, [ssb.tr], [tmpb.tr])
                act(P, tmpb.t[:, 1:2], tmpb.t[:, 1:2], AF.Sqrt, [tmpb.tr], [tmpb.tr])
                P.op("dve", lambda e: e.reciprocal(out=ssb.t[:, 1:2], in_=tmpb.t[:, 1:2]), [tmpb.tr], [ssb.tr])
                stt(P, "dve", cqb.t[:], psQ.t[:], ssb.t[:, 1:2], qag.t[:], ALU.mult, ALU.mult,
                    [psQ.tr, ssb.tr, qag.tr], [cqb.tr])
                for j in range(4):
                    tp(P, psT.t[:, 512 + j * 128:512 + (j + 1) * 128], cqb.t[:, j * 128:(j + 1) * 128], identb.t[:],
                       [cqb.tr, identb.tr], [psT.tr])
                for j in range(4):
                    cp(P, "act" if j % 2 == 0 else "dve", cqT.t[:, j, q0:q0 + 128],
                       psT.t[:, 512 + j * 128:512 + (j + 1) * 128], [psT.tr], [t_cq[(s - 2) // 4]])
                for half in range(2):
                    pg = psG[half]
                    for n in range(2):
                        c0 = 896 + half * D + n * 512
                        for kk in range(8):
                            mm(P, pg.t[:, n * 512:(n + 1) * 512], z_.t[:, kk, :], wres.t[:, kk, c0:c0 + 512],
                               kk == 0, kk == 7, [z_.tr, wres.tr], [pg.tr])
                    act(P, sgf.t[:, half * D:(half + 1) * D], pg.t[:], AF.Silu, [pg.tr], [sgf.tr])
                st_ = sgT[0]
                for hq in range(4):
                    for j in range(4):
                        h = hq * 4 + j
                        tp(P, psF.t[:, j * 128:(j + 1) * 128], sgf.t[:, h * 128:(h + 1) * 128], identf.t[:],
                           [sgf.tr, identf.tr], [psF.tr])
                    cp(P, "dve" if hq % 2 == 0 else "act", st_.t[:, hq * 4:(hq + 1) * 4, :].rearrange("p h t -> p (h t)"),
                       psF.t[:], [psF.tr], [st_.tr])
                dma(P, "sp", SGT[:, :, q0:q0 + 128].rearrange("h v q -> v h q"), st_.t[:], [st_.tr], [t_SGT])
        P.end_phase()
    if stop_after <= 4:
        return finish(k, nc, P)

    with ExitStack() as S:
        qw = [P.buf(S, "qw", [128, 4, 256], BF16) for _ in range(2)]
        kw = [P.buf(S, "kw", [128, 2, 256], BF16) for _ in range(2)]
        KnT = P.buf(S, "KnT", [128, NK], BF16)
        Vh = P.buf(S, "Vh", [128, NS, 128], BF16)
        QnT = P.buf(S, "QnT", [128, NQ], BF16)
        QrT = P.buf(S, "QrT2", [128, NQ], BF16)
        sgt = P.buf(S, "sgt", [128, NQ], F32)
        rq = [[P.buf(S, "rq", [128, 512], F32) for _ in range(2)] for _ in range(2)]
        q1 = P.buf(S, "q1", [128, 512], F32)
        q2 = P.buf(S, "q2", [128, 512], F32)
        PT = [P.buf(S, "PT", [128, 1024], BF16) for _ in range(3)]
        laccA = P.buf(S, "laccA", [128, 1024], F32)
        laccB = P.buf(S, "laccB", [128, 1024], F32)
        onesf = P.buf(S, "onesf", [128, 128], F32)
        rl = P.buf(S, "rl", [128, 512], F32)
        yo = [P.buf(S, "yo", [128, 512], BF16) for _ in range(2)]
        yf = P.buf(S, "yf", [128, 512], F32)
        P.op("pool", lambda e: e.memset(onesf.t[:], 1.0), [], [onesf.tr])
        psS = [P.pbuf(S, "ps5S", [128, 1024]) for _ in range(3)]
        psO = P.pbuf(S, "ps5O", [128, 512])
        psL = P.pbuf(S, "ps5L", [128, 512])

        class _PX:
            def __init__(self, b, c0):
                self.tr = b.tr
                self.t = b.t[:, c0:c0 + 512]
        psX = [_PX(psS[i], j * 512) for j in range(2) for i in range(3)]
        t_kn = [Tr() for _ in range(17)]
        t_v = [Tr() for _ in range(17)]
        t_qn = [Tr() for _ in range(8)]
        ix = 0
        ip = 0
        for h in range(HEADS):
            qw_, kw_ = qw[h % 2], kw[h % 2]
            for kk in range(4):
                dma(P, "pool", qw_.t[:, kk, :], qbw[kk * 128:(kk + 1) * 128, h * 256:(h + 1) * 256], [], [qw_.tr])
            for kk in range(2):
                dma(P, "pool", kw_.t[:, kk, :], kvbw[kk * 128:(kk + 1) * 128, h * 256:(h + 1) * 256], [], [kw_.tr])
            dma(P, "sp", sgt.t[:], SGT[h], [t_SGT], [sgt.tr])
            for kc in range(17):
                c0 = kc * 512
                cn = min(512, NK - c0)
                px = psX[ix % 6]
                ix += 1
                for kk in range(2):
                    mm(P, px.t[:, 0:cn], kw_.t[:, kk, 0:128], ckvT.t[:, kk, c0:c0 + cn], kk == 0, kk == 1,
                       [kw_.tr, t_kv[kc]], [px.tr])
                cp(P, "dve" if kc % 2 == 0 else "pool" if False else "dve", KnT.t[:, c0:c0 + cn], px.t[:, 0:cn],
                   [px.tr], [t_kn[kc]])
                px = psX[ix % 6]
                ix += 1
                nt = cn // 128
                for j in range(nt):
                    s = kc * 4 + j
                    for kk in range(2):
                        mm(P, px.t[:, j * 128:(j + 1) * 128], ckvT.t[:, kk, s * 128:(s + 1) * 128], kw_.t[:, kk, 128:256],
                           kk == 0, kk == 1, [kw_.tr, t_kv[kc]], [px.tr])
                cp(P, "act", Vh.t[:, kc * 4:kc * 4 + nt, :].rearrange("p s v -> p (s v)"), px.t[:, 0:cn],
                   [px.tr], [t_v[kc]])
            for qc in range(8):
                c0 = qc * 512
                px = psX[ix % 6]
                ix += 1
                for kk in range(4):
                    mm(P, px.t[:], qw_.t[:, kk, 0:128], cqT.t[:, kk, c0:c0 + 512], kk == 0, kk == 3,
                       [qw_.tr, t_cq[qc]], [px.tr])
                cp(P, "dve", QnT.t[:, c0:c0 + 512], px.t[:], [px.tr], [t_qn[qc]])
                rq_ = rq[qc % 2]
                for hp in range(2):
                    dma(P, "sp", rq_[0].t[hp * 64:(hp + 1) * 64, :], ropeQ_cosT[:, c0:c0 + 512], [], [rq_[0].tr])
                    dma(P, "sp", rq_[1].t[hp * 64:(hp + 1) * 64, :], ropeQ_sinT[:, c0:c0 + 512], [], [rq_[1].tr])
                px = psX[ix % 6]
                ix += 1
                for hp in range(2):
                    for kk in range(4):
                        mm(P, px.t[hp * 64:(hp + 1) * 64, :], qw_.t[:, kk, 128:192], cqT.t[:, kk, c0:c0 + 512], kk == 0, kk == 3,
                           [qw_.tr, t_cq[qc]], [px.tr])
                tt(P, "dve", q1.t[:], px.t[:], rq_[0].t[:], ALU.mult, [px.tr, rq_[0].tr], [q1.tr])
                px = psX[ix % 6]
                ix += 1
                for hp in range(2):
                    for kk in range(4):
                        mm(P, px.t[hp * 64:(hp + 1) * 64, :], qw_.t[:, kk, 192:256], cqT.t[:, kk, c0:c0 + 512], kk == 0, kk == 3,
                           [qw_.tr, t_cq[qc]], [px.tr])
                tt(P, "dve", q2.t[:], px.t[:], rq_[1].t[:], ALU.mult, [px.tr, rq_[1].tr], [q2.tr])
                tt(P, "pool", QrT.t[:, c0:c0 + 512], q1.t[:], q2.t[:], ALU.add, [q1.tr, q2.tr], [t_qn[qc]])
            for qc in range(8):
                c0 = qc * 512
                NP = NS // 2
                for p in range(NP + 2):
                    if p < NP:
                        ps = psS[p % 3]
                        pt = PT[p % 3]
                        for j in range(2):
                            s = 2 * p + j
                            kc = s // 4
                            mm(P, ps.t[:, j * 512:(j + 1) * 512], KnT.t[:, s * 128:(s + 1) * 128], QnT.t[:, c0:c0 + 512],
                               True, False, [t_kn[kc], t_qn[qc]], [ps.tr])
                        for j in range(2):
                            s = 2 * p + j
                            kc = s // 4
                            mm(P, ps.t[:, j * 512:(j + 1) * 512], krT.t[j * 64:(j + 1) * 64, p * 128:(p + 1) * 128],
                               QrT.t[j * 64:(j + 1) * 64, c0:c0 + 512], False, True, [t_kv[kc], t_qn[qc]], [ps.tr])
                        act(P, pt.t[:], ps.t[:], AF.Exp, [ps.tr], [pt.tr], scale=MLA_SCALE)
                    if p >= 2:
                        q = p - 2
                        pt0 = PT[q % 3]
                        for j in range(2):
                            s0 = 2 * q + j
                            mm(P, psO.t[:], Vh.t[:, s0, :], pt0.t[:, j * 512:(j + 1) * 512], s0 == 0, s0 == NS - 1,
                               [t_v[s0 // 4], pt0.tr], [psO.tr])
                        la, eng = (laccA, "dve") if q % 3 != 2 else (laccB, "pool")
                        if q == 0 or q == 2:
                            cp(P, eng, la.t[:], pt0.t[:], [pt0.tr], [la.tr])
                        else:
                            tt(P, eng, la.t[:], la.t[:], pt0.t[:], ALU.add, [la.tr, pt0.tr], [la.tr])
                for j, (la, c1) in enumerate(((laccA, 0), (laccA, 512), (laccB, 0), (laccB, 512))):
                    mm(P, psL.t[:], onesf.t[:], la.t[:, c1:c1 + 512], j == 0, j == 3, [onesf.tr, la.tr], [psL.tr])
                P.op("dve", lambda e: e.reciprocal(out=rl.t[:], in_=psL.t[:]), [psL.tr], [rl.tr])
                tt(P, "dve", yf.t[:], psO.t[:], rl.t[:], ALU.mult, [psO.tr, rl.tr], [yf.tr])
                yo_ = yo[qc % 2]
                tt(P, "pool", yo_.t[:], yf.t[:], sgt.t[:, c0:c0 + 512], ALU.mult, [yf.tr, sgt.tr], [yo_.tr])
                dma(P, "sp", YT[h, :, c0:c0 + 512], yo_.t[:], [yo_.tr], [t_YT[h]])
        P.end_phase()
    if stop_after <= 5:
        return finish(k, nc, P)

    with ExitStack() as S:
        ow = P.buf(S, "ow1", [128, 16, D], BF16)
        load_w_bf16(ow, 0, out_w[1], 0, D, nk=16)
        gt1 = load_bc(S, "gt1", MS[1, 0:1, 2 * D:3 * D])
        fgb = load_bc(S, "fgb", final_g[0:1, :])
        yT = [P.buf(S, "yT6", [128, 16, 128], BF16) for _ in range(2)]
        xt = [P.buf(S, "xt6", [128, D], F32) for _ in range(2)]
        bt = P.buf(S, "bt6", [128, D], F32)
        junk = P.buf(S, "junk6", [128, D], BF16)
        ssb = P.buf(S, "ssb6", [128, 8], F32)
        tmpb = P.buf(S, "tmpb6", [128, 8], F32)
        psB = [P.pbuf(S, "ps6B", [128, D]) for _ in range(2)]
        for i in range(NOWN):
            s = i + 2
            q0 = i * 128
            y_ = yT[i % 2]
            x_ = xt[i % 2]
            pb = psB[i % 2]
            dma(P, "sp", y_.t[:], YT[:, :, q0:q0 + 128].rearrange("h v q -> v h q"), t_YT, [y_.tr])
            dma(P, "act", x_.t[:], H1[s], [t_H1[s]], [x_.tr])
            for n in range(2):
                for kk in range(16):
                    mm(P, pb.t[:, n * 512:(n + 1) * 512], y_.t[:, kk, :], ow.t[:, kk, n * 512:(n + 1) * 512],
                       kk == 0, kk == 15, [y_.tr, ow.tr], [pb.tr])
            tt(P, "dve", bt.t[:], pb.t[:], gt1.t[:], ALU.mult, [pb.tr, gt1.tr], [bt.tr])
            tt(P, "pool", x_.t[:], x_.t[:], bt.t[:], ALU.add, [x_.tr, bt.tr], [x_.tr])
            act(P, junk.t[:], x_.t[:], AF.Square, [x_.tr], [junk.tr, ssb.tr], accum_out=ssb.t[:, 0:1])
            rstd_from_ss(P, ssb, tmpb, D)
            stt(P, "dve", x_.t[:], x_.t[:], ssb.t[:, 0:1], fgb.t[:], ALU.mult, ALU.mult, [x_.tr, ssb.tr, fgb.tr], [x_.tr])
            dma(P, "sp", out_d[i], x_.t[:], [x_.tr], [])
        P.end_phase()
    return finish(k, nc, P)


def finish(k, nc, P):
    if any(len(v) for v in P.ops.values()):
        P.end_phase()
    P.close()
    return nc


def _consts(mir):
    ident = np.eye(128, dtype=np.float32)
    s = np.arange(128)[:, None]
    t = np.arange(128)[None, :]
    same = (s // 64) == (t // 64)
    tb_up = (same & (s <= t)).astype(np.float32)
    tb_dn = (same & (s >= t)).astype(np.float32)
    mid = (t // 64) * 64 + 32
    td_up = tb_up - tb_up[:, mid[0]]
    td_dn = tb_dn - tb_dn[:, mid[0]]
    tr_up = (same & (s > t)).astype(np.float32)
    tr_dn = (same & (s < t)).astype(np.float32)
    tris = np.stack([tb_up, td_up, tr_up, tb_dn, td_dn, tr_dn]).astype(np.float32)
    ci = np.zeros((128, 2), np.float32)
    ci[:64, 0] = 1
    ci[64:, 1] = 1
    p = np.arange(128)[:, None] % 64
    j = np.arange(64)[None, :]
    masks = np.stack([(p <= j), (p >= j)]).astype(np.float32)
    L = 384
    bands = np.zeros((20, 128, 128), np.float32)
    for g, w in enumerate((2, 4, 8, 16)):
        Bm = np.zeros((L, L), np.float64)
        for jl in range(L):
            to = (L - 1 - jl) if mir else jl
            lo = max(0, to - w // 2)
            hi = min(L, to + w // 2)
            cnt = hi - lo
            for so in range(lo, hi):
                sl = (L - 1 - so) if mir else so
                Bm[sl, jl] += 1.0 / cnt
            Bm[jl, jl] -= 1.0
        bands[g * 5 + 0] = Bm[0:128, 0:128]
        bands[g * 5 + 1] = Bm[128:256, 128:256]
        bands[g * 5 + 2] = Bm[256:384, 256:384]
        bands[g * 5 + 3] = Bm[0:128, 128:256]
        bands[g * 5 + 4] = Bm[256:384, 128:256]
    return ident, tris, ci, masks, bands


def _rope_tables(mir):
    inv = (10000.0 ** (-2.0 * np.arange(16, dtype=np.float32) / 32.0)).astype(np.float32)
    jl = np.arange(T)
    to = (T - 1 - jl) if mir else jl
    pr = (to // 64).astype(np.float32)
    pc = (to % 64).astype(np.float32)
    ang_r = pr[:, None] * inv[None, :]
    ang_c = pc[:, None] * inv[None, :]
    cr, sr = np.cos(ang_r).astype(np.float32), np.sin(ang_r).astype(np.float32)
    cc, sc = np.cos(ang_c).astype(np.float32), np.sin(ang_c).astype(np.float32)
    cos = np.concatenate([cr, cr, cc, cc], axis=1)
    sin = np.concatenate([-sr, sr, -sc, sc], axis=1)
    cosK = np.concatenate([np.ones((TC, 64), np.float32), cos], axis=0).reshape(NS, 128, 64)
    sinK = np.concatenate([np.zeros((TC, 64), np.float32), sin], axis=0).reshape(NS, 128, 64)
    cosQ = np.ascontiguousarray(cos[:NQ].T)
    sinQ = np.ascontiguousarray(sin[:NQ].T)
    return cosK, sinK, cosQ, sinQ


_PERM64 = np.concatenate([np.arange(16, 32), np.arange(0, 16), np.arange(48, 64), np.arange(32, 48)])

_NC_CACHE = {}


def _prep_inputs(x, c, ctx, c_ctx, ada_w, ada_b, norm_g, out_w, ev_in_w, hg_lb, hg_norm_g, pool_w,
                 pool_scale, od_in_w, qa_norm_g, qb_w, kva_norm_g, kvb_w, final_norm_g):
    f = lambda a: np.ascontiguousarray(np.asarray(a, dtype=np.float32))
    x, c, ctx, c_ctx = f(x), f(c), f(ctx), f(c_ctx)
    ev = f(ev_in_w)[0]
    od = f(od_in_w)[0]
    qb = f(qb_w)[0].reshape(512, HEADS, 192)
    qb_ext = np.concatenate([qb, qb[:, :, 128:][:, :, _PERM64]], axis=2).reshape(512, HEADS * 256)
    kr = od[:, 768:832]
    w1 = np.concatenate([od[:, 0:768], kr, kr[:, _PERM64], od[:, 832:]], axis=1)
    lb = f(hg_lb)
    shared = dict(ada_w=f(ada_w), ada_b=f(ada_b), norm_g=f(norm_g), final_g=f(final_norm_g)[None, :],
                  hgn=f(hg_norm_g), pool_w=f(pool_w)[0], pool_scale=f(pool_scale), out_w=f(out_w),
                  w1=np.ascontiguousarray(w1), qa_g=f(qa_norm_g), kva_g=f(kva_norm_g),
                  qbw=np.ascontiguousarray(qb_ext), kvbw=f(kvb_w)[0])
    per_mir = {}
    for mir in (False, True):
        ident, tris, ci, masks, bands = _consts(mir)
        cosK, sinK, cosQ, sinQ = _rope_tables(mir)
        q_, ff, fb, i_, ga, u_, gb = [ev[:, j * D:(j + 1) * D] for j in range(7)]
        w0 = np.concatenate([q_, fb, ff, i_, ga, u_, gb] if mir else [q_, ff, fb, i_, ga, u_, gb], axis=1)
        lbr = lb[::-1] if mir else lb
        per_mir[mir] = dict(ident=ident, tris=tris, ci=ci, masks=masks, bands=bands, ropeK_cos=cosK, ropeK_sin=sinK,
                            ropeQ_cosT=cosQ, ropeQ_sinT=sinQ, w0=np.ascontiguousarray(w0),
                            lbraw=np.ascontiguousarray(lbr))
    in_maps = []
    for core in range(8):
        b, half = core // 2, core % 2
        mir = half == 1
        xl = x[b][::-1] if mir else x[b]
        cl = ctx[b][::-1] if mir else ctx[b]
        xs = np.concatenate([cl, xl], axis=0).reshape(NS, 128, D)
        cvec = np.stack([c[b], c_ctx], axis=0)
        cvT = np.ascontiguousarray(cvec.reshape(2, 8, 128).transpose(2, 1, 0))
        m = dict(xs=np.ascontiguousarray(xs), cvT=cvT)
        m.update(shared)
        m.update(per_mir[mir])
        in_maps.append(m)
    return in_maps


def kernel(x, c, ctx, c_ctx, ada_w, ada_b, norm_g, out_w, ev_in_w, hg_lb, hg_norm_g, pool_w,
           pool_scale, od_in_w, qa_norm_g, qb_w, kva_norm_g, kvb_w, final_norm_g):
    in_maps = _prep_inputs(x, c, ctx, c_ctx, ada_w, ada_b, norm_g, out_w, ev_in_w, hg_lb, hg_norm_g, pool_w,
                           pool_scale, od_in_w, qa_norm_g, qb_w, kva_norm_g, kvb_w, final_norm_g)
    if "nc" not in _NC_CACHE:
        _NC_CACHE["nc"] = build_program()
    nc = _NC_CACHE["nc"]
    res = run_bass_kernel_spmd(nc, in_maps, core_ids=list(range(8)))
    out = np.zeros((4, T, D), np.float32)
    for core in range(8):
        b, half = core // 2, core % 2
        o = np.asarray(res.results[core]["out"]).reshape(NQ, D)
        if half == 1:
            out[b, NQ:] = o[::-1]
        else:
            out[b, :NQ] = o
    return out
```

```python
import numpy as np
from contextlib import ExitStack
import concourse.bass as bass
import concourse.mybir as mybir
from concourse.bass_utils import run_bass_kernel_spmd

F32 = mybir.dt.float32
BF16 = mybir.dt.bfloat16
AF = mybir.ActivationFunctionType
ALU = mybir.AluOpType
AX = mybir.AxisListType

D = 1024
T = 8192
TC = 256
NS = 66
NOWN = 32
NQ = 4096
NK = 8448
HEADS = 16
EPS = 1e-6
MLA_SCALE = 192.0 ** -0.5

ENGS = ("pe", "act", "dve", "pool", "sp")
NDMA = 12


class Tr:
    __slots__ = ("w", "r")

    def __init__(self):
        self.w = None
        self.r = {}


class Buf:
    __slots__ = ("t", "tr")

    def __init__(self, t):
        self.t = t
        self.tr = Tr()


class Prog:
    def __init__(self, nc):
        self.nc = nc
        self.es = ExitStack()
        self.ops = {e: [] for e in ENGS}
        self.cnt = {e: 0 for e in ENGS}
        self.waited = {e: {} for e in ENGS}
        self.sems = {}
        for e in ("pe", "act", "dve", "pool"):
            self.sems["c_" + e] = self.es.enter_context(nc.semaphore("c_" + e))
        self.dma_cnt = {}
        self.dma_rr = {}
        for q in ("sp", "act", "pool"):
            self.dma_rr[q] = 0
            for j in range(NDMA):
                k = "d_%s_%d" % (q, j)
                self.sems[k] = self.es.enter_context(nc.semaphore(k))
                self.dma_cnt[k] = 0
        self.final_events = {}
        self.uid = 0

    def buf(self, stack, name, shape, dtype):
        self.uid += 1
        return Buf(stack.enter_context(self.nc.sbuf_tensor("%s_%d" % (name, self.uid), list(shape), dtype)))

    def pbuf(self, stack, name, shape, dtype=F32):
        self.uid += 1
        return Buf(stack.enter_context(self.nc.psum_tensor("%s_%d" % (name, self.uid), list(shape), dtype)))

    def _deps(self, eng, reads, writes, pe_like, own=None):
        need = {}
        if own is None:
            own = "c_" + eng
        for t in reads:
            if t.w is not None:
                if t.w[0] == own and pe_like:
                    continue
                if need.get(t.w[0], 0) < t.w[1]:
                    need[t.w[0]] = t.w[1]
        for t in writes:
            if t.w is not None and t.w[0] != own:
                if need.get(t.w[0], 0) < t.w[1]:
                    need[t.w[0]] = t.w[1]
            for k, v in t.r.items():
                if k != own and need.get(k, 0) < v:
                    need[k] = v
        out = []
        wd = self.waited[eng]
        for k, v in need.items():
            if wd.get(k, 0) >= v:
                continue
            wd[k] = v
            out.append((k, v))
        return out

    def _commit(self, ev, reads, writes):
        k, v = ev
        for t in reads:
            if t.r.get(k, 0) < v:
                t.r[k] = v
        for t in writes:
            t.w = ev
            t.r = {}

    def op(self, eng, fn, reads=(), writes=()):
        waits = self._deps(eng, reads, writes, eng == "pe")
        self.cnt[eng] += 1
        ev = ("c_" + eng, self.cnt[eng])
        self.ops[eng].append((waits, fn, ("c_" + eng, 1)))
        self._commit(ev, reads, writes)

    def dma(self, q, fn, reads=(), writes=()):
        j = self.dma_rr[q]
        self.dma_rr[q] = (j + 1) % NDMA
        k = "d_%s_%d" % (q, j)
        waits = self._deps(q, reads, writes, False, own="__none__")
        prev = self.dma_cnt[k]
        if prev > 0 and self.waited[q].get(k, 0) < prev:
            self.waited[q][k] = prev
            waits.append((k, prev))
        self.dma_cnt[k] = prev + 16
        ev = (k, prev + 16)
        self.ops[q].append((waits, fn, (k, 16)))
        self._commit(ev, reads, writes)
        self.final_events[k] = prev + 16

    def barrier(self):
        allev = []
        for e in ("pe", "act", "dve", "pool"):
            if self.cnt[e] > 0:
                allev.append(("c_" + e, self.cnt[e]))
        for k, v in self.final_events.items():
            allev.append((k, v))
        for e in ENGS:
            w = []
            for k, v in allev:
                if k == "c_" + e:
                    continue
                if self.waited[e].get(k, 0) < v:
                    self.waited[e][k] = v
                    w.append((k, v))
            if w:
                self.ops[e].append((w, None, None))

    def emit_block(self):
        nc = self.nc
        sems = self.sems
        ops = self.ops
        self.ops = {e: [] for e in ENGS}
        with nc.Block() as block:
            def run(engname):
                def body(e):
                    for waits, fn, inc in ops[engname]:
                        for k, v in waits:
                            e.wait_ge(sems[k], v)
                        if fn is not None:
                            fn(e).then_inc(sems[inc[0]], inc[1])
                return body
            block.tensor(run("pe"))
            block.scalar(run("act"))
            block.vector(run("dve"))
            block.gpsimd(run("pool"))
            block.sync(run("sp"))

    def end_phase(self):
        self.barrier()
        self.emit_block()

    def close(self):
        self.es.close()


class K:
    pass


def mm(P, out, lhsT, rhs, start, stop, R, W):
    P.op("pe", lambda e: e.matmul(out, lhsT=lhsT, rhs=rhs, start=start, stop=stop), R, W)


def tp(P, out, in_, ident, R, W):
    P.op("pe", lambda e: e.transpose(out, in_, ident), R, W)


def act(P, out, in_, func, R, W, **kw):
    P.op("act", lambda e: e.activation(out=out, in_=in_, func=func, **kw), R, W)


def tt(P, eng, out, in0, in1, op, R, W):
    P.op(eng, lambda e: e.tensor_tensor(out=out, in0=in0, in1=in1, op=op), R, W)


def ts(P, eng, out, in0, s1, s2, op0, op1, R, W):
    if s2 is None:
        P.op(eng, lambda e: e.tensor_scalar(out=out, in0=in0, scalar1=s1, scalar2=None, op0=op0), R, W)
    else:
        P.op(eng, lambda e: e.tensor_scalar(out=out, in0=in0, scalar1=s1, scalar2=s2, op0=op0, op1=op1), R, W)


def stt(P, eng, out, in0, scalar, in1, op0, op1, R, W):
    P.op(eng, lambda e: e.scalar_tensor_tensor(out=out, in0=in0, scalar=scalar, in1=in1, op0=op0, op1=op1), R, W)


def cp(P, eng, out, in_, R, W):
    if eng == "act":
        P.op("act", lambda e: e.copy(out=out, in_=in_), R, W)
    else:
        P.op(eng, lambda e: e.tensor_copy(out=out, in_=in_), R, W)


def dma(P, q, out, in_, R, W):
    P.dma(q, lambda e: e.dma_start(out=out, in_=in_), R, W)


def rstd_from_ss(P, ss, tmp, n, c0=0, c1=1):
    ts(P, "dve", tmp.t[:, c0:c1], ss.t[:, c0:c1], 1.0 / n, EPS, ALU.mult, ALU.add, [ss.tr], [tmp.tr])
    act(P, tmp.t[:, c0:c1], tmp.t[:, c0:c1], AF.Sqrt, [tmp.tr], [tmp.tr])
    P.op("dve", lambda e: e.reciprocal(out=ss.t[:, c0:c1], in_=tmp.t[:, c0:c1]), [tmp.tr], [ss.tr])


def build_program(debug=False, stop_after=99, lim=None):
    nc = bass.Bass("TRN2", target_bir_lowering=False)
    P = Prog(nc)
    k = K()
    k.nc, k.P = nc, P

    def din(name, shape, dt=F32):
        return nc.dram_tensor(name, list(shape), dt, kind="ExternalInput").ap()

    dbg = set(debug) if debug else set()

    def dscr(name, shape, dt):
        return nc.dram_tensor(name, list(shape), dt, kind=("ExternalOutput" if name in dbg else "Internal")).ap()

    xs = din("xs", [NS, 128, D])
    cvT = din("cvT", [128, 8, 2])
    ada_w = din("ada_w", [2, D, 3 * D])
    ada_b = din("ada_b", [2, 3 * D])
    norm_g = din("norm_g", [2, D])
    final_g = din("final_g", [1, D])
    w0 = din("w0", [D, 7 * D])
    lbraw = din("lbraw", [2, 3, D])
    hgn = din("hgn", [1, D])
    pool_w = din("pool_w", [4, 256, 256])
    pool_scale = din("pool_scale", [1, D])
    out_w = din("out_w", [2, 2 * D, D])
    w1 = din("w1", [D, 2944])
    qa_g = din("qa_g", [1, 512])
    kva_g = din("kva_g", [1, 256])
    qbw = din("qbw", [512, HEADS * 256])
    kvbw = din("kvbw", [256, HEADS * 256])
    ropeK_cos = din("ropeK_cos", [NS, 128, 64])
    ropeK_sin = din("ropeK_sin", [NS, 128, 64])
    ropeQ_cosT = din("ropeQ_cosT", [64, NQ])
    ropeQ_sinT = din("ropeQ_sinT", [64, NQ])
    ident_d = din("ident", [128, 128])
    tris_d = din("tris", [6, 128, 128])
    ci_d = din("ci", [128, 2])
    masks_d = din("masks", [2, 128, 64])
    bands_d = din("bands", [20, 128, 128])
    out_d = nc.dram_tensor("out", [NOWN, 128, D], F32, kind="ExternalOutput").ap()

    MS = dscr("MS", [2, 2, 3 * D], F32)
    ZT = dscr("ZT", [NS, 128, 8, 128], BF16)
    ODN = dscr("ODN", [NS, 128, D], F32)
    OUP = dscr("OUP", [NS, 128, D], F32)
    H1 = dscr("H1", [NS, 128, D], F32)
    SGT = dscr("SGT", [HEADS, 128, NQ], F32)
    YT = dscr("YT", [HEADS, 128, NQ], BF16)
    t_MS, t_SGT = Tr(), Tr()
    t_ZT = [Tr() for _ in range(NS)]
    t_ODN = [Tr() for _ in range(NS)]
    t_OUP = [Tr() for _ in range(NS)]
    t_H1 = [Tr() for _ in range(NS)]
    t_YT = [Tr() for _ in range(HEADS)]

    G = P.es
    identb = P.buf(G, "identb", [128, 128], BF16)
    identf = P.buf(G, "identf", [128, 128], F32)
    tri = P.buf(G, "tri", [128, 6, 128], F32)
    cib = P.buf(G, "cib", [128, 2], F32)
    maskb = P.buf(G, "maskb", [128, 2, 64], F32)
    dma(P, "pool", identb.t[:], ident_d, [], [identb.tr])
    dma(P, "sp", identf.t[:], ident_d, [], [identf.tr])
    dma(P, "sp", tri.t[:], tris_d.rearrange("m s t -> s m t"), [], [tri.tr])
    dma(P, "sp", cib.t[:], ci_d, [], [cib.tr])
    dma(P, "sp", maskb.t[:], masks_d.rearrange("m s t -> s m t"), [], [maskb.tr])

    with ExitStack() as S:
        cv = P.buf(S, "cv", [128, 8, 2], F32)
        sv = P.buf(S, "sv", [128, 8, 2], F32)
        aw = [P.buf(S, "aw%d" % i, [128, 8, 512], F32) for i in range(2)]
        ab = [P.buf(S, "ab%d" % i, [2, 512], F32) for i in range(2)]
        mrow = [P.buf(S, "mrow%d" % i, [2, 512], F32) for i in range(2)]
        pm = [P.pbuf(S, "pm%d" % i, [128, 512]) for i in range(2)]
        dma(P, "sp", cv.t[:], cvT, [], [cv.tr])
        act(P, sv.t[:], cv.t[:], AF.Silu, [cv.tr], [sv.tr])
        it = 0
        for l in range(2):
            for n in range(6):
                a_, b_, m_, p_ = aw[it % 2], ab[it % 2], mrow[it % 2], pm[it % 2]
                it += 1
                for kk in range(8):
                    dma(P, "sp" if kk % 2 == 0 else "act", a_.t[:, kk, :],
                        ada_w[l, kk * 128:(kk + 1) * 128, n * 512:(n + 1) * 512], [], [a_.tr])
                dma(P, "sp", b_.t[:], ada_b[l:l + 1, n * 512:(n + 1) * 512].partition_broadcast(2), [], [b_.tr])
                for kk in range(8):
                    mm(P, p_.t[0:2, :], sv.t[:, kk, :], a_.t[:, kk, :], kk == 0, kk == 7, [sv.tr, a_.tr], [p_.tr])
                tt(P, "dve", m_.t[:], p_.t[0:2, :], b_.t[:], ALU.add, [p_.tr, b_.tr], [m_.tr])
                dma(P, "sp", MS[l, :, n * 512:(n + 1) * 512], m_.t[:], [m_.tr], [t_MS])
        P.end_phase()
    if stop_after <= 0:
        return finish(k, nc, P)

    def load_bc(S, name, src_row, q="sp"):
        n = src_row.shape[-1]
        b = P.buf(S, name, [128, n], F32)
        dma(P, q, b.t[:], src_row.partition_broadcast(128), [t_MS], [b.tr])
        return b

    def mod_tiles(S, l, want_gate):
        ng = load_bc(S, "ng", norm_g[l:l + 1, :])
        res = []
        for r in range(2):
            sh = load_bc(S, "sh", MS[l, r:r + 1, 0:D])
            sc = load_bc(S, "sc", MS[l, r:r + 1, D:2 * D])
            stt(P, "dve", sc.t[:], sc.t[:], 1.0, ng.t[:], ALU.add, ALU.mult, [sc.tr, ng.tr], [sc.tr])
            gt = None
            if want_gate[r]:
                gt = load_bc(S, "gt", MS[l, r:r + 1, 2 * D:3 * D])
            res.append((sc, sh, gt))
        return res

    def make_z(xt, Gb, SHb, junk, ssb, tmpb, zb):
        act(P, junk.t[:], xt.t[:], AF.Square, [xt.tr], [junk.tr, ssb.tr], accum_out=ssb.t[:, 0:1])
        rstd_from_ss(P, ssb, tmpb, D)
        stt(P, "dve", xt.t[:], xt.t[:], ssb.t[:, 0:1], Gb.t[:], ALU.mult, ALU.mult, [xt.tr, ssb.tr, Gb.tr], [xt.tr])
        tt(P, "pool", zb.t[:], xt.t[:], SHb.t[:], ALU.add, [xt.tr, SHb.tr], [zb.tr])

    def transpose_to(psT, src, ncols, dstT, evac_eng):
        nb = ncols // 128
        for j in range(nb):
            tp(P, psT.t[:, j * 128:(j + 1) * 128], src.t[:, j * 128:(j + 1) * 128], identb.t[:],
               [src.tr, identb.tr], [psT.tr])
        cp(P, evac_eng, dstT.t[:].rearrange("p k t -> p (k t)"), psT.t[:, 0:ncols], [psT.tr], [dstT.tr])

    def load_w_bf16(dst, dcol, src, scol, ncols, nk=8):
        for kk in range(nk):
            for c0 in range(0, ncols, 512):
                c1 = min(ncols, c0 + 512)
                dma(P, "pool", dst.t[:, kk, dcol + c0:dcol + c1],
                    src[kk * 128:(kk + 1) * 128, scol + c0:scol + c1], [], [dst.tr])

    def zipper(gens):
        gens = list(gens)
        while gens:
            nxt = []
            for g in gens:
                try:
                    next(g)
                    nxt.append(g)
                except StopIteration:
                    pass
            gens = nxt

    def hgrn_pass(dn):
        with ExitStack() as S:
            ncols = 3 * D
            wres = P.buf(S, "wres", [128, 8, ncols], BF16)
            load_w_bf16(wres, 0, w0, 0, D)
            load_w_bf16(wres, D, w0, (2 * D if dn else D), D)
            load_w_bf16(wres, 2 * D, w0, 3 * D, D)
            d_idx = 1 if dn else 0
            e3 = P.buf(S, "e3", [128, 3, D], F32)
            omlb = P.buf(S, "omlb", [128, D], F32)
            for j in range(3):
                dma(P, "sp", e3.t[:, j, :], lbraw[d_idx, j:j + 1, :].partition_broadcast(128), [], [e3.tr])
            act(P, e3.t[:], e3.t[:], AF.Exp, [e3.tr], [e3.tr])
            tt(P, "dve", omlb.t[:], e3.t[:, 1, :], e3.t[:, 2, :], ALU.add, [e3.tr], [omlb.tr])
            tt(P, "dve", e3.t[:, 2, :], omlb.t[:], e3.t[:, 0, :], ALU.add, [omlb.tr, e3.tr], [e3.tr])
            P.op("dve", lambda e: e.reciprocal(out=e3.t[:, 1, :], in_=e3.t[:, 2, :]), [e3.tr], [e3.tr])
            tt(P, "dve", omlb.t[:], omlb.t[:], e3.t[:, 1, :], ALU.mult, [omlb.tr, e3.tr], [omlb.tr])
            if dn:
                mods = mod_tiles(S, 0, (False, False))
            xt = [P.buf(S, "xt", [128, D], F32) for _ in range(2)]
            junk = P.buf(S, "junk", [128, D], BF16)
            ssz = P.buf(S, "ssz", [128, 8], F32)
            tmpz = P.buf(S, "tmpz", [128, 8], F32)
            zb = P.buf(S, "zb", [128, D], BF16)
            zT = [P.buf(S, "zT", [128, 8, 128], BF16) for _ in range(2)]
            HW = 512

            class HB:
                pass
            hb = []
            for hf in range(2):
                b = HB()
                b.St = P.buf(S, "St", [128, 4, 128], F32)
                b.Sbf = P.buf(S, "Sbf", [128, 4, 128], BF16)
                P.op("pool", lambda e, b=b: e.memset(b.St.t[:], 0.0), [], [b.St.tr])
                P.op("pool", lambda e, b=b: e.memset(b.Sbf.t[:], 0.0), [], [b.Sbf.tr])
                b.Qf = P.buf(S, "Qf", [128, HW], F32)
                b.S2 = P.buf(S, "S2", [128, HW], F32)
                b.Kf = P.buf(S, "Kf", [128, HW], F32)
                b.Gl = P.buf(S, "Gl", [128, HW], F32)
                b.V = P.buf(S, "V", [128, HW], BF16)
                b.E = [P.buf(S, "E", [128, HW], F32) for _ in range(2)]
                b.Qe = P.buf(S, "Qe", [128, HW], BF16)
                b.Ke = P.buf(S, "Ke", [128, HW], BF16)
                b.Qb = P.buf(S, "Qb", [128, HW], BF16)
                b.Kend = P.buf(S, "Kend", [128, HW], BF16)
                b.QeT = P.buf(S, "QeT", [128, 4, 128], BF16)
                b.KeT = P.buf(S, "KeT", [128, 4, 128], BF16)
                b.QbT = P.buf(S, "QbT", [128, 4, 128], BF16)
                b.Asb = P.buf(S, "Asb", [128, 4, 64], BF16)
                b.eb = P.buf(S, "eb", [128, 4, 2], F32)
                b.osb = [P.buf(S, "osb", [128, HW], F32) for _ in range(2)]
                b.ssb = P.buf(S, "ssb", [128, 4], F32)
                b.tmpb = P.buf(S, "tmpb", [128, 4], F32)
                b.psA = P.pbuf(S, "psA", [128, HW])
                b.psB = P.pbuf(S, "psB", [128, HW])
                b.psC = P.pbuf(S, "psC", [128, HW])
                b.psT = P.pbuf(S, "psT", [128, 2 * HW], BF16)
                hb.append(b)
            tb = 3 if dn else 0
            order = [1, 0] + list(range(NS - 1, 1, -1)) if dn else list(range(NS))
            corder = (1, 0) if dn else (0, 1)
            if lim is not None:
                order = order[:lim]

            def tr4(psT, src, dstT, evac_eng):
                for j in range(4):
                    tp(P, psT.t[:, j * 128:(j + 1) * 128], src.t[:, j * 128:(j + 1) * 128], identb.t[:],
                       [src.tr, identb.tr], [psT.tr])
                cp(P, evac_eng, dstT.t[:].rearrange("p k t -> p (k t)"), psT.t[:, 0:HW], [psT.tr], [dstT.tr])

            def half_job(it, s, hf, z_):
                b = hb[hf]
                c0 = hf * HW
                psA, psB, psC, psT = b.psA, b.psB, b.psC, b.psT
                for ps, col in ((psA, 0), (psB, D), (psC, 2 * D)):
                    for kk in range(8):
                        mm(P, ps.t[:], z_.t[:, kk, :], wres.t[:, kk, col + c0:col + c0 + HW], kk == 0, kk == 7,
                           [z_.tr, wres.tr], [ps.tr])
                yield
                act(P, b.Qf.t[:], psA.t[:], AF.Silu, [psA.tr], [b.Qf.tr])
                act(P, b.S2.t[:], psB.t[:], AF.Sigmoid, [psB.tr], [b.S2.tr], scale=-1.0)
                cp(P, "dve", b.V.t[:], psC.t[:], [psC.tr], [b.V.tr])
                yield
                tt(P, "dve", b.Kf.t[:], b.S2.t[:], omlb.t[:, c0:c0 + HW], ALU.mult, [b.S2.tr, omlb.tr], [b.Kf.tr])
                yield
                act(P, b.Gl.t[:], b.Kf.t[:], AF.Ln, [b.Kf.tr], [b.Gl.tr], scale=-1.0, bias=1.0)
                yield
                for ps, ti in ((psA, tb + 1), (psB, tb + 0), (psC, tb + 2)):
                    mm(P, ps.t[:], tri.t[:, ti, :], b.Gl.t[:], True, True, [tri.tr, b.Gl.tr], [ps.tr])
                yield
                act(P, b.E[0].t[:], psA.t[:], AF.Exp, [psA.tr], [b.E[0].tr])
                act(P, b.E[1].t[:], psA.t[:], AF.Exp, [psA.tr], [b.E[1].tr], scale=-1.0)
                yield
                tt(P, "dve", b.Qe.t[:], b.Qf.t[:], b.E[0].t[:], ALU.mult, [b.Qf.tr, b.E[0].tr], [b.Qe.tr])
                tt(P, "pool", b.Ke.t[:], b.Kf.t[:], b.E[1].t[:], ALU.mult, [b.Kf.tr, b.E[1].tr], [b.Ke.tr])
                yield
                act(P, b.E[0].t[:], psB.t[:], AF.Exp, [psB.tr], [b.E[0].tr])
                act(P, b.E[1].t[:], psC.t[:], AF.Exp, [psC.tr], [b.E[1].tr])
                yield
                tt(P, "dve", b.Qb.t[:], b.Qf.t[:], b.E[0].t[:], ALU.mult, [b.Qf.tr, b.E[0].tr], [b.Qb.tr])
                tt(P, "pool", b.Kend.t[:], b.Kf.t[:], b.E[1].t[:], ALU.mult, [b.Kf.tr, b.E[1].tr], [b.Kend.tr])
                yield
                tr4(psT, b.Qe, b.QeT, "act")
                yield
                tr4(psT, b.Ke, b.KeT, "dve")
                yield
                tr4(psT, b.Qb, b.QbT, "act")
                yield
                for h in range(4):
                    mm(P, psA.t[:, 256 + 2 * h:258 + 2 * h], b.Gl.t[:, h * 128:(h + 1) * 128], cib.t[:], True, True,
                       [b.Gl.tr, cib.tr], [psA.tr])
                for h in range(4):
                    for c in range(2):
                        mm(P, psA.t[c * 64:(c + 1) * 64, h * 64:(h + 1) * 64], b.KeT.t[:, h, c * 64:(c + 1) * 64],
                           b.QeT.t[:, h, c * 64:(c + 1) * 64], True, True, [b.KeT.tr, b.QeT.tr], [psA.tr])
                yield
                act(P, b.eb.t[:].rearrange("p h c -> p (h c)"), psA.t[:, 256:264], AF.Exp, [psA.tr], [b.eb.tr])
                tt(P, "dve", b.Asb.t[:], psA.t[:, 0:256].rearrange("p (h t) -> p h t", h=4),
                   maskb.t[:, (1 if dn else 0), :].unsqueeze(1).to_broadcast([128, 4, 64]), ALU.mult,
                   [psA.tr, maskb.tr], [b.Asb.tr])
                yield
                for c in corder:
                    r0, r1 = c * 64, (c + 1) * 64
                    for h in range(4):
                        hs = slice(h * 128, (h + 1) * 128)
                        mm(P, psB.t[r0:r1, hs], b.Asb.t[r0:r1, h, :], b.V.t[r0:r1, hs], True, False,
                           [b.Asb.tr, b.V.tr], [psB.tr])
                        mm(P, psB.t[r0:r1, hs], b.QbT.t[:, h, r0:r1], b.Sbf.t[:, h, :], False, True,
                           [b.QbT.tr, b.Sbf.tr], [psB.tr])
                    for h in range(4):
                        hs = slice(h * 128, (h + 1) * 128)
                        mm(P, psC.t[:, hs], b.Kend.t[r0:r1, hs], b.V.t[r0:r1, hs], True, True,
                           [b.Kend.tr, b.V.tr], [psC.tr])
                    tt(P, "dve", b.St.t[:], b.St.t[:], b.eb.t[:, :, c:c + 1].to_broadcast([128, 4, 128]), ALU.mult,
                       [b.St.tr, b.eb.tr], [b.St.tr])
                    yield
                    tt(P, "dve", b.St.t[:].rearrange("p h v -> p (h v)"), b.St.t[:].rearrange("p h v -> p (h v)"),
                       psC.t[:], ALU.add, [b.St.tr, psC.tr], [b.St.tr])
                    yield
                    cp(P, "act", b.Sbf.t[:], b.St.t[:], [b.St.tr], [b.Sbf.tr])
                    yield
                ob = b.osb[it % 2]
                cp(P, "act", ob.t[:], psB.t[:], [psB.tr], [ob.tr])
                yield
                if dn:
                    dma(P, "sp", ODN[s][:, c0:c0 + HW], ob.t[:], [ob.tr], [t_ODN[s]])
                else:
                    dma(P, "sp", OUP[s][:, c0:c0 + HW], ob.t[:], [ob.tr], [t_OUP[s]])

            for it, s in enumerate(order):
                z_ = zT[it % 2]
                if dn:
                    x_ = xt[it % 2]
                    r = 1 if s < 2 else 0
                    dma(P, "sp", x_.t[:], xs[s], [], [x_.tr])
                    make_z(x_, mods[r][0], mods[r][1], junk, ssz, tmpz, zb)
                    transpose_to(hb[0].psT, zb, D, z_, "act")
                    dma(P, "sp", ZT[s], z_.t[:], [z_.tr], [t_ZT[s]])
                else:
                    dma(P, "sp", z_.t[:], ZT[s], [t_ZT[s]], [z_.tr])
                zipper([half_job(it, s, 0, z_), half_job(it, s, 1, z_)])
            P.end_phase()

    hgrn_pass(True)
    if stop_after <= 1:
        return finish(k, nc, P)
    hgrn_pass(False)
    if stop_after <= 2:
        return finish(k, nc, P)

    with ExitStack() as S:
        wres = P.buf(S, "wres3", [128, 8, 3 * D], BF16)
        load_w_bf16(wres, 0, w0, 5 * D, 2 * D)
        load_w_bf16(wres, 2 * D, w0, 4 * D, D)
        hgb = load_bc(S, "hgb", hgn[0:1, :])
        SGA = [P.buf(S, "SGA", [128, D], F32) for _ in range(2)]
        oup = P.buf(S, "oup", [128, D], F32)
        odn = P.buf(S, "odn", [128, D], F32)
        ssb = P.buf(S, "ssb3", [128, 8], F32)
        tmpb = P.buf(S, "tmpb3", [128, 8], F32)
        ow = P.buf(S, "ow0", [128, 16, D], BF16)
        load_w_bf16(ow, 0, out_w[0], 0, D, nk=16)
        pw = P.buf(S, "pw", [128, 4, 2, 256], BF16)
        for g in range(4):
            for j in range(2):
                dma(P, "pool", pw.t[:, g, j, :], pool_w[g, j * 128:(j + 1) * 128, :], [], [pw.tr])
        bands = P.buf(S, "bands", [128, 20, 128], BF16)
        for j in range(20):
            dma(P, "pool", bands.t[:, j, :], bands_d[j], [], [bands.tr])
        psc = load_bc(S, "psc", pool_scale[0:1, :])
        gts = []
        for r in range(2):
            gts.append(load_bc(S, "gt0", MS[0, r:r + 1, 2 * D:3 * D]))
        zT = [P.buf(S, "zT3", [128, 8, 128], BF16) for _ in range(2)]
        U = [P.buf(S, "U", [128, D], BF16) for _ in range(3)]
        SGB = [P.buf(S, "SGB", [128, D], F32) for _ in range(2)]
        ppT = P.buf(S, "ppT", [128, 8, 128], BF16)
        yb = P.buf(S, "yb", [128, 2 * D], BF16)
        yT = P.buf(S, "yT", [128, 16, 128], BF16)
        bt = P.buf(S, "bt", [128, D], F32)
        xt = [P.buf(S, "xt3", [128, D], F32) for _ in range(2)]
        psA = P.pbuf(S, "ps3A", [128, D])
        psB = P.pbuf(S, "ps3B", [128, D])
        psC = P.pbuf(S, "ps3C", [128, D])
        psT = [P.pbuf(S, "ps3T", [128, D], BF16) for _ in range(2)]

        def seq_info(s):
            return (0, 1) if s < 2 else (2, NS - 1)

        def post(s, itp):
            lo, hi = seq_info(s)
            var = 0 if s == lo else (2 if s == hi else 1)
            for cb in range(8):
                g = cb // 2
                terms = []
                if s > lo:
                    terms.append((U[(s - 1) % 3], g * 5 + 3))
                terms.append((U[s % 3], g * 5 + var))
                if s < hi:
                    terms.append((U[(s + 1) % 3], g * 5 + 4))
                for i, (ub, bi) in enumerate(terms):
                    mm(P, psC.t[:, cb * 128:(cb + 1) * 128], ub.t[:, cb * 128:(cb + 1) * 128], bands.t[:, bi, :],
                       i == 0, i == len(terms) - 1, [ub.tr, bands.tr], [psC.tr])
            cp(P, "act", ppT.t[:].rearrange("p k t -> p (k t)"), psC.t[:], [psC.tr], [ppT.tr])
            for g in range(4):
                for j in range(2):
                    mm(P, psA.t[:, g * 256:(g + 1) * 256], ppT.t[:, 2 * g + j, :], pw.t[:, g, j, :], j == 0, j == 1,
                       [ppT.tr, pw.tr], [psA.tr])
            sg = SGB[s % 2]
            tt(P, "dve", bt.t[:], psA.t[:], psc.t[:], ALU.mult, [psA.tr, psc.tr], [bt.tr])
            tt(P, "pool", yb.t[:, D:2 * D], bt.t[:], sg.t[:], ALU.mult, [bt.tr, sg.tr], [yb.tr])
            dma(P, "act", oup.t[:], OUP[s], [t_OUP[s]], [oup.tr])
            dma(P, "act", odn.t[:], ODN[s], [t_ODN[s]], [odn.tr])
            tt(P, "dve", oup.t[:], oup.t[:], odn.t[:], ALU.add, [oup.tr, odn.tr], [oup.tr])
            tt(P, "pool", odn.t[:], oup.t[:], oup.t[:], ALU.mult, [oup.tr], [odn.tr])
            P.op("dve", lambda e: e.tensor_reduce(out=ssb.t[:], in_=odn.t[:].rearrange("p (h v) -> p h v", h=8),
                                                 axis=AX.X, op=ALU.add), [odn.tr], [ssb.tr])
            rstd_from_ss(P, ssb, tmpb, 128, 0, 8)
            tt(P, "dve", oup.t[:].rearrange("p (h v) -> p h v", h=8), oup.t[:].rearrange("p (h v) -> p h v", h=8),
               ssb.t[:].unsqueeze(2).to_broadcast([128, 8, 128]), ALU.mult, [oup.tr, ssb.tr], [oup.tr])
            tt(P, "pool", oup.t[:], oup.t[:], hgb.t[:], ALU.mult, [oup.tr, hgb.tr], [oup.tr])
            tt(P, "dve", yb.t[:, 0:D], oup.t[:], SGA[s % 2].t[:], ALU.mult, [oup.tr, SGA[s % 2].tr], [yb.tr])
            for j in range(16):
                tp(P, psT[j // 8].t[:, (j % 8) * 128:(j % 8 + 1) * 128], yb.t[:, j * 128:(j + 1) * 128], identb.t[:],
                   [yb.tr, identb.tr], [psT[j // 8].tr])
            cp(P, "act", yT.t[:, 0:8, :].rearrange("p k t -> p (k t)"), psT[0].t[:], [psT[0].tr], [yT.tr])
            cp(P, "dve", yT.t[:, 8:16, :].rearrange("p k t -> p (k t)"), psT[1].t[:], [psT[1].tr], [yT.tr])
            for n in range(2):
                for kk in range(16):
                    mm(P, psB.t[:, n * 512:(n + 1) * 512], yT.t[:, kk, :], ow.t[:, kk, n * 512:(n + 1) * 512],
                       kk == 0, kk == 15, [yT.tr, ow.tr], [psB.tr])
            x_ = xt[itp % 2]
            dma(P, "sp", x_.t[:], xs[s], [], [x_.tr])
            gt = gts[1 if s < 2 else 0]
            tt(P, "dve", bt.t[:], psB.t[:], gt.t[:], ALU.mult, [psB.tr, gt.tr], [bt.tr])
            tt(P, "pool", x_.t[:], x_.t[:], bt.t[:], ALU.add, [x_.tr, bt.tr], [x_.tr])
            dma(P, "sp", H1[s], x_.t[:], [x_.tr], [t_H1[s]])

        itp = 0
        for s in range(NS):
            z_ = zT[s % 2]
            dma(P, "sp", z_.t[:], ZT[s], [t_ZT[s]], [z_.tr])
            for ps, col in ((psA, 0), (psB, D)):
                for n in range(2):
                    for kk in range(8):
                        mm(P, ps.t[:, n * 512:(n + 1) * 512], z_.t[:, kk, :],
                           wres.t[:, kk, col + n * 512:col + (n + 1) * 512], kk == 0, kk == 7,
                           [z_.tr, wres.tr], [ps.tr])
            cp(P, "dve", U[s % 3].t[:], psA.t[:], [psA.tr], [U[s % 3].tr])
            act(P, SGB[s % 2].t[:], psB.t[:], AF.Silu, [psB.tr], [SGB[s % 2].tr])
            for n in range(2):
                for kk in range(8):
                    mm(P, psA.t[:, n * 512:(n + 1) * 512], z_.t[:, kk, :],
                       wres.t[:, kk, 2 * D + n * 512:2 * D + (n + 1) * 512], kk == 0, kk == 7,
                       [z_.tr, wres.tr], [psA.tr])
            act(P, SGA[s % 2].t[:], psA.t[:], AF.Silu, [psA.tr], [SGA[s % 2].tr])
            lo, hi = seq_info(s)
            if s > lo:
                post(s - 1, itp)
                itp += 1
            if s == hi:
                post(s, itp)
                itp += 1
        P.end_phase()
    if stop_after <= 3:
        return finish(k, nc, P)

    cqT = P.buf(G, "cqT", [128, 4, NQ], BF16)
    ckvT = P.buf(G, "ckvT", [128, 2, NK], BF16)
    krT = P.buf(G, "krT2", [128, NK // 2], BF16)
    t_cq = [Tr() for _ in range(8)]
    t_kv = [Tr() for _ in range(17)]

    with ExitStack() as S:
        wres = P.buf(S, "wres4", [128, 8, 2944], BF16)
        load_w_bf16(wres, 0, w1, 0, 2944)
        mods = mod_tiles(S, 1, (False, False))
        qag = load_bc(S, "qag", qa_g[0:1, :])
        kvag = load_bc(S, "kvag", kva_g[0:1, :])
        xt = [P.buf(S, "xt4", [128, D], F32) for _ in range(2)]
        junk = P.buf(S, "junk4", [128, D], BF16)
        ssb = P.buf(S, "ssb4", [128, 8], F32)
        tmpb = P.buf(S, "tmpb4", [128, 8], F32)
        zb = P.buf(S, "zb4", [128, D], BF16)
        zT = [P.buf(S, "zT4", [128, 8, 128], BF16) for _ in range(2)]
        kvb = P.buf(S, "kvb", [128, 256], BF16)
        rc = [P.buf(S, "rc", [128, 64], F32) for _ in range(2)]
        rs = [P.buf(S, "rs", [128, 64], F32) for _ in range(2)]
        krf = P.buf(S, "krf", [128, 64], F32)
        krf2 = P.buf(S, "krf2", [128, 64], F32)
        krb = P.buf(S, "krb", [128, 64], BF16)
        cqb = P.buf(S, "cqb", [128, 512], BF16)
        sgf = P.buf(S, "sgf", [128, 2 * D], F32)
        sgT = [P.buf(S, "sgT", [128, 16, 128], F32) for _ in range(1)]
        psK = P.pbuf(S, "ps4K", [128, 512])
        psQ = P.pbuf(S, "ps4Q", [128, 512])
        psG = [P.pbuf(S, "ps4G", [128, D]) for _ in range(2)]
        psT = P.pbuf(S, "ps4T", [128, D], BF16)
        psF = P.pbuf(S, "ps4F", [128, 512])
        for s in range(NS):
            x_ = xt[s % 2]
            z_ = zT[s % 2]
            own = 2 <= s < 2 + NOWN
            r = 1 if s < 2 else 0
            dma(P, "sp", x_.t[:], H1[s], [t_H1[s]], [x_.tr])
            dma(P, "act", rc[s % 2].t[:], ropeK_cos[s], [], [rc[s % 2].tr])
            dma(P, "act", rs[s % 2].t[:], ropeK_sin[s], [], [rs[s % 2].tr])
            make_z(x_, mods[r][0], mods[r][1], junk, ssb, tmpb, zb)
            transpose_to(psT, zb, D, z_, "act")
            for kk in range(8):
                mm(P, psK.t[:, 0:384], z_.t[:, kk, :], wres.t[:, kk, 512:896], kk == 0, kk == 7,
                   [z_.tr, wres.tr], [psK.tr])
            act(P, junk.t[:, 0:256], psK.t[:, 0:256], AF.Square, [psK.tr], [junk.tr, ssb.tr], accum_out=ssb.t[:, 0:1])
            rstd_from_ss(P, ssb, tmpb, 256)
            stt(P, "dve", kvb.t[:], psK.t[:, 0:256], ssb.t[:, 0:1], kvag.t[:], ALU.mult, ALU.mult,
                [psK.tr, ssb.tr, kvag.tr], [kvb.tr])
            tt(P, "dve", krf.t[:], psK.t[:, 256:320], rc[s % 2].t[:], ALU.mult, [psK.tr, rc[s % 2].tr], [krf.tr])
            tt(P, "dve", krf2.t[:], psK.t[:, 320:384], rs[s % 2].t[:], ALU.mult, [psK.tr, rs[s % 2].tr], [krf2.tr])
            tt(P, "pool", krb.t[:], krf.t[:], krf2.t[:], ALU.add, [krf.tr, krf2.tr], [krb.tr])
            kc_tr = t_kv[s // 4]
            for j in range(2):
                tp(P, psT.t[:, j * 128:(j + 1) * 128], kvb.t[:, j * 128:(j + 1) * 128], identb.t[:],
                   [kvb.tr, identb.tr], [psT.tr])
            hp = s % 2
            tp(P, psT.t[hp * 64:(hp + 1) * 64, 256:384], krb.t[:], identb.t[:], [krb.tr, identb.tr], [psT.tr])
            for j in range(2):
                cp(P, "act", ckvT.t[:, j, s * 128:(s + 1) * 128], psT.t[:, j * 128:(j + 1) * 128], [psT.tr], [kc_tr])
            cp(P, "dve", krT.t[hp * 64:(hp + 1) * 64, (s // 2) * 128:(s // 2 + 1) * 128],
               psT.t[hp * 64:(hp + 1) * 64, 256:384], [psT.tr], [kc_tr])
            if own:
                q0 = (s - 2) * 128
                for kk in range(8):
                    mm(P, psQ.t[:], z_.t[:, kk, :], wres.t[:, kk, 0:512], kk == 0, kk == 7, [z_.tr, wres.tr], [psQ.tr])
                act(P, junk.t[:, 0:512], psQ.t[:], AF.Square, [psQ.tr], [junk.tr, ssb.tr], accum_out=ssb.t[:, 1:2])
                ts(P, "dve", tmpb.t[:, 1:2], ssb.t[:, 1:2], 1.0 / 512, EPS, ALU.mult, ALU.add, [ssb.tr], [tmpb.tr])
                act(P, tmpb.t[:, 1:2], tmpb.t[:, 1:2], AF.Sqrt, [tmpb.tr], [tmpb.tr])
                P.op("dve", lambda e: e.reciprocal(out=ssb.t[:, 1:2], in_=tmpb.t[:, 1:2]), [tmpb.tr], [ssb.tr])
                stt(P, "dve", cqb.t[:], psQ.t[:], ssb.t[:, 1:2], qag.t[:], ALU.mult, ALU.mult,
                    [psQ.tr, ssb.tr, qag.tr], [cqb.tr])
                for j in range(4):
                    tp(P, psT.t[:, 512 + j * 128:512 + (j + 1) * 128], cqb.t[:, j * 128:(j + 1) * 128], identb.t[:],
                       [cqb.tr, identb.tr], [psT.tr])
                for j in range(4):
                    cp(P, "act" if j % 2 == 0 else "dve", cqT.t[:, j, q0:q0 + 128],
                       psT.t[:, 512 + j * 128:512 + (j + 1) * 128], [psT.tr], [t_cq[(s - 2) // 4]])
                for half in range(2):
                    pg = psG[half]
                    for n in range(2):
                        c0 = 896 + half * D + n * 512
                        for kk in range(8):
                            mm(P, pg.t[:, n * 512:(n + 1) * 512], z_.t[:, kk, :], wres.t[:, kk, c0:c0 + 512],
                               kk == 0, kk == 7, [z_.tr, wres.tr], [pg.tr])
                    act(P, sgf.t[:, half * D:(half + 1) * D], pg.t[:], AF.Silu, [pg.tr], [sgf.tr])
                st_ = sgT[0]
                for hq in range(4):
                    for j in range(4):
                        h = hq * 4 + j
                        tp(P, psF.t[:, j * 128:(j + 1) * 128], sgf.t[:, h * 128:(h + 1) * 128], identf.t[:],
                           [sgf.tr, identf.tr], [psF.tr])
                    cp(P, "dve" if hq % 2 == 0 else "act", st_.t[:, hq * 4:(hq + 1) * 4, :].rearrange("p h t -> p (h t)"),
                       psF.t[:], [psF.tr], [st_.tr])
                dma(P, "sp", SGT[:, :, q0:q0 + 128].rearrange("h v q -> v h q"), st_.t[:], [st_.tr], [t_SGT])
        P.end_phase()
    if stop_after <= 4:
        return finish(k, nc, P)

    with ExitStack() as S:
        qw = [P.buf(S, "qw", [128, 4, 256], BF16) for _ in range(2)]
        kw = [P.buf(S, "kw", [128, 2, 256], BF16) for _ in range(2)]
        KnT = P.buf(S, "KnT", [128, NK], BF16)
        Vh = P.buf(S, "Vh", [128, NS, 128], BF16)
        QnT = P.buf(S, "QnT", [128, NQ], BF16)
        QrT = P.buf(S, "QrT2", [128, NQ], BF16)
        sgt = P.buf(S, "sgt", [128, NQ], F32)
        rq = [[P.buf(S, "rq", [128, 512], F32) for _ in range(2)] for _ in range(2)]
        q1 = P.buf(S, "q1", [128, 512], F32)
        q2 = P.buf(S, "q2", [128, 512], F32)
        PT = [P.buf(S, "PT", [128, 1024], BF16) for _ in range(3)]
        laccA = P.buf(S, "laccA", [128, 1024], F32)
        laccB = P.buf(S, "laccB", [128, 1024], F32)
        onesf = P.buf(S, "onesf", [128, 128], F32)
        rl = P.buf(S, "rl", [128, 512], F32)
        yo = [P.buf(S, "yo", [128, 512], BF16) for _ in range(2)]
        yf = P.buf(S, "yf", [128, 512], F32)
        P.op("pool", lambda e: e.memset(onesf.t[:], 1.0), [], [onesf.tr])
        psS = [P.pbuf(S, "ps5S", [128, 1024]) for _ in range(3)]
        psO = P.pbuf(S, "ps5O", [128, 512])
        psL = P.pbuf(S, "ps5L", [128, 512])

        class _PX:
            def __init__(self, b, c0):
                self.tr = b.tr
                self.t = b.t[:, c0:c0 + 512]
        psX = [_PX(psS[i], j * 512) for j in range(2) for i in range(3)]
        t_kn = [Tr() for _ in range(17)]
        t_v = [Tr() for _ in range(17)]
        t_qn = [Tr() for _ in range(8)]
        ix = 0
        ip = 0
        for h in range(HEADS):
            qw_, kw_ = qw[h % 2], kw[h % 2]
            for kk in range(4):
                dma(P, "pool", qw_.t[:, kk, :], qbw[kk * 128:(kk + 1) * 128, h * 256:(h + 1) * 256], [], [qw_.tr])
            for kk in range(2):
                dma(P, "pool", kw_.t[:, kk, :], kvbw[kk * 128:(kk + 1) * 128, h * 256:(h + 1) * 256], [], [kw_.tr])
            dma(P, "sp", sgt.t[:], SGT[h], [t_SGT], [sgt.tr])
            for kc in range(17):
                c0 = kc * 512
                cn = min(512, NK - c0)
                px = psX[ix % 6]
                ix += 1
                for kk in range(2):
                    mm(P, px.t[:, 0:cn], kw_.t[:, kk, 0:128], ckvT.t[:, kk, c0:c0 + cn], kk == 0, kk == 1,
                       [kw_.tr, t_kv[kc]], [px.tr])
                cp(P, "dve" if kc % 2 == 0 else "pool" if False else "dve", KnT.t[:, c0:c0 + cn], px.t[:, 0:cn],
                   [px.tr], [t_kn[kc]])
                px = psX[ix % 6]
                ix += 1
                nt = cn // 128
                for j in range(nt):
                    s = kc * 4 + j
                    for kk in range(2):
                        mm(P, px.t[:, j * 128:(j + 1) * 128], ckvT.t[:, kk, s * 128:(s + 1) * 128], kw_.t[:, kk, 128:256],
                           kk == 0, kk == 1, [kw_.tr, t_kv[kc]], [px.tr])
                cp(P, "act", Vh.t[:, kc * 4:kc * 4 + nt, :].rearrange("p s v -> p (s v)"), px.t[:, 0:cn],
                   [px.tr], [t_v[kc]])
            for qc in range(8):
                c0 = qc * 512
                px = psX[ix % 6]
                ix += 1
                for kk in range(4):
                    mm(P, px.t[:], qw_.t[:, kk, 0:128], cqT.t[:, kk, c0:c0 + 512], kk == 0, kk == 3,
                       [qw_.tr, t_cq[qc]], [px.tr])
                cp(P, "dve", QnT.t[:, c0:c0 + 512], px.t[:], [px.tr], [t_qn[qc]])
                rq_ = rq[qc % 2]
                for hp in range(2):
                    dma(P, "sp", rq_[0].t[hp * 64:(hp + 1) * 64, :], ropeQ_cosT[:, c0:c0 + 512], [], [rq_[0].tr])
                    dma(P, "sp", rq_[1].t[hp * 64:(hp + 1) * 64, :], ropeQ_sinT[:, c0:c0 + 512], [], [rq_[1].tr])
                px = psX[ix % 6]
                ix += 1
                for hp in range(2):
                    for kk in range(4):
                        mm(P, px.t[hp * 64:(hp + 1) * 64, :], qw_.t[:, kk, 128:192], cqT.t[:, kk, c0:c0 + 512], kk == 0, kk == 3,
                           [qw_.tr, t_cq[qc]], [px.tr])
                tt(P, "dve", q1.t[:], px.t[:], rq_[0].t[:], ALU.mult, [px.tr, rq_[0].tr], [q1.tr])
                px = psX[ix % 6]
                ix += 1
                for hp in range(2):
                    for kk in range(4):
                        mm(P, px.t[hp * 64:(hp + 1) * 64, :], qw_.t[:, kk, 192:256], cqT.t[:, kk, c0:c0 + 512], kk == 0, kk == 3,
                           [qw_.tr, t_cq[qc]], [px.tr])
                tt(P, "dve", q2.t[:], px.t[:], rq_[1].t[:], ALU.mult, [px.tr, rq_[1].tr], [q2.tr])
                tt(P, "pool", QrT.t[:, c0:c0 + 512], q1.t[:], q2.t[:], ALU.add, [q1.tr, q2.tr], [t_qn[qc]])
            for qc in range(8):
                c0 = qc * 512
                NP = NS // 2
                for p in range(NP + 2):
                    if p < NP:
                        ps = psS[p % 3]
                        pt = PT[p % 3]
                        for j in range(2):
                            s = 2 * p + j
                            kc = s // 4
                            mm(P, ps.t[:, j * 512:(j + 1) * 512], KnT.t[:, s * 128:(s + 1) * 128], QnT.t[:, c0:c0 + 512],
                               True, False, [t_kn[kc], t_qn[qc]], [ps.tr])
                        for j in range(2):
                            s = 2 * p + j
                            kc = s // 4
                            mm(P, ps.t[:, j * 512:(j + 1) * 512], krT.t[j * 64:(j + 1) * 64, p * 128:(p + 1) * 128],
                               QrT.t[j * 64:(j + 1) * 64, c0:c0 + 512], False, True, [t_kv[kc], t_qn[qc]], [ps.tr])
                        act(P, pt.t[:], ps.t[:], AF.Exp, [ps.tr], [pt.tr], scale=MLA_SCALE)
                    if p >= 2:
                        q = p - 2
                        pt0 = PT[q % 3]
                        for j in range(2):
                            s0 = 2 * q + j
                            mm(P, psO.t[:], Vh.t[:, s0, :], pt0.t[:, j * 512:(j + 1) * 512], s0 == 0, s0 == NS - 1,
                               [t_v[s0 // 4], pt0.tr], [psO.tr])
                        la, eng = (laccA, "dve") if q % 3 != 2 else (laccB, "pool")
                        if q == 0 or q == 2:
                            cp(P, eng, la.t[:], pt0.t[:], [pt0.tr], [la.tr])
                        else:
                            tt(P, eng, la.t[:], la.t[:], pt0.t[:], ALU.add, [la.tr, pt0.tr], [la.tr])
                for j, (la, c1) in enumerate(((laccA, 0), (laccA, 512), (laccB, 0), (laccB, 512))):
                    mm(P, psL.t[:], onesf.t[:], la.t[:, c1:c1 + 512], j == 0, j == 3, [onesf.tr, la.tr], [psL.tr])
                P.op("dve", lambda e: e.reciprocal(out=rl.t[:], in_=psL.t[:]), [psL.tr], [rl.tr])
                tt(P, "dve", yf.t[:], psO.t[:], rl.t[:], ALU.mult, [psO.tr, rl.tr], [yf.tr])
                yo_ = yo[qc % 2]
                tt(P, "pool", yo_.t[:], yf.t[:], sgt.t[:, c0:c0 + 512], ALU.mult, [yf.tr, sgt.tr], [yo_.tr])
                dma(P, "sp", YT[h, :, c0:c0 + 512], yo_.t[:], [yo_.tr], [t_YT[h]])
        P.end_phase()
    if stop_after <= 5:
        return finish(k, nc, P)

    with ExitStack() as S:
        ow = P.buf(S, "ow1", [128, 16, D], BF16)
        load_w_bf16(ow, 0, out_w[1], 0, D, nk=16)
        gt1 = load_bc(S, "gt1", MS[1, 0:1, 2 * D:3 * D])
        fgb = load_bc(S, "fgb", final_g[0:1, :])
        yT = [P.buf(S, "yT6", [128, 16, 128], BF16) for _ in range(2)]
        xt = [P.buf(S, "xt6", [128, D], F32) for _ in range(2)]
        bt = P.buf(S, "bt6", [128, D], F32)
        junk = P.buf(S, "junk6", [128, D], BF16)
        ssb = P.buf(S, "ssb6", [128, 8], F32)
        tmpb = P.buf(S, "tmpb6", [128, 8], F32)
        psB = [P.pbuf(S, "ps6B", [128, D]) for _ in range(2)]
        for i in range(NOWN):
            s = i + 2
            q0 = i * 128
            y_ = yT[i % 2]
            x_ = xt[i % 2]
            pb = psB[i % 2]
            dma(P, "sp", y_.t[:], YT[:, :, q0:q0 + 128].rearrange("h v q -> v h q"), t_YT, [y_.tr])
            dma(P, "act", x_.t[:], H1[s], [t_H1[s]], [x_.tr])
            for n in range(2):
                for kk in range(16):
                    mm(P, pb.t[:, n * 512:(n + 1) * 512], y_.t[:, kk, :], ow.t[:, kk, n * 512:(n + 1) * 512],
                       kk == 0, kk == 15, [y_.tr, ow.tr], [pb.tr])
            tt(P, "dve", bt.t[:], pb.t[:], gt1.t[:], ALU.mult, [pb.tr, gt1.tr], [bt.tr])
            tt(P, "pool", x_.t[:], x_.t[:], bt.t[:], ALU.add, [x_.tr, bt.tr], [x_.tr])
            act(P, junk.t[:], x_.t[:], AF.Square, [x_.tr], [junk.tr, ssb.tr], accum_out=ssb.t[:, 0:1])
            rstd_from_ss(P, ssb, tmpb, D)
            stt(P, "dve", x_.t[:], x_.t[:], ssb.t[:, 0:1], fgb.t[:], ALU.mult, ALU.mult, [x_.tr, ssb.tr, fgb.tr], [x_.tr])
            dma(P, "sp", out_d[i], x_.t[:], [x_.tr], [])
        P.end_phase()
    return finish(k, nc, P)


def finish(k, nc, P):
    if any(len(v) for v in P.ops.values()):
        P.end_phase()
    P.close()
    return nc


def _consts(mir):
    ident = np.eye(128, dtype=np.float32)
    s = np.arange(128)[:, None]
    t = np.arange(128)[None, :]
    same = (s // 64) == (t // 64)
    tb_up = (same & (s <= t)).astype(np.float32)
    tb_dn = (same & (s >= t)).astype(np.float32)
    mid = (t // 64) * 64 + 32
    td_up = tb_up - tb_up[:, mid[0]]
    td_dn = tb_dn - tb_dn[:, mid[0]]
    tr_up = (same & (s > t)).astype(np.float32)
    tr_dn = (same & (s < t)).astype(np.float32)
    tris = np.stack([tb_up, td_up, tr_up, tb_dn, td_dn, tr_dn]).astype(np.float32)
    ci = np.zeros((128, 2), np.float32)
    ci[:64, 0] = 1
    ci[64:, 1] = 1
    p = np.arange(128)[:, None] % 64
    j = np.arange(64)[None, :]
    masks = np.stack([(p <= j), (p >= j)]).astype(np.float32)
    L = 384
    bands = np.zeros((20, 128, 128), np.float32)
    for g, w in enumerate((2, 4, 8, 16)):
        Bm = np.zeros((L, L), np.float64)
        for jl in range(L):
            to = (L - 1 - jl) if mir else jl
            lo = max(0, to - w // 2)
            hi = min(L, to + w // 2)
            cnt = hi - lo
            for so in range(lo, hi):
                sl = (L - 1 - so) if mir else so
                Bm[sl, jl] += 1.0 / cnt
            Bm[jl, jl] -= 1.0
        bands[g * 5 + 0] = Bm[0:128, 0:128]
        bands[g * 5 + 1] = Bm[128:256, 128:256]
        bands[g * 5 + 2] = Bm[256:384, 256:384]
        bands[g * 5 + 3] = Bm[0:128, 128:256]
        bands[g * 5 + 4] = Bm[256:384, 128:256]
    return ident, tris, ci, masks, bands


def _rope_tables(mir):
    inv = (10000.0 ** (-2.0 * np.arange(16, dtype=np.float32) / 32.0)).astype(np.float32)
    jl = np.arange(T)
    to = (T - 1 - jl) if mir else jl
    pr = (to // 64).astype(np.float32)
    pc = (to % 64).astype(np.float32)
    ang_r = pr[:, None] * inv[None, :]
    ang_c = pc[:, None] * inv[None, :]
    cr, sr = np.cos(ang_r).astype(np.float32), np.sin(ang_r).astype(np.float32)
    cc, sc = np.cos(ang_c).astype(np.float32), np.sin(ang_c).astype(np.float32)
    cos = np.concatenate([cr, cr, cc, cc], axis=1)
    sin = np.concatenate([-sr, sr, -sc, sc], axis=1)
    cosK = np.concatenate([np.ones((TC, 64), np.float32), cos], axis=0).reshape(NS, 128, 64)
    sinK = np.concatenate([np.zeros((TC, 64), np.float32), sin], axis=0).reshape(NS, 128, 64)
    cosQ = np.ascontiguousarray(cos[:NQ].T)
    sinQ = np.ascontiguousarray(sin[:NQ].T)
    return cosK, sinK, cosQ, sinQ


_PERM64 = np.concatenate([np.arange(16, 32), np.arange(0, 16), np.arange(48, 64), np.arange(32, 48)])

_NC_CACHE = {}


def _prep_inputs(x, c, ctx, c_ctx, ada_w, ada_b, norm_g, out_w, ev_in_w, hg_lb, hg_norm_g, pool_w,
                 pool_scale, od_in_w, qa_norm_g, qb_w, kva_norm_g, kvb_w, final_norm_g):
    f = lambda a: np.ascontiguousarray(np.asarray(a, dtype=np.float32))
    x, c, ctx, c_ctx = f(x), f(c), f(ctx), f(c_ctx)
    ev = f(ev_in_w)[0]
    od = f(od_in_w)[0]
    qb = f(qb_w)[0].reshape(512, HEADS, 192)
    qb_ext = np.concatenate([qb, qb[:, :, 128:][:, :, _PERM64]], axis=2).reshape(512, HEADS * 256)
    kr = od[:, 768:832]
    w1 = np.concatenate([od[:, 0:768], kr, kr[:, _PERM64], od[:, 832:]], axis=1)
    lb = f(hg_lb)
    shared = dict(ada_w=f(ada_w), ada_b=f(ada_b), norm_g=f(norm_g), final_g=f(final_norm_g)[None, :],
                  hgn=f(hg_norm_g), pool_w=f(pool_w)[0], pool_scale=f(pool_scale), out_w=f(out_w),
                  w1=np.ascontiguousarray(w1), qa_g=f(qa_norm_g), kva_g=f(kva_norm_g),
                  qbw=np.ascontiguousarray(qb_ext), kvbw=f(kvb_w)[0])
    per_mir = {}
    for mir in (False, True):
        ident, tris, ci, masks, bands = _consts(mir)
        cosK, sinK, cosQ, sinQ = _rope_tables(mir)
        q_, ff, fb, i_, ga, u_, gb = [ev[:, j * D:(j + 1) * D] for j in range(7)]
        w0 = np.concatenate([q_, fb, ff, i_, ga, u_, gb] if mir else [q_, ff, fb, i_, ga, u_, gb], axis=1)
        lbr = lb[::-1] if mir else lb
        per_mir[mir] = dict(ident=ident, tris=tris, ci=ci, masks=masks, bands=bands, ropeK_cos=cosK, ropeK_sin=sinK,
                            ropeQ_cosT=cosQ, ropeQ_sinT=sinQ, w0=np.ascontiguousarray(w0),
                            lbraw=np.ascontiguousarray(lbr))
    in_maps = []
    for core in range(8):
        b, half = core // 2, core % 2
        mir = half == 1
        xl = x[b][::-1] if mir else x[b]
        cl = ctx[b][::-1] if mir else ctx[b]
        xs = np.concatenate([cl, xl], axis=0).reshape(NS, 128, D)
        cvec = np.stack([c[b], c_ctx], axis=0)
        cvT = np.ascontiguousarray(cvec.reshape(2, 8, 128).transpose(2, 1, 0))
        m = dict(xs=np.ascontiguousarray(xs), cvT=cvT)
        m.update(shared)
        m.update(per_mir[mir])
        in_maps.append(m)
    return in_maps


def kernel(x, c, ctx, c_ctx, ada_w, ada_b, norm_g, out_w, ev_in_w, hg_lb, hg_norm_g, pool_w,
           pool_scale, od_in_w, qa_norm_g, qb_w, kva_norm_g, kvb_w, final_norm_g):
    in_maps = _prep_inputs(x, c, ctx, c_ctx, ada_w, ada_b, norm_g, out_w, ev_in_w, hg_lb, hg_norm_g, pool_w,
                           pool_scale, od_in_w, qa_norm_g, qb_w, kva_norm_g, kvb_w, final_norm_g)
    if "nc" not in _NC_CACHE:
        _NC_CACHE["nc"] = build_program()
    nc = _NC_CACHE["nc"]
    res = run_bass_kernel_spmd(nc, in_maps, core_ids=list(range(8)))
    out = np.zeros((4, T, D), np.float32)
    for core in range(8):
        b, half = core // 2, core % 2
        o = np.asarray(res.results[core]["out"]).reshape(NQ, D)
        if half == 1:
            out[b, NQ:] = o[::-1]
        else:
            out[b, :NQ] = o
    return out
```

```python
import numpy as np
from contextlib import ExitStack
import concourse.bass as bass
import concourse.mybir as mybir
from concourse.bass_utils import run_bass_kernel_spmd

F32 = mybir.dt.float32
BF16 = mybir.dt.bfloat16
AF = mybir.ActivationFunctionType
ALU = mybir.AluOpType
AX = mybir.AxisListType

D = 1024
T = 8192
TC = 256
NS = 66
NOWN = 32
NQ = 4096
NK = 8448
HEADS = 16
EPS = 1e-6
MLA_SCALE = 192.0 ** -0.5

ENGS = ("pe", "act", "dve", "pool", "sp")
NDMA = 12


class Tr:
    __slots__ = ("w", "r")

    def __init__(self):
        self.w = None
        self.r = {}


class Buf:
    __slots__ = ("t", "tr")

    def __init__(self, t):
        self.t = t
        self.tr = Tr()


class Prog:
    def __init__(self, nc):
        self.nc = nc
        self.es = ExitStack()
        self.ops = {e: [] for e in ENGS}
        self.cnt = {e: 0 for e in ENGS}
        self.waited = {e: {} for e in ENGS}
        self.sems = {}
        for e in ("pe", "act", "dve", "pool"):
            self.sems["c_" + e] = self.es.enter_context(nc.semaphore("c_" + e))
        self.dma_cnt = {}
        self.dma_rr = {}
        for q in ("sp", "act", "pool"):
            self.dma_rr[q] = 0
            for j in range(NDMA):
                k = "d_%s_%d" % (q, j)
                self.sems[k] = self.es.enter_context(nc.semaphore(k))
                self.dma_cnt[k] = 0
        self.final_events = {}
        self.uid = 0

    def buf(self, stack, name, shape, dtype):
        self.uid += 1
        return Buf(stack.enter_context(self.nc.sbuf_tensor("%s_%d" % (name, self.uid), list(shape), dtype)))

    def pbuf(self, stack, name, shape, dtype=F32):
        self.uid += 1
        return Buf(stack.enter_context(self.nc.psum_tensor("%s_%d" % (name, self.uid), list(shape), dtype)))

    def _deps(self, eng, reads, writes, pe_like, own=None):
        need = {}
        if own is None:
            own = "c_" + eng
        for t in reads:
            if t.w is not None:
                if t.w[0] == own and pe_like:
                    continue
                if need.get(t.w[0], 0) < t.w[1]:
                    need[t.w[0]] = t.w[1]
        for t in writes:
            if t.w is not None and t.w[0] != own:
                if need.get(t.w[0], 0) < t.w[1]:
                    need[t.w[0]] = t.w[1]
            for k, v in t.r.items():
                if k != own and need.get(k, 0) < v:
                    need[k] = v
        out = []
        wd = self.waited[eng]
        for k, v in need.items():
            if wd.get(k, 0) >= v:
                continue
            wd[k] = v
            out.append((k, v))
        return out

    def _commit(self, ev, reads, writes):
        k, v = ev
        for t in reads:
            if t.r.get(k, 0) < v:
                t.r[k] = v
        for t in writes:
            t.w = ev
            t.r = {}

    def op(self, eng, fn, reads=(), writes=()):
        waits = self._deps(eng, reads, writes, eng == "pe")
        self.cnt[eng] += 1
        ev = ("c_" + eng, self.cnt[eng])
        self.ops[eng].append((waits, fn, ("c_" + eng, 1)))
        self._commit(ev, reads, writes)

    def dma(self, q, fn, reads=(), writes=()):
        j = self.dma_rr[q]
        self.dma_rr[q] = (j + 1) % NDMA
        k = "d_%s_%d" % (q, j)
        waits = self._deps(q, reads, writes, False, own="__none__")
        prev = self.dma_cnt[k]
        if prev > 0 and self.waited[q].get(k, 0) < prev:
            self.waited[q][k] = prev
            waits.append((k, prev))
        self.dma_cnt[k] = prev + 16
        ev = (k, prev + 16)
        self.ops[q].append((waits, fn, (k, 16)))
        self._commit(ev, reads, writes)
        self.final_events[k] = prev + 16

    def barrier(self):
        allev = []
        for e in ("pe", "act", "dve", "pool"):
            if self.cnt[e] > 0:
                allev.append(("c_" + e, self.cnt[e]))
        for k, v in self.final_events.items():
            allev.append((k, v))
        for e in ENGS:
            w = []
            for k, v in allev:
                if k == "c_" + e:
                    continue
                if self.waited[e].get(k, 0) < v:
                    self.waited[e][k] = v
                    w.append((k, v))
            if w:
                self.ops[e].append((w, None, None))

    def emit_block(self):
        nc = self.nc
        sems = self.sems
        ops = self.ops
        self.ops = {e: [] for e in ENGS}
        with nc.Block() as block:
            def run(engname):
                def body(e):
                    for waits, fn, inc in ops[engname]:
                        for k, v in waits:
                            e.wait_ge(sems[k], v)
                        if fn is not None:
                            fn(e).then_inc(sems[inc[0]], inc[1])
                return body
            block.tensor(run("pe"))
            block.scalar(run("act"))
            block.vector(run("dve"))
            block.gpsimd(run("pool"))
            block.sync(run("sp"))

    def end_phase(self):
        self.barrier()
        self.emit_block()

    def close(self):
        self.es.close()


class K:
    pass


def mm(P, out, lhsT, rhs, start, stop, R, W):
    P.op("pe", lambda e: e.matmul(out, lhsT=lhsT, rhs=rhs, start=start, stop=stop), R, W)


def tp(P, out, in_, ident, R, W):
    P.op("pe", lambda e: e.transpose(out, in_, ident), R, W)


def act(P, out, in_, func, R, W, **kw):
    P.op("act", lambda e: e.activation(out=out, in_=in_, func=func, **kw), R, W)


def tt(P, eng, out, in0, in1, op, R, W):
    P.op(eng, lambda e: e.tensor_tensor(out=out, in0=in0, in1=in1, op=op), R, W)


def ts(P, eng, out, in0, s1, s2, op0, op1, R, W):
    if s2 is None:
        P.op(eng, lambda e: e.tensor_scalar(out=out, in0=in0, scalar1=s1, scalar2=None, op0=op0), R, W)
    else:
        P.op(eng, lambda e: e.tensor_scalar(out=out, in0=in0, scalar1=s1, scalar2=s2, op0=op0, op1=op1), R, W)


def stt(P, eng, out, in0, scalar, in1, op0, op1, R, W):
    P.op(eng, lambda e: e.scalar_tensor_tensor(out=out, in0=in0, scalar=scalar, in1=in1, op0=op0, op1=op1), R, W)


def cp(P, eng, out, in_, R, W):
    if eng == "act":
        P.op("act", lambda e: e.copy(out=out, in_=in_), R, W)
    else:
        P.op(eng, lambda e: e.tensor_copy(out=out, in_=in_), R, W)


def dma(P, q, out, in_, R, W):
    P.dma(q, lambda e: e.dma_start(out=out, in_=in_), R, W)


def rstd_from_ss(P, ss, tmp, n, c0=0, c1=1):
    ts(P, "dve", tmp.t[:, c0:c1], ss.t[:, c0:c1], 1.0 / n, EPS, ALU.mult, ALU.add, [ss.tr], [tmp.tr])
    act(P, tmp.t[:, c0:c1], tmp.t[:, c0:c1], AF.Sqrt, [tmp.tr], [tmp.tr])
    P.op("dve", lambda e: e.reciprocal(out=ss.t[:, c0:c1], in_=tmp.t[:, c0:c1]), [tmp.tr], [ss.tr])


def build_program(debug=False, stop_after=99, lim=None):
    nc = bass.Bass("TRN2", target_bir_lowering=False)
    P = Prog(nc)
    k = K()
    k.nc, k.P = nc, P

    def din(name, shape, dt=F32):
        return nc.dram_tensor(name, list(shape), dt, kind="ExternalInput").ap()

    dbg = set(debug) if debug else set()

    def dscr(name, shape, dt):
        return nc.dram_tensor(name, list(shape), dt, kind=("ExternalOutput" if name in dbg else "Internal")).ap()

    xs = din("xs", [NS, 128, D])
    cvT = din("cvT", [128, 8, 2])
    ada_w = din("ada_w", [2, D, 3 * D])
    ada_b = din("ada_b", [2, 3 * D])
    norm_g = din("norm_g", [2, D])
    final_g = din("final_g", [1, D])
    w0 = din("w0", [D, 7 * D])
    lbraw = din("lbraw", [2, 3, D])
    hgn = din("hgn", [1, D])
    pool_w = din("pool_w", [4, 256, 256])
    pool_scale = din("pool_scale", [1, D])
    out_w = din("out_w", [2, 2 * D, D])
    w1 = din("w1", [D, 2944])
    qa_g = din("qa_g", [1, 512])
    kva_g = din("kva_g", [1, 256])
    qbw = din("qbw", [512, HEADS * 256])
    kvbw = din("kvbw", [256, HEADS * 256])
    ropeK_cos = din("ropeK_cos", [NS, 128, 64])
    ropeK_sin = din("ropeK_sin", [NS, 128, 64])
    ropeQ_cosT = din("ropeQ_cosT", [64, NQ])
    ropeQ_sinT = din("ropeQ_sinT", [64, NQ])
    ident_d = din("ident", [128, 128])
    tris_d = din("tris", [6, 128, 128])
    ci_d = din("ci", [128, 2])
    masks_d = din("masks", [2, 128, 64])
    bands_d = din("bands", [20, 128, 128])
    out_d = nc.dram_tensor("out", [NOWN, 128, D], F32, kind="ExternalOutput").ap()

    MS = dscr("MS", [2, 2, 3 * D], F32)
    ZT = dscr("ZT", [NS, 128, 8, 128], BF16)
    ODN = dscr("ODN", [NS, 128, D], F32)
    OUP = dscr("OUP", [NS, 128, D], F32)
    H1 = dscr("H1", [NS, 128, D], F32)
    SGT = dscr("SGT", [HEADS, 128, NQ], F32)
    YT = dscr("YT", [HEADS, 128, NQ], BF16)
    t_MS, t_SGT = Tr(), Tr()
    t_ZT = [Tr() for _ in range(NS)]
    t_ODN = [Tr() for _ in range(NS)]
    t_OUP = [Tr() for _ in range(NS)]
    t_H1 = [Tr() for _ in range(NS)]
    t_YT = [Tr() for _ in range(HEADS)]

    G = P.es
    identb = P.buf(G, "identb", [128, 128], BF16)
    identf = P.buf(G, "identf", [128, 128], F32)
    tri = P.buf(G, "tri", [128, 6, 128], F32)
    cib = P.buf(G, "cib", [128, 2], F32)
    maskb = P.buf(G, "maskb", [128, 2, 64], F32)
    dma(P, "pool", identb.t[:], ident_d, [], [identb.tr])
    dma(P, "sp", identf.t[:], ident_d, [], [identf.tr])
    dma(P, "sp", tri.t[:], tris_d.rearrange("m s t -> s m t"), [], [tri.tr])
    dma(P, "sp", cib.t[:], ci_d, [], [cib.tr])
    dma(P, "sp", maskb.t[:], masks_d.rearrange("m s t -> s m t"), [], [maskb.tr])

    with ExitStack() as S:
        cv = P.buf(S, "cv", [128, 8, 2], F32)
        sv = P.buf(S, "sv", [128, 8, 2], F32)
        aw = [P.buf(S, "aw%d" % i, [128, 8, 512], F32) for i in range(2)]
        ab = [P.buf(S, "ab%d" % i, [2, 512], F32) for i in range(2)]
        mrow = [P.buf(S, "mrow%d" % i, [2, 512], F32) for i in range(2)]
        pm = [P.pbuf(S, "pm%d" % i, [128, 512]) for i in range(2)]
        dma(P, "sp", cv.t[:], cvT, [], [cv.tr])
        act(P, sv.t[:], cv.t[:], AF.Silu, [cv.tr], [sv.tr])
        it = 0
        for l in range(2):
            for n in range(6):
                a_, b_, m_, p_ = aw[it % 2], ab[it % 2], mrow[it % 2], pm[it % 2]
                it += 1
                for kk in range(8):
                    dma(P, "sp" if kk % 2 == 0 else "act", a_.t[:, kk, :],
                        ada_w[l, kk * 128:(kk + 1) * 128, n * 512:(n + 1) * 512], [], [a_.tr])
                dma(P, "sp", b_.t[:], ada_b[l:l + 1, n * 512:(n + 1) * 512].partition_broadcast(2), [], [b_.tr])
                for kk in range(8):
                    mm(P, p_.t[0:2, :], sv.t[:, kk, :], a_.t[:, kk, :], kk == 0, kk == 7, [sv.tr, a_.tr], [p_.tr])
                tt(P, "dve", m_.t[:], p_.t[0:2, :], b_.t[:], ALU.add, [p_.tr, b_.tr], [m_.tr])
                dma(P, "sp", MS[l, :, n * 512:(n + 1) * 512], m_.t[:], [m_.tr], [t_MS])
        P.end_phase()
    if stop_after <= 0:
        return finish(k, nc, P)

    def load_bc(S, name, src_row, q="sp"):
        n = src_row.shape[-1]
        b = P.buf(S, name, [128, n], F32)
        dma(P, q, b.t[:], src_row.partition_broadcast(128), [t_MS], [b.tr])
        return b

    def mod_tiles(S, l, want_gate):
        ng = load_bc(S, "ng", norm_g[l:l + 1, :])
        res = []
        for r in range(2):
            sh = load_bc(S, "sh", MS[l, r:r + 1, 0:D])
            sc = load_bc(S, "sc", MS[l, r:r + 1, D:2 * D])
            stt(P, "dve", sc.t[:], sc.t[:], 1.0, ng.t[:], ALU.add, ALU.mult, [sc.tr, ng.tr], [sc.tr])
            gt = None
            if want_gate[r]:
                gt = load_bc(S, "gt", MS[l, r:r + 1, 2 * D:3 * D])
            res.append((sc, sh, gt))
        return res

    def make_z(xt, Gb, SHb, junk, ssb, tmpb, zb):
        act(P, junk.t[:], xt.t[:], AF.Square, [xt.tr], [junk.tr, ssb.tr], accum_out=ssb.t[:, 0:1])
        rstd_from_ss(P, ssb, tmpb, D)
        stt(P, "dve", xt.t[:], xt.t[:], ssb.t[:, 0:1], Gb.t[:], ALU.mult, ALU.mult, [xt.tr, ssb.tr, Gb.tr], [xt.tr])
        tt(P, "pool", zb.t[:], xt.t[:], SHb.t[:], ALU.add, [xt.tr, SHb.tr], [zb.tr])

    def transpose_to(psT, src, ncols, dstT, evac_eng):
        nb = ncols // 128
        for j in range(nb):
            tp(P, psT.t[:, j * 128:(j + 1) * 128], src.t[:, j * 128:(j + 1) * 128], identb.t[:],
               [src.tr, identb.tr], [psT.tr])
        cp(P, evac_eng, dstT.t[:].rearrange("p k t -> p (k t)"), psT.t[:, 0:ncols], [psT.tr], [dstT.tr])

    def load_w_bf16(dst, dcol, src, scol, ncols, nk=8):
        for kk in range(nk):
            for c0 in range(0, ncols, 512):
                c1 = min(ncols, c0 + 512)
                dma(P, "pool", dst.t[:, kk, dcol + c0:dcol + c1],
                    src[kk * 128:(kk + 1) * 128, scol + c0:scol + c1], [], [dst.tr])

    def zipper(gens):
        gens = list(gens)
        while gens:
            nxt = []
            for g in gens:
                try:
                    next(g)
                    nxt.append(g)
                except StopIteration:
                    pass
            gens = nxt

    def hgrn_pass(dn):
        with ExitStack() as S:
            ncols = 3 * D
            wres = P.buf(S, "wres", [128, 8, ncols], BF16)
            load_w_bf16(wres, 0, w0, 0, D)
            load_w_bf16(wres, D, w0, (2 * D if dn else D), D)
            load_w_bf16(wres, 2 * D, w0, 3 * D, D)
            d_idx = 1 if dn else 0
            e3 = P.buf(S, "e3", [128, 3, D], F32)
            omlb = P.buf(S, "omlb", [128, D], F32)
            for j in range(3):
                dma(P, "sp", e3.t[:, j, :], lbraw[d_idx, j:j + 1, :].partition_broadcast(128), [], [e3.tr])
            act(P, e3.t[:], e3.t[:], AF.Exp, [e3.tr], [e3.tr])
            tt(P, "dve", omlb.t[:], e3.t[:, 1, :], e3.t[:, 2, :], ALU.add, [e3.tr], [omlb.tr])
            tt(P, "dve", e3.t[:, 2, :], omlb.t[:], e3.t[:, 0, :], ALU.add, [omlb.tr, e3.tr], [e3.tr])
            P.op("dve", lambda e: e.reciprocal(out=e3.t[:, 1, :], in_=e3.t[:, 2, :]), [e3.tr], [e3.tr])
            tt(P, "dve", omlb.t[:], omlb.t[:], e3.t[:, 1, :], ALU.mult, [omlb.tr, e3.tr], [omlb.tr])
            if dn:
                mods = mod_tiles(S, 0, (False, False))
            xt = [P.buf(S, "xt", [128, D], F32) for _ in range(2)]
            junk = P.buf(S, "junk", [128, D], BF16)
            ssz = P.buf(S, "ssz", [128, 8], F32)
            tmpz = P.buf(S, "tmpz", [128, 8], F32)
            zb = P.buf(S, "zb", [128, D], BF16)
            zT = [P.buf(S, "zT", [128, 8, 128], BF16) for _ in range(2)]
            HW = 512

            class HB:
                pass
            hb = []
            for hf in range(2):
                b = HB()
                b.St = P.buf(S, "St", [128, 4, 128], F32)
                b.Sbf = P.buf(S, "Sbf", [128, 4, 128], BF16)
                P.op("pool", lambda e, b=b: e.memset(b.St.t[:], 0.0), [], [b.St.tr])
                P.op("pool", lambda e, b=b: e.memset(b.Sbf.t[:], 0.0), [], [b.Sbf.tr])
                b.Qf = P.buf(S, "Qf", [128, HW], F32)
                b.S2 = P.buf(S, "S2", [128, HW], F32)
                b.Kf = P.buf(S, "Kf", [128, HW], F32)
                b.Gl = P.buf(S, "Gl", [128, HW], F32)
                b.V = P.buf(S, "V", [128, HW], BF16)
                b.E = [P.buf(S, "E", [128, HW], F32) for _ in range(2)]
                b.Qe = P.buf(S, "Qe", [128, HW], BF16)
                b.Ke = P.buf(S, "Ke", [128, HW], BF16)
                b.Qb = P.buf(S, "Qb", [128, HW], BF16)
                b.Kend = P.buf(S, "Kend", [128, HW], BF16)
                b.QeT = P.buf(S, "QeT", [128, 4, 128], BF16)
                b.KeT = P.buf(S, "KeT", [128, 4, 128], BF16)
                b.QbT = P.buf(S, "QbT", [128, 4, 128], BF16)
                b.Asb = P.buf(S, "Asb", [128, 4, 64], BF16)
                b.eb = P.buf(S, "eb", [128, 4, 2], F32)
                b.osb = [P.buf(S, "osb", [128, HW], F32) for _ in range(2)]
                b.ssb = P.buf(S, "ssb", [128, 4], F32)
                b.tmpb = P.buf(S, "tmpb", [128, 4], F32)
                b.psA = P.pbuf(S, "psA", [128, HW])
                b.psB = P.pbuf(S, "psB", [128, HW])
                b.psC = P.pbuf(S, "psC", [128, HW])
                b.psT = P.pbuf(S, "psT", [128, 2 * HW], BF16)
                hb.append(b)
            tb = 3 if dn else 0
            order = [1, 0] + list(range(NS - 1, 1, -1)) if dn else list(range(NS))
            corder = (1, 0) if dn else (0, 1)
            if lim is not None:
                order = order[:lim]

            def tr4(psT, src, dstT, evac_eng):
                for j in range(4):
                    tp(P, psT.t[:, j * 128:(j + 1) * 128], src.t[:, j * 128:(j + 1) * 128], identb.t[:],
                       [src.tr, identb.tr], [psT.tr])
                cp(P, evac_eng, dstT.t[:].rearrange("p k t -> p (k t)"), psT.t[:, 0:HW], [psT.tr], [dstT.tr])

            def half_job(it, s, hf, z_):
                b = hb[hf]
                c0 = hf * HW
                psA, psB, psC, psT = b.psA, b.psB, b.psC, b.psT
                for ps, col in ((psA, 0), (psB, D), (psC, 2 * D)):
                    for kk in range(8):
                        mm(P, ps.t[:], z_.t[:, kk, :], wres.t[:, kk, col + c0:col + c0 + HW], kk == 0, kk == 7,
                           [z_.tr, wres.tr], [ps.tr])
                yield
                act(P, b.Qf.t[:], psA.t[:], AF.Silu, [psA.tr], [b.Qf.tr])
                act(P, b.S2.t[:], psB.t[:], AF.Sigmoid, [psB.tr], [b.S2.tr], scale=-1.0)
                cp(P, "dve", b.V.t[:], psC.t[:], [psC.tr], [b.V.tr])
                yield
                tt(P, "dve", b.Kf.t[:], b.S2.t[:], omlb.t[:, c0:c0 + HW], ALU.mult, [b.S2.tr, omlb.tr], [b.Kf.tr])
                yield
                act(P, b.Gl.t[:], b.Kf.t[:], AF.Ln, [b.Kf.tr], [b.Gl.tr], scale=-1.0, bias=1.0)
                yield
                for ps, ti in ((psA, tb + 1), (psB, tb + 0), (psC, tb + 2)):
                    mm(P, ps.t[:], tri.t[:, ti, :], b.Gl.t[:], True, True, [tri.tr, b.Gl.tr], [ps.tr])
                yield
                act(P, b.E[0].t[:], psA.t[:], AF.Exp, [psA.tr], [b.E[0].tr])
                act(P, b.E[1].t[:], psA.t[:], AF.Exp, [psA.tr], [b.E[1].tr], scale=-1.0)
                yield
                tt(P, "dve", b.Qe.t[:], b.Qf.t[:], b.E[0].t[:], ALU.mult, [b.Qf.tr, b.E[0].tr], [b.Qe.tr])
                tt(P, "pool", b.Ke.t[:], b.Kf.t[:], b.E[1].t[:], ALU.mult, [b.Kf.tr, b.E[1].tr], [b.Ke.tr])
                yield
                act(P, b.E[0].t[:], psB.t[:], AF.Exp, [psB.tr], [b.E[0].tr])
                act(P, b.E[1].t[:], psC.t[:], AF.Exp, [psC.tr], [b.E[1].tr])
                yield
                tt(P, "dve", b.Qb.t[:], b.Qf.t[:], b.E[0].t[:], ALU.mult, [b.Qf.tr, b.E[0].tr], [b.Qb.tr])
                tt(P, "pool", b.Kend.t[:], b.Kf.t[:], b.E[1].t[:], ALU.mult, [b.Kf.tr, b.E[1].tr], [b.Kend.tr])
                yield
                tr4(psT, b.Qe, b.QeT, "act")
                yield
                tr4(psT, b.Ke, b.KeT, "dve")
                yield
                tr4(psT, b.Qb, b.QbT, "act")
                yield
                for h in range(4):
                    mm(P, psA.t[:, 256 + 2 * h:258 + 2 * h], b.Gl.t[:, h * 128:(h + 1) * 128], cib.t[:], True, True,
                       [b.Gl.tr, cib.tr], [psA.tr])
                for h in range(4):
                    for c in range(2):
                        mm(P, psA.t[c * 64:(c + 1) * 64, h * 64:(h + 1) * 64], b.KeT.t[:, h, c * 64:(c + 1) * 64],
                           b.QeT.t[:, h, c * 64:(c + 1) * 64], True, True, [b.KeT.tr, b.QeT.tr], [psA.tr])
                yield
                act(P, b.eb.t[:].rearrange("p h c -> p (h c)"), psA.t[:, 256:264], AF.Exp, [psA.tr], [b.eb.tr])
                tt(P, "dve", b.Asb.t[:], psA.t[:, 0:256].rearrange("p (h t) -> p h t", h=4),
                   maskb.t[:, (1 if dn else 0), :].unsqueeze(1).to_broadcast([128, 4, 64]), ALU.mult,
                   [psA.tr, maskb.tr], [b.Asb.tr])
                yield
                for c in corder:
                    r0, r1 = c * 64, (c + 1) * 64
                    for h in range(4):
                        hs = slice(h * 128, (h + 1) * 128)
                        mm(P, psB.t[r0:r1, hs], b.Asb.t[r0:r1, h, :], b.V.t[r0:r1, hs], True, False,
                           [b.Asb.tr, b.V.tr], [psB.tr])
                        mm(P, psB.t[r0:r1, hs], b.QbT.t[:, h, r0:r1], b.Sbf.t[:, h, :], False, True,
                           [b.QbT.tr, b.Sbf.tr], [psB.tr])
                    for h in range(4):
                        hs = slice(h * 128, (h + 1) * 128)
                        mm(P, psC.t[:, hs], b.Kend.t[r0:r1, hs], b.V.t[r0:r1, hs], True, True,
                           [b.Kend.tr, b.V.tr], [psC.tr])
                    tt(P, "dve", b.St.t[:], b.St.t[:], b.eb.t[:, :, c:c + 1].to_broadcast([128, 4, 128]), ALU.mult,
                       [b.St.tr, b.eb.tr], [b.St.tr])
                    yield
                    tt(P, "dve", b.St.t[:].rearrange("p h v -> p (h v)"), b.St.t[:].rearrange("p h v -> p (h v)"),
                       psC.t[:], ALU.add, [b.St.tr, psC.tr], [b.St.tr])
                    yield
                    cp(P, "act", b.Sbf.t[:], b.St.t[:], [b.St.tr], [b.Sbf.tr])
                    yield
                ob = b.osb[it % 2]
                cp(P, "act", ob.t[:], psB.t[:], [psB.tr], [ob.tr])
                yield
                if dn:
                    dma(P, "sp", ODN[s][:, c0:c0 + HW], ob.t[:], [ob.tr], [t_ODN[s]])
                else:
                    dma(P, "sp", OUP[s][:, c0:c0 + HW], ob.t[:], [ob.tr], [t_OUP[s]])

            for it, s in enumerate(order):
                z_ = zT[it % 2]
                if dn:
                    x_ = xt[it % 2]
                    r = 1 if s < 2 else 0
                    dma(P, "sp", x_.t[:], xs[s], [], [x_.tr])
                    make_z(x_, mods[r][0], mods[r][1], junk, ssz, tmpz, zb)
                    transpose_to(hb[0].psT, zb, D, z_, "act")
                    dma(P, "sp", ZT[s], z_.t[:], [z_.tr], [t_ZT[s]])
                else:
                    dma(P, "sp", z_.t[:], ZT[s], [t_ZT[s]], [z_.tr])
                zipper([half_job(it, s, 0, z_), half_job(it, s, 1, z_)])
            P.end_phase()

    hgrn_pass(True)
    if stop_after <= 1:
        return finish(k, nc, P)
    hgrn_pass(False)
    if stop_after <= 2:
        return finish(k, nc, P)

    with ExitStack() as S:
        wres = P.buf(S, "wres3", [128, 8, 3 * D], BF16)
        load_w_bf16(wres, 0, w0, 5 * D, 2 * D)
        load_w_bf16(wres, 2 * D, w0, 4 * D, D)
        hgb = load_bc(S, "hgb", hgn[0:1, :])
        SGA = [P.buf(S, "SGA", [128, D], F32) for _ in range(2)]
        oup = P.buf(S, "oup", [128, D], F32)
        odn = P.buf(S, "odn", [128, D], F32)
        ssb = P.buf(S, "ssb3", [128, 8], F32)
        tmpb = P.buf(S, "tmpb3", [128, 8], F32)
        ow = P.buf(S, "ow0", [128, 16, D], BF16)
        load_w_bf16(ow, 0, out_w[0], 0, D, nk=16)
        pw = P.buf(S, "pw", [128, 4, 2, 256], BF16)
        for g in range(4):
            for j in range(2):
                dma(P, "pool", pw.t[:, g, j, :], pool_w[g, j * 128:(j + 1) * 128, :], [], [pw.tr])
        bands = P.buf(S, "bands", [128, 20, 128], BF16)
        for j in range(20):
            dma(P, "pool", bands.t[:, j, :], bands_d[j], [], [bands.tr])
        psc = load_bc(S, "psc", pool_scale[0:1, :])
        gts = []
        for r in range(2):
            gts.append(load_bc(S, "gt0", MS[0, r:r + 1, 2 * D:3 * D]))
        zT = [P.buf(S, "zT3", [128, 8, 128], BF16) for _ in range(2)]
        U = [P.buf(S, "U", [128, D], BF16) for _ in range(4)]
        SGB = [P.buf(S, "SGB", [128, D], F32) for _ in range(3)]
        SGA3 = [P.buf(S, "SGA3", [128, D], F32) for _ in range(3)]
        ppT = P.buf(S, "ppT", [128, 8, 128], BF16)
        yb = P.buf(S, "yb", [128, 2 * D], BF16)
        yT = P.buf(S, "yT", [128, 16, 128], BF16)
        bt = P.buf(S, "bt", [128, D], F32)
        bt2 = P.buf(S, "bt2", [128, D], F32)
        xt = [P.buf(S, "xt3", [128, D], F32) for _ in range(2)]
        psM = [P.pbuf(S, "ps3M", [128, 512]) for _ in range(2)]
        psC = P.pbuf(S, "ps3C", [128, D])
        psD = P.pbuf(S, "ps3D", [128, D])
        psT = P.pbuf(S, "ps3T", [128, D], BF16)

        def seq_info(s):
            return (0, 1) if s < 2 else (2, NS - 1)

        def main_gen(s):
            z_ = zT[s % 2]
            dma(P, "sp", z_.t[:], ZT[s], [t_ZT[s]], [z_.tr])
            gi = 0
            for col, kind in ((0, "u"), (D, "gb"), (2 * D, "ga")):
                for n in range(2):
                    ps = psM[gi % 2]
                    gi += 1
                    for kk in range(8):
                        mm(P, ps.t[:], z_.t[:, kk, :], wres.t[:, kk, col + n * 512:col + (n + 1) * 512],
                           kk == 0, kk == 7, [z_.tr, wres.tr], [ps.tr])
                    yield
                    cs = slice(n * 512, (n + 1) * 512)
                    if kind == "u":
                        cp(P, "dve", U[s % 4].t[:, cs], ps.t[:], [ps.tr], [U[s % 4].tr])
                    elif kind == "gb":
                        act(P, SGB[s % 3].t[:, cs], ps.t[:], AF.Silu, [ps.tr], [SGB[s % 3].tr])
                    else:
                        act(P, SGA3[s % 3].t[:, cs], ps.t[:], AF.Silu, [ps.tr], [SGA3[s % 3].tr])
                    yield

        def post_gen(s, itp):
            lo, hi = seq_info(s)
            var = 0 if s == lo else (2 if s == hi else 1)
            dma(P, "act", oup.t[:], OUP[s], [t_OUP[s]], [oup.tr])
            dma(P, "act", odn.t[:], ODN[s], [t_ODN[s]], [odn.tr])
            x_ = xt[itp % 2]
            dma(P, "sp", x_.t[:], xs[s], [], [x_.tr])
            for cb in range(8):
                g = cb // 2
                terms = []
                if s > lo:
                    terms.append((U[(s - 1) % 4], g * 5 + 3))
                terms.append((U[s % 4], g * 5 + var))
                if s < hi:
                    terms.append((U[(s + 1) % 4], g * 5 + 4))
                for i, (ub, bi) in enumerate(terms):
                    mm(P, psC.t[:, cb * 128:(cb + 1) * 128], ub.t[:, cb * 128:(cb + 1) * 128], bands.t[:, bi, :],
                       i == 0, i == len(terms) - 1, [ub.tr, bands.tr], [psC.tr])
            yield
            cp(P, "act", ppT.t[:].rearrange("p k t -> p (k t)"), psC.t[:], [psC.tr], [ppT.tr])
            tt(P, "dve", oup.t[:], oup.t[:], odn.t[:], ALU.add, [oup.tr, odn.tr], [oup.tr])
            yield
            for g in range(4):
                for j in range(2):
                    mm(P, psC.t[:, g * 256:(g + 1) * 256], ppT.t[:, 2 * g + j, :], pw.t[:, g, j, :], j == 0, j == 1,
                       [ppT.tr, pw.tr], [psC.tr])
            tt(P, "pool", odn.t[:], oup.t[:], oup.t[:], ALU.mult, [oup.tr], [odn.tr])
            yield
            tt(P, "dve", bt.t[:], psC.t[:], psc.t[:], ALU.mult, [psC.tr, psc.tr], [bt.tr])
            P.op("dve", lambda e: e.tensor_reduce(out=ssb.t[:], in_=odn.t[:].rearrange("p (h v) -> p h v", h=8),
                                                 axis=AX.X, op=ALU.add), [odn.tr], [ssb.tr])
            yield
            tt(P, "pool", yb.t[:, D:2 * D], bt.t[:], SGB[s % 3].t[:], ALU.mult, [bt.tr, SGB[s % 3].tr], [yb.tr])
            ts(P, "dve", tmpb.t[:], ssb.t[:], 1.0 / 128, EPS, ALU.mult, ALU.add, [ssb.tr], [tmpb.tr])
            yield
            act(P, tmpb.t[:], tmpb.t[:], AF.Sqrt, [tmpb.tr], [tmpb.tr])
            yield
            P.op("dve", lambda e: e.reciprocal(out=ssb.t[:], in_=tmpb.t[:]), [tmpb.tr], [ssb.tr])
            yield
            tt(P, "dve", oup.t[:].rearrange("p (h v) -> p h v", h=8), oup.t[:].rearrange("p (h v) -> p h v", h=8),
               ssb.t[:].unsqueeze(2).to_broadcast([128, 8, 128]), ALU.mult, [oup.tr, ssb.tr], [oup.tr])
            yield
            tt(P, "pool", oup.t[:], oup.t[:], hgb.t[:], ALU.mult, [oup.tr, hgb.tr], [oup.tr])
            yield
            tt(P, "dve", yb.t[:, 0:D], oup.t[:], SGA3[s % 3].t[:], ALU.mult, [oup.tr, SGA3[s % 3].tr], [yb.tr])
            yield
            for rnd in range(2):
                for j in range(8):
                    jj = rnd * 8 + j
                    tp(P, psT.t[:, j * 128:(j + 1) * 128], yb.t[:, jj * 128:(jj + 1) * 128], identb.t[:],
                       [yb.tr, identb.tr], [psT.tr])
                yield
                cp(P, "act" if rnd == 0 else "dve", yT.t[:, rnd * 8:(rnd + 1) * 8, :].rearrange("p k t -> p (k t)"),
                   psT.t[:], [psT.tr], [yT.tr])
                yield
            for n in range(2):
                for kk in range(16):
                    mm(P, psD.t[:, n * 512:(n + 1) * 512], yT.t[:, kk, :], ow.t[:, kk, n * 512:(n + 1) * 512],
                       kk == 0, kk == 15, [yT.tr, ow.tr], [psD.tr])
            yield
            gt = gts[1 if s < 2 else 0]
            tt(P, "dve", bt2.t[:], psD.t[:], gt.t[:], ALU.mult, [psD.tr, gt.tr], [bt2.tr])
            yield
            tt(P, "pool", x_.t[:], x_.t[:], bt2.t[:], ALU.add, [x_.tr, bt2.tr], [x_.tr])
            yield
            dma(P, "sp", H1[s], x_.t[:], [x_.tr], [t_H1[s]])

        itp = 0
        for s in range(NS + 2):
            gens = []
            if s < NS:
                gens.append(main_gen(s))
            if s >= 2:
                gens.append(post_gen(s - 2, itp))
                itp += 1
            zipper(gens)
        P.end_phase()
    if stop_after <= 3:
        return finish(k, nc, P)

    cqT = P.buf(G, "cqT", [128, 4, NQ], BF16)
    ckvT = P.buf(G, "ckvT", [128, 2, NK], BF16)
    krT = P.buf(G, "krT2", [128, NK // 2], BF16)
    t_cq = [Tr() for _ in range(8)]
    t_kv = [Tr() for _ in range(17)]

    with ExitStack() as S:
        wres = P.buf(S, "wres4", [128, 8, 2944], BF16)
        load_w_bf16(wres, 0, w1, 0, 2944)
        mods = mod_tiles(S, 1, (False, False))
        qag = load_bc(S, "qag", qa_g[0:1, :])
        kvag = load_bc(S, "kvag", kva_g[0:1, :])
        xt = [P.buf(S, "xt4", [128, D], F32) for _ in range(2)]
        junk = P.buf(S, "junk4", [128, D], BF16)
        ssb = P.buf(S, "ssb4", [128, 8], F32)
        tmpb = P.buf(S, "tmpb4", [128, 8], F32)
        zb = P.buf(S, "zb4", [128, D], BF16)
        zT = [P.buf(S, "zT4", [128, 8, 128], BF16) for _ in range(2)]
        kvb = P.buf(S, "kvb", [128, 256], BF16)
        rc = [P.buf(S, "rc", [128, 64], F32) for _ in range(2)]
        rs = [P.buf(S, "rs", [128, 64], F32) for _ in range(2)]
        krf = P.buf(S, "krf", [128, 64], F32)
        krf2 = P.buf(S, "krf2", [128, 64], F32)
        krb = P.buf(S, "krb", [128, 64], BF16)
        cqb = P.buf(S, "cqb", [128, 512], BF16)
        sgf = P.buf(S, "sgf", [128, 2 * D], F32)
        sgT = [P.buf(S, "sgT", [128, 16, 128], F32) for _ in range(1)]
        psK = P.pbuf(S, "ps4K", [128, 512])
        psQ = P.pbuf(S, "ps4Q", [128, 512])
        psG = [P.pbuf(S, "ps4G", [128, D]) for _ in range(2)]
        psT = P.pbuf(S, "ps4T", [128, D], BF16)
        psF = P.pbuf(S, "ps4F", [128, 512])
        for s in range(NS):
            x_ = xt[s % 2]
            z_ = zT[s % 2]
            own = 2 <= s < 2 + NOWN
            r = 1 if s < 2 else 0
            dma(P, "sp", x_.t[:], H1[s], [t_H1[s]], [x_.tr])
            dma(P, "act", rc[s % 2].t[:], ropeK_cos[s], [], [rc[s % 2].tr])
            dma(P, "act", rs[s % 2].t[:], ropeK_sin[s], [], [rs[s % 2].tr])
            make_z(x_, mods[r][0], mods[r][1], junk, ssb, tmpb, zb)
            transpose_to(psT, zb, D, z_, "act")
            for kk in range(8):
                mm(P, psK.t[:, 0:384], z_.t[:, kk, :], wres.t[:, kk, 512:896], kk == 0, kk == 7,
                   [z_.tr, wres.tr], [psK.tr])
            act(P, junk.t[:, 0:256], psK.t[:, 0:256], AF.Square, [psK.tr], [junk.tr, ssb.tr], accum_out=ssb.t[:, 0:1])
            rstd_from_ss(P, ssb, tmpb, 256)
            stt(P, "dve", kvb.t[:], psK.t[:, 0:256], ssb.t[:, 0:1], kvag.t[:], ALU.mult, ALU.mult,
                [psK.tr, ssb.tr, kvag.tr], [kvb.tr])
            tt(P, "dve", krf.t[:], psK.t[:, 256:320], rc[s % 2].t[:], ALU.mult, [psK.tr, rc[s % 2].tr], [krf.tr])
            tt(P, "dve", krf2.t[:], psK.t[:, 320:384], rs[s % 2].t[:], ALU.mult, [psK.tr, rs[s % 2].tr], [krf2.tr])
            tt(P, "pool", krb.t[:], krf.t[:], krf2.t[:], ALU.add, [krf.tr, krf2.tr], [krb.tr])
            kc_tr = t_kv[s // 4]
            for j in range(2):
                tp(P, psT.t[:, j * 128:(j + 1) * 128], kvb.t[:, j * 128:(j + 1) * 128], identb.t[:],
                   [kvb.tr, identb.tr], [psT.tr])
            hp = s % 2
            tp(P, psT.t[hp * 64:(hp + 1) * 64, 256:384], krb.t[:], identb.t[:], [krb.tr, identb.tr], [psT.tr])
            for j in range(2):
                cp(P, "act", ckvT.t[:, j, s * 128:(s + 1) * 128], psT.t[:, j * 128:(j + 1) * 128], [psT.tr], [kc_tr])
            cp(P, "dve", krT.t[hp * 64:(hp + 1) * 64, (s // 2) * 128:(s // 2 + 1) * 128],
               psT.t[hp * 64:(hp + 1) * 64, 256:384], [psT.tr], [kc_tr])
            if own:
                q0 = (s - 2) * 128
                for kk in range(8):
                    mm(P, psQ.t[:], z_.t[:, kk, :], wres.t[:, kk, 0:512], kk == 0, kk == 7, [z_.tr, wres.tr], [psQ.tr])
                act(P, junk.t[:, 0:512], psQ.t[:], AF.Square, [psQ.tr], [junk.tr, ssb.tr], accum_out=ssb.t[:, 1:2])
                ts(P, "dve", tmpb.t[:, 1:2], ssb.t[:, 1:2], 1.0 / 512, EPS, ALU.mult, ALU.add, [ssb.tr], [tmpb.tr])
                act(P, tmpb.t[:, 1:2], tmpb.t[:, 1:2], AF.Sqrt, [tmpb.tr], [tmpb.tr])
                P.op("dve", lambda e: e.reciprocal(out=ssb.t[:, 1:2], in_=tmpb.t[:, 1:2]), [tmpb.tr], [ssb.tr])
                stt(P, "dve", cqb.t[:], psQ.t[:], ssb.t[:, 1:2], qag.t[:], ALU.mult, ALU.mult,
                    [psQ.tr, ssb.tr, qag.tr], [cqb.tr])
                for j in range(4):
                    tp(P, psT.t[:, 512 + j * 128:512 + (j + 1) * 128], cqb.t[:, j * 128:(j + 1) * 128], identb.t[:],
                       [cqb.tr, identb.tr], [psT.tr])
                for j in range(4):
                    cp(P, "act" if j % 2 == 0 else "dve", cqT.t[:, j, q0:q0 + 128],
                       psT.t[:, 512 + j * 128:512 + (j + 1) * 128], [psT.tr], [t_cq[(s - 2) // 4]])
                for half in range(2):
                    pg = psG[half]
                    for n in range(2):
                        c0 = 896 + half * D + n * 512
                        for kk in range(8):
                            mm(P, pg.t[:, n * 512:(n + 1) * 512], z_.t[:, kk, :], wres.t[:, kk, c0:c0 + 512],
                               kk == 0, kk == 7, [z_.tr, wres.tr], [pg.tr])
                    act(P, sgf.t[:, half * D:(half + 1) * D], pg.t[:], AF.Silu, [pg.tr], [sgf.tr])
                st_ = sgT[0]
                for hq in range(4):
                    for j in range(4):
                        h = hq * 4 + j
                        tp(P, psF.t[:, j * 128:(j + 1) * 128], sgf.t[:, h * 128:(h + 1) * 128], identf.t[:],
                           [sgf.tr, identf.tr], [psF.tr])
                    cp(P, "dve" if hq % 2 == 0 else "act", st_.t[:, hq * 4:(hq + 1) * 4, :].rearrange("p h t -> p (h t)"),
                       psF.t[:], [psF.tr], [st_.tr])
                dma(P, "sp", SGT[:, :, q0:q0 + 128].rearrange("h v q -> v h q"), st_.t[:], [st_.tr], [t_SGT])
        P.end_phase()
    if stop_after <= 4:
        return finish(k, nc, P)

    with ExitStack() as S:
        qw = [P.buf(S, "qw", [128, 4, 256], BF16) for _ in range(2)]
        kw = [P.buf(S, "kw", [128, 2, 256], BF16) for _ in range(2)]
        KnT = P.buf(S, "KnT", [128, NK], BF16)
        Vh = P.buf(S, "Vh", [128, NS, 128], BF16)
        QnT = P.buf(S, "QnT", [128, NQ], BF16)
        QrT = P.buf(S, "QrT2", [128, NQ], BF16)
        sgt = P.buf(S, "sgt", [128, NQ], F32)
        rq = [[P.buf(S, "rq", [128, 512], F32) for _ in range(2)] for _ in range(2)]
        q1 = P.buf(S, "q1", [128, 512], F32)
        q2 = P.buf(S, "q2", [128, 512], F32)
        PT = [P.buf(S, "PT", [128, 1024], BF16) for _ in range(3)]
        laccA = P.buf(S, "laccA", [128, 1024], F32)
        laccB = P.buf(S, "laccB", [128, 1024], F32)
        onesf = P.buf(S, "onesf", [128, 128], F32)
        rl = P.buf(S, "rl", [128, 512], F32)
        yo = [P.buf(S, "yo", [128, 512], BF16) for _ in range(2)]
        yf = P.buf(S, "yf", [128, 512], F32)
        P.op("pool", lambda e: e.memset(onesf.t[:], 1.0), [], [onesf.tr])
        psS = [P.pbuf(S, "ps5S", [128, 1024]) for _ in range(3)]
        psO = P.pbuf(S, "ps5O", [128, 512])
        psL = P.pbuf(S, "ps5L", [128, 512])

        class _PX:
            def __init__(self, b, c0):
                self.tr = b.tr
                self.t = b.t[:, c0:c0 + 512]
        psX = [_PX(psS[i], j * 512) for j in range(2) for i in range(3)]
        t_kn = [Tr() for _ in range(17)]
        t_v = [Tr() for _ in range(17)]
        t_qn = [Tr() for _ in range(8)]
        ix = 0
        ip = 0
        for h in range(HEADS):
            qw_, kw_ = qw[h % 2], kw[h % 2]
            for kk in range(4):
                dma(P, "pool", qw_.t[:, kk, :], qbw[kk * 128:(kk + 1) * 128, h * 256:(h + 1) * 256], [], [qw_.tr])
            for kk in range(2):
                dma(P, "pool", kw_.t[:, kk, :], kvbw[kk * 128:(kk + 1) * 128, h * 256:(h + 1) * 256], [], [kw_.tr])
            dma(P, "sp", sgt.t[:], SGT[h], [t_SGT], [sgt.tr])
            for kc in range(17):
                c0 = kc * 512
                cn = min(512, NK - c0)
                px = psX[ix % 6]
                ix += 1
                for kk in range(2):
                    mm(P, px.t[:, 0:cn], kw_.t[:, kk, 0:128], ckvT.t[:, kk, c0:c0 + cn], kk == 0, kk == 1,
                       [kw_.tr, t_kv[kc]], [px.tr])
                cp(P, "dve" if kc % 2 == 0 else "pool" if False else "dve", KnT.t[:, c0:c0 + cn], px.t[:, 0:cn],
                   [px.tr], [t_kn[kc]])
                px = psX[ix % 6]
                ix += 1
                nt = cn // 128
                for j in range(nt):
                    s = kc * 4 + j
                    for kk in range(2):
                        mm(P, px.t[:, j * 128:(j + 1) * 128], ckvT.t[:, kk, s * 128:(s + 1) * 128], kw_.t[:, kk, 128:256],
                           kk == 0, kk == 1, [kw_.tr, t_kv[kc]], [px.tr])
                cp(P, "act", Vh.t[:, kc * 4:kc * 4 + nt, :].rearrange("p s v -> p (s v)"), px.t[:, 0:cn],
                   [px.tr], [t_v[kc]])
            for qc in range(8):
                c0 = qc * 512
                px = psX[ix % 6]
                ix += 1
                for kk in range(4):
                    mm(P, px.t[:], qw_.t[:, kk, 0:128], cqT.t[:, kk, c0:c0 + 512], kk == 0, kk == 3,
                       [qw_.tr, t_cq[qc]], [px.tr])
                cp(P, "dve", QnT.t[:, c0:c0 + 512], px.t[:], [px.tr], [t_qn[qc]])
                rq_ = rq[qc % 2]
                for hp in range(2):
                    dma(P, "sp", rq_[0].t[hp * 64:(hp + 1) * 64, :], ropeQ_cosT[:, c0:c0 + 512], [], [rq_[0].tr])
                    dma(P, "sp", rq_[1].t[hp * 64:(hp + 1) * 64, :], ropeQ_sinT[:, c0:c0 + 512], [], [rq_[1].tr])
                px = psX[ix % 6]
                ix += 1
                for hp in range(2):
                    for kk in range(4):
                        mm(P, px.t[hp * 64:(hp + 1) * 64, :], qw_.t[:, kk, 128:192], cqT.t[:, kk, c0:c0 + 512], kk == 0, kk == 3,
                           [qw_.tr, t_cq[qc]], [px.tr])
                tt(P, "dve", q1.t[:], px.t[:], rq_[0].t[:], ALU.mult, [px.tr, rq_[0].tr], [q1.tr])
                px = psX[ix % 6]
                ix += 1
                for hp in range(2):
                    for kk in range(4):
                        mm(P, px.t[hp * 64:(hp + 1) * 64, :], qw_.t[:, kk, 192:256], cqT.t[:, kk, c0:c0 + 512], kk == 0, kk == 3,
                           [qw_.tr, t_cq[qc]], [px.tr])
                tt(P, "dve", q2.t[:], px.t[:], rq_[1].t[:], ALU.mult, [px.tr, rq_[1].tr], [q2.tr])
                tt(P, "pool", QrT.t[:, c0:c0 + 512], q1.t[:], q2.t[:], ALU.add, [q1.tr, q2.tr], [t_qn[qc]])
            for qc in range(8):
                c0 = qc * 512
                NP = NS // 2
                for p in range(NP + 2):
                    if p < NP:
                        ps = psS[p % 3]
                        pt = PT[p % 3]
                        for j in range(2):
                            s = 2 * p + j
                            kc = s // 4
                            mm(P, ps.t[:, j * 512:(j + 1) * 512], KnT.t[:, s * 128:(s + 1) * 128], QnT.t[:, c0:c0 + 512],
                               True, False, [t_kn[kc], t_qn[qc]], [ps.tr])
                        for j in range(2):
                            s = 2 * p + j
                            kc = s // 4
                            mm(P, ps.t[:, j * 512:(j + 1) * 512], krT.t[j * 64:(j + 1) * 64, p * 128:(p + 1) * 128],
                               QrT.t[j * 64:(j + 1) * 64, c0:c0 + 512], False, True, [t_kv[kc], t_qn[qc]], [ps.tr])
                        act(P, pt.t[:], ps.t[:], AF.Exp, [ps.tr], [pt.tr], scale=MLA_SCALE)
                    if p >= 2:
                        q = p - 2
                        pt0 = PT[q % 3]
                        for j in range(2):
                            s0 = 2 * q + j
                            mm(P, psO.t[:], Vh.t[:, s0, :], pt0.t[:, j * 512:(j + 1) * 512], s0 == 0, s0 == NS - 1,
                               [t_v[s0 // 4], pt0.tr], [psO.tr])
                        la, eng = laccA, "dve"
                        if q == 0:
                            cp(P, eng, la.t[:], pt0.t[:], [pt0.tr], [la.tr])
                        else:
                            tt(P, eng, la.t[:], la.t[:], pt0.t[:], ALU.add, [la.tr, pt0.tr], [la.tr])
                for j, (la, c1) in enumerate(((laccA, 0), (laccA, 512))):
                    mm(P, psL.t[:], onesf.t[:], la.t[:, c1:c1 + 512], j == 0, j == 1, [onesf.tr, la.tr], [psL.tr])
                P.op("dve", lambda e: e.reciprocal(out=rl.t[:], in_=psL.t[:]), [psL.tr], [rl.tr])
                tt(P, "dve", yf.t[:], psO.t[:], rl.t[:], ALU.mult, [psO.tr, rl.tr], [yf.tr])
                yo_ = yo[qc % 2]
                tt(P, "pool", yo_.t[:], yf.t[:], sgt.t[:, c0:c0 + 512], ALU.mult, [yf.tr, sgt.tr], [yo_.tr])
                dma(P, "sp", YT[h, :, c0:c0 + 512], yo_.t[:], [yo_.tr], [t_YT[h]])
        P.end_phase()
    if stop_after <= 5:
        return finish(k, nc, P)

    with ExitStack() as S:
        ow = P.buf(S, "ow1", [128, 16, D], BF16)
        load_w_bf16(ow, 0, out_w[1], 0, D, nk=16)
        gt1 = load_bc(S, "gt1", MS[1, 0:1, 2 * D:3 * D])
        fgb = load_bc(S, "fgb", final_g[0:1, :])
        yT = [P.buf(S, "yT6", [128, 16, 128], BF16) for _ in range(2)]
        xt = [P.buf(S, "xt6", [128, D], F32) for _ in range(2)]
        bt = P.buf(S, "bt6", [128, D], F32)
        junk = P.buf(S, "junk6", [128, D], BF16)
        ssb = P.buf(S, "ssb6", [128, 8], F32)
        tmpb = P.buf(S, "tmpb6", [128, 8], F32)
        psB = [P.pbuf(S, "ps6B", [128, D]) for _ in range(2)]
        for i in range(NOWN):
            s = i + 2
            q0 = i * 128
            y_ = yT[i % 2]
            x_ = xt[i % 2]
            pb = psB[i % 2]
            dma(P, "sp", y_.t[:], YT[:, :, q0:q0 + 128].rearrange("h v q -> v h q"), t_YT, [y_.tr])
            dma(P, "act", x_.t[:], H1[s], [t_H1[s]], [x_.tr])
            for n in range(2):
                for kk in range(16):
                    mm(P, pb.t[:, n * 512:(n + 1) * 512], y_.t[:, kk, :], ow.t[:, kk, n * 512:(n + 1) * 512],
                       kk == 0, kk == 15, [y_.tr, ow.tr], [pb.tr])
            tt(P, "dve", bt.t[:], pb.t[:], gt1.t[:], ALU.mult, [pb.tr, gt1.tr], [bt.tr])
            tt(P, "pool", x_.t[:], x_.t[:], bt.t[:], ALU.add, [x_.tr, bt.tr], [x_.tr])
            act(P, junk.t[:], x_.t[:], AF.Square, [x_.tr], [junk.tr, ssb.tr], accum_out=ssb.t[:, 0:1])
            rstd_from_ss(P, ssb, tmpb, D)
            stt(P, "dve", x_.t[:], x_.t[:], ssb.t[:, 0:1], fgb.t[:], ALU.mult, ALU.mult, [x_.tr, ssb.tr, fgb.tr], [x_.tr])
            dma(P, "sp", out_d[i], x_.t[:], [x_.tr], [])
        P.end_phase()
    return finish(k, nc, P)


def finish(k, nc, P):
    if any(len(v) for v in P.ops.values()):
        P.end_phase()
    P.close()
    return nc


def _consts(mir):
    ident = np.eye(128, dtype=np.float32)
    s = np.arange(128)[:, None]
    t = np.arange(128)[None, :]
    same = (s // 64) == (t // 64)
    tb_up = (same & (s <= t)).astype(np.float32)
    tb_dn = (same & (s >= t)).astype(np.float32)
    mid = (t // 64) * 64 + 32
    td_up = tb_up - tb_up[:, mid[0]]
    td_dn = tb_dn - tb_dn[:, mid[0]]
    tr_up = (same & (s > t)).astype(np.float32)
    tr_dn = (same & (s < t)).astype(np.float32)
    tris = np.stack([tb_up, td_up, tr_up, tb_dn, td_dn, tr_dn]).astype(np.float32)
    ci = np.zeros((128, 2), np.float32)
    ci[:64, 0] = 1
    ci[64:, 1] = 1
    p = np.arange(128)[:, None] % 64
    j = np.arange(64)[None, :]
    masks = np.stack([(p <= j), (p >= j)]).astype(np.float32)
    L = 384
    bands = np.zeros((20, 128, 128), np.float32)
    for g, w in enumerate((2, 4, 8, 16)):
        Bm = np.zeros((L, L), np.float64)
        for jl in range(L):
            to = (L - 1 - jl) if mir else jl
            lo = max(0, to - w // 2)
            hi = min(L, to + w // 2)
            cnt = hi - lo
            for so in range(lo, hi):
                sl = (L - 1 - so) if mir else so
                Bm[sl, jl] += 1.0 / cnt
            Bm[jl, jl] -= 1.0
        bands[g * 5 + 0] = Bm[0:128, 0:128]
        bands[g * 5 + 1] = Bm[128:256, 128:256]
        bands[g * 5 + 2] = Bm[256:384, 256:384]
        bands[g * 5 + 3] = Bm[0:128, 128:256]
        bands[g * 5 + 4] = Bm[256:384, 128:256]
    return ident, tris, ci, masks, bands


def _rope_tables(mir):
    inv = (10000.0 ** (-2.0 * np.arange(16, dtype=np.float32) / 32.0)).astype(np.float32)
    jl = np.arange(T)
    to = (T - 1 - jl) if mir else jl
    pr = (to // 64).astype(np.float32)
    pc = (to % 64).astype(np.float32)
    ang_r = pr[:, None] * inv[None, :]
    ang_c = pc[:, None] * inv[None, :]
    cr, sr = np.cos(ang_r).astype(np.float32), np.sin(ang_r).astype(np.float32)
    cc, sc = np.cos(ang_c).astype(np.float32), np.sin(ang_c).astype(np.float32)
    cos = np.concatenate([cr, cr, cc, cc], axis=1)
    sin = np.concatenate([-sr, sr, -sc, sc], axis=1)
    cosK = np.concatenate([np.ones((TC, 64), np.float32), cos], axis=0).reshape(NS, 128, 64)
    sinK = np.concatenate([np.zeros((TC, 64), np.float32), sin], axis=0).reshape(NS, 128, 64)
    cosQ = np.ascontiguousarray(cos[:NQ].T)
    sinQ = np.ascontiguousarray(sin[:NQ].T)
    return cosK, sinK, cosQ, sinQ


_PERM64 = np.concatenate([np.arange(16, 32), np.arange(0, 16), np.arange(48, 64), np.arange(32, 48)])

_NC_CACHE = {}


def _prep_inputs(x, c, ctx, c_ctx, ada_w, ada_b, norm_g, out_w, ev_in_w, hg_lb, hg_norm_g, pool_w,
                 pool_scale, od_in_w, qa_norm_g, qb_w, kva_norm_g, kvb_w, final_norm_g):
    f = lambda a: np.ascontiguousarray(np.asarray(a, dtype=np.float32))
    x, c, ctx, c_ctx = f(x), f(c), f(ctx), f(c_ctx)
    ev = f(ev_in_w)[0]
    od = f(od_in_w)[0]
    qb = f(qb_w)[0].reshape(512, HEADS, 192)
    qb_ext = np.concatenate([qb, qb[:, :, 128:][:, :, _PERM64]], axis=2).reshape(512, HEADS * 256)
    kr = od[:, 768:832]
    w1 = np.concatenate([od[:, 0:768], kr, kr[:, _PERM64], od[:, 832:]], axis=1)
    lb = f(hg_lb)
    shared = dict(ada_w=f(ada_w), ada_b=f(ada_b), norm_g=f(norm_g), final_g=f(final_norm_g)[None, :],
                  hgn=f(hg_norm_g), pool_w=f(pool_w)[0], pool_scale=f(pool_scale), out_w=f(out_w),
                  w1=np.ascontiguousarray(w1), qa_g=f(qa_norm_g), kva_g=f(kva_norm_g),
                  qbw=np.ascontiguousarray(qb_ext), kvbw=f(kvb_w)[0])
    per_mir = {}
    for mir in (False, True):
        ident, tris, ci, masks, bands = _consts(mir)
        cosK, sinK, cosQ, sinQ = _rope_tables(mir)
        q_, ff, fb, i_, ga, u_, gb = [ev[:, j * D:(j + 1) * D] for j in range(7)]
        w0 = np.concatenate([q_, fb, ff, i_, ga, u_, gb] if mir else [q_, ff, fb, i_, ga, u_, gb], axis=1)
        lbr = lb[::-1] if mir else lb
        per_mir[mir] = dict(ident=ident, tris=tris, ci=ci, masks=masks, bands=bands, ropeK_cos=cosK, ropeK_sin=sinK,
                            ropeQ_cosT=cosQ, ropeQ_sinT=sinQ, w0=np.ascontiguousarray(w0),
                            lbraw=np.ascontiguousarray(lbr))
    in_maps = []
    for core in range(8):
        b, half = core // 2, core % 2
        mir = half == 1
        xl = x[b][::-1] if mir else x[b]
        cl = ctx[b][::-1] if mir else ctx[b]
        xs = np.concatenate([cl, xl], axis=0).reshape(NS, 128, D)
        cvec = np.stack([c[b], c_ctx], axis=0)
        cvT = np.ascontiguousarray(cvec.reshape(2, 8, 128).transpose(2, 1, 0))
        m = dict(xs=np.ascontiguousarray(xs), cvT=cvT)
        m.update(shared)
        m.update(per_mir[mir])
        in_maps.append(m)
    return in_maps


def kernel(x, c, ctx, c_ctx, ada_w, ada_b, norm_g, out_w, ev_in_w, hg_lb, hg_norm_g, pool_w,
           pool_scale, od_in_w, qa_norm_g, qb_w, kva_norm_g, kvb_w, final_norm_g):
    in_maps = _prep_inputs(x, c, ctx, c_ctx, ada_w, ada_b, norm_g, out_w, ev_in_w, hg_lb, hg_norm_g, pool_w,
                           pool_scale, od_in_w, qa_norm_g, qb_w, kva_norm_g, kvb_w, final_norm_g)
    if "nc" not in _NC_CACHE:
        _NC_CACHE["nc"] = build_program()
    nc = _NC_CACHE["nc"]
    res = run_bass_kernel_spmd(nc, in_maps, core_ids=list(range(8)))
    out = np.zeros((4, T, D), np.float32)
    for core in range(8):
        b, half = core // 2, core % 2
        o = np.asarray(res.results[core]["out"]).reshape(NQ, D)
        if half == 1:
            out[b, NQ:] = o[::-1]
        else:
            out[b, :NQ] = o
    return out
```

```python
import numpy as np
from contextlib import ExitStack
import concourse.bass as bass
import concourse.mybir as mybir
from concourse.bass_utils import run_bass_kernel_spmd

F32 = mybir.dt.float32
BF16 = mybir.dt.bfloat16
AF = mybir.ActivationFunctionType
ALU = mybir.AluOpType
AX = mybir.AxisListType

D = 1024
T = 8192
TC = 256
NS = 66
NOWN = 32
NQ = 4096
NK = 8448
HEADS = 16
EPS = 1e-6
MLA_SCALE = 192.0 ** -0.5

ENGS = ("pe", "act", "dve", "pool", "sp")
NDMA = 12


class Tr:
    __slots__ = ("w", "r")

    def __init__(self):
        self.w = None
        self.r = {}


class Buf:
    __slots__ = ("t", "tr")

    def __init__(self, t):
        self.t = t
        self.tr = Tr()


class Prog:
    def __init__(self, nc):
        self.nc = nc
        self.es = ExitStack()
        self.ops = {e: [] for e in ENGS}
        self.cnt = {e: 0 for e in ENGS}
        self.waited = {e: {} for e in ENGS}
        self.sems = {}
        for e in ("pe", "act", "dve", "pool"):
            self.sems["c_" + e] = self.es.enter_context(nc.semaphore("c_" + e))
        self.dma_cnt = {}
        self.dma_rr = {}
        for q in ("sp", "act", "pool"):
            self.dma_rr[q] = 0
            for j in range(NDMA):
                k = "d_%s_%d" % (q, j)
                self.sems[k] = self.es.enter_context(nc.semaphore(k))
                self.dma_cnt[k] = 0
        self.final_events = {}
        self.uid = 0

    def buf(self, stack, name, shape, dtype):
        self.uid += 1
        return Buf(stack.enter_context(self.nc.sbuf_tensor("%s_%d" % (name, self.uid), list(shape), dtype)))

    def pbuf(self, stack, name, shape, dtype=F32):
        self.uid += 1
        return Buf(stack.enter_context(self.nc.psum_tensor("%s_%d" % (name, self.uid), list(shape), dtype)))

    def _deps(self, eng, reads, writes, pe_like, own=None):
        need = {}
        if own is None:
            own = "c_" + eng
        for t in reads:
            if t.w is not None:
                if t.w[0] == own and pe_like:
                    continue
                if need.get(t.w[0], 0) < t.w[1]:
                    need[t.w[0]] = t.w[1]
        for t in writes:
            if t.w is not None and t.w[0] != own:
                if need.get(t.w[0], 0) < t.w[1]:
                    need[t.w[0]] = t.w[1]
            for k, v in t.r.items():
                if k != own and need.get(k, 0) < v:
                    need[k] = v
        out = []
        wd = self.waited[eng]
        for k, v in need.items():
            if wd.get(k, 0) >= v:
                continue
            wd[k] = v
            out.append((k, v))
        return out

    def _commit(self, ev, reads, writes):
        k, v = ev
        for t in reads:
            if t.r.get(k, 0) < v:
                t.r[k] = v
        for t in writes:
            t.w = ev
            t.r = {}

    def op(self, eng, fn, reads=(), writes=()):
        waits = self._deps(eng, reads, writes, eng == "pe")
        self.cnt[eng] += 1
        ev = ("c_" + eng, self.cnt[eng])
        self.ops[eng].append((waits, fn, ("c_" + eng, 1)))
        self._commit(ev, reads, writes)

    def dma(self, q, fn, reads=(), writes=()):
        j = self.dma_rr[q]
        self.dma_rr[q] = (j + 1) % NDMA
        k = "d_%s_%d" % (q, j)
        waits = self._deps(q, reads, writes, False, own="__none__")
        prev = self.dma_cnt[k]
        if prev > 0 and self.waited[q].get(k, 0) < prev:
            self.waited[q][k] = prev
            waits.append((k, prev))
        self.dma_cnt[k] = prev + 16
        ev = (k, prev + 16)
        self.ops[q].append((waits, fn, (k, 16)))
        self._commit(ev, reads, writes)
        self.final_events[k] = prev + 16

    def barrier(self):
        allev = []
        for e in ("pe", "act", "dve", "pool"):
            if self.cnt[e] > 0:
                allev.append(("c_" + e, self.cnt[e]))
        for k, v in self.final_events.items():
            allev.append((k, v))
        for e in ENGS:
            w = []
            for k, v in allev:
                if k == "c_" + e:
                    continue
                if self.waited[e].get(k, 0) < v:
                    self.waited[e][k] = v
                    w.append((k, v))
            if w:
                self.ops[e].append((w, None, None))

    def emit_block(self):
        nc = self.nc
        sems = self.sems
        ops = self.ops
        self.ops = {e: [] for e in ENGS}
        with nc.Block() as block:
            def run(engname):
                def body(e):
                    for waits, fn, inc in ops[engname]:
                        for k, v in waits:
                            e.wait_ge(sems[k], v)
                        if fn is not None:
                            fn(e).then_inc(sems[inc[0]], inc[1])
                return body
            block.tensor(run("pe"))
            block.scalar(run("act"))
            block.vector(run("dve"))
            block.gpsimd(run("pool"))
            block.sync(run("sp"))

    def end_phase(self):
        self.barrier()
        self.emit_block()

    def close(self):
        self.es.close()


class K:
    pass


def mm(P, out, lhsT, rhs, start, stop, R, W):
    P.op("pe", lambda e: e.matmul(out, lhsT=lhsT, rhs=rhs, start=start, stop=stop), R, W)


def tp(P, out, in_, ident, R, W):
    P.op("pe", lambda e: e.transpose(out, in_, ident), R, W)


def act(P, out, in_, func, R, W, **kw):
    P.op("act", lambda e: e.activation(out=out, in_=in_, func=func, **kw), R, W)


def tt(P, eng, out, in0, in1, op, R, W):
    P.op(eng, lambda e: e.tensor_tensor(out=out, in0=in0, in1=in1, op=op), R, W)


def ts(P, eng, out, in0, s1, s2, op0, op1, R, W):
    if s2 is None:
        P.op(eng, lambda e: e.tensor_scalar(out=out, in0=in0, scalar1=s1, scalar2=None, op0=op0), R, W)
    else:
        P.op(eng, lambda e: e.tensor_scalar(out=out, in0=in0, scalar1=s1, scalar2=s2, op0=op0, op1=op1), R, W)


def stt(P, eng, out, in0, scalar, in1, op0, op1, R, W):
    P.op(eng, lambda e: e.scalar_tensor_tensor(out=out, in0=in0, scalar=scalar, in1=in1, op0=op0, op1=op1), R, W)


def cp(P, eng, out, in_, R, W):
    if eng == "act":
        P.op("act", lambda e: e.copy(out=out, in_=in_), R, W)
    else:
        P.op(eng, lambda e: e.tensor_copy(out=out, in_=in_), R, W)


def dma(P, q, out, in_, R, W):
    P.dma(q, lambda e: e.dma_start(out=out, in_=in_), R, W)


def rstd_from_ss(P, ss, tmp, n, c0=0, c1=1):
    ts(P, "dve", tmp.t[:, c0:c1], ss.t[:, c0:c1], 1.0 / n, EPS, ALU.mult, ALU.add, [ss.tr], [tmp.tr])
    act(P, tmp.t[:, c0:c1], tmp.t[:, c0:c1], AF.Sqrt, [tmp.tr], [tmp.tr])
    P.op("dve", lambda e: e.reciprocal(out=ss.t[:, c0:c1], in_=tmp.t[:, c0:c1]), [tmp.tr], [ss.tr])


def build_program(debug=False, stop_after=99, lim=None):
    nc = bass.Bass("TRN2", target_bir_lowering=False)
    P = Prog(nc)
    k = K()
    k.nc, k.P = nc, P

    def din(name, shape, dt=F32):
        return nc.dram_tensor(name, list(shape), dt, kind="ExternalInput").ap()

    dbg = set(debug) if debug else set()

    def dscr(name, shape, dt):
        return nc.dram_tensor(name, list(shape), dt, kind=("ExternalOutput" if name in dbg else "Internal")).ap()

    xs = din("xs", [NS, 128, D])
    cvT = din("cvT", [128, 8, 2])
    ada_w = din("ada_w", [2, D, 3 * D])
    ada_b = din("ada_b", [2, 3 * D])
    norm_g = din("norm_g", [2, D])
    final_g = din("final_g", [1, D])
    w0 = din("w0", [D, 7 * D])
    lbraw = din("lbraw", [2, 3, D])
    hgn = din("hgn", [1, D])
    pool_w = din("pool_w", [4, 256, 256])
    pool_scale = din("pool_scale", [1, D])
    out_w = din("out_w", [2, 2 * D, D])
    w1 = din("w1", [D, 2944])
    qa_g = din("qa_g", [1, 512])
    kva_g = din("kva_g", [1, 256])
    qbw = din("qbw", [512, HEADS * 256])
    kvbw = din("kvbw", [256, HEADS * 256])
    ropeK_cos = din("ropeK_cos", [NS, 128, 64])
    ropeK_sin = din("ropeK_sin", [NS, 128, 64])
    ropeQ_cosT = din("ropeQ_cosT", [64, NQ])
    ropeQ_sinT = din("ropeQ_sinT", [64, NQ])
    ident_d = din("ident", [128, 128])
    tris_d = din("tris", [6, 128, 128])
    ci_d = din("ci", [128, 2])
    masks_d = din("masks", [2, 128, 64])
    bands_d = din("bands", [20, 128, 128])
    out_d = nc.dram_tensor("out", [NOWN, 128, D], F32, kind="ExternalOutput").ap()

    MS = dscr("MS", [2, 2, 3 * D], F32)
    ZT = dscr("ZT", [NS, 128, 8, 128], BF16)
    ODN = dscr("ODN", [NS, 128, D], F32)
    OUP = dscr("OUP", [NS, 128, D], F32)
    H1 = dscr("H1", [NS, 128, D], F32)
    SGT = dscr("SGT", [HEADS, 128, NQ], F32)
    YT = dscr("YT", [HEADS, 128, NQ], BF16)
    t_MS, t_SGT = Tr(), Tr()
    t_ZT = [Tr() for _ in range(NS)]
    t_ODN = [Tr() for _ in range(NS)]
    t_OUP = [Tr() for _ in range(NS)]
    t_H1 = [Tr() for _ in range(NS)]
    t_YT = [Tr() for _ in range(HEADS)]

    G = P.es
    identb = P.buf(G, "identb", [128, 128], BF16)
    identf = P.buf(G, "identf", [128, 128], F32)
    tri = P.buf(G, "tri", [128, 6, 128], F32)
    cib = P.buf(G, "cib", [128, 2], F32)
    maskb = P.buf(G, "maskb", [128, 2, 64], F32)
    dma(P, "pool", identb.t[:], ident_d, [], [identb.tr])
    dma(P, "sp", identf.t[:], ident_d, [], [identf.tr])
    dma(P, "sp", tri.t[:], tris_d.rearrange("m s t -> s m t"), [], [tri.tr])
    dma(P, "sp", cib.t[:], ci_d, [], [cib.tr])
    dma(P, "sp", maskb.t[:], masks_d.rearrange("m s t -> s m t"), [], [maskb.tr])

    with ExitStack() as S:
        cv = P.buf(S, "cv", [128, 8, 2], F32)
        sv = P.buf(S, "sv", [128, 8, 2], F32)
        aw = [P.buf(S, "aw%d" % i, [128, 8, 512], F32) for i in range(2)]
        ab = [P.buf(S, "ab%d" % i, [2, 512], F32) for i in range(2)]
        mrow = [P.buf(S, "mrow%d" % i, [2, 512], F32) for i in range(2)]
        pm = [P.pbuf(S, "pm%d" % i, [128, 512]) for i in range(2)]
        dma(P, "sp", cv.t[:], cvT, [], [cv.tr])
        act(P, sv.t[:], cv.t[:], AF.Silu, [cv.tr], [sv.tr])
        it = 0
        for l in range(2):
            for n in range(6):
                a_, b_, m_, p_ = aw[it % 2], ab[it % 2], mrow[it % 2], pm[it % 2]
                it += 1
                for kk in range(8):
                    dma(P, "sp" if kk % 2 == 0 else "act", a_.t[:, kk, :],
                        ada_w[l, kk * 128:(kk + 1) * 128, n * 512:(n + 1) * 512], [], [a_.tr])
                dma(P, "sp", b_.t[:], ada_b[l:l + 1, n * 512:(n + 1) * 512].partition_broadcast(2), [], [b_.tr])
                for kk in range(8):
                    mm(P, p_.t[0:2, :], sv.t[:, kk, :], a_.t[:, kk, :], kk == 0, kk == 7, [sv.tr, a_.tr], [p_.tr])
                tt(P, "dve", m_.t[:], p_.t[0:2, :], b_.t[:], ALU.add, [p_.tr, b_.tr], [m_.tr])
                dma(P, "sp", MS[l, :, n * 512:(n + 1) * 512], m_.t[:], [m_.tr], [t_MS])
        P.end_phase()
    if stop_after <= 0:
        return finish(k, nc, P)

    def load_bc(S, name, src_row, q="sp"):
        n = src_row.shape[-1]
        b = P.buf(S, name, [128, n], F32)
        dma(P, q, b.t[:], src_row.partition_broadcast(128), [t_MS], [b.tr])
        return b

    def mod_tiles(S, l, want_gate):
        ng = load_bc(S, "ng", norm_g[l:l + 1, :])
        res = []
        for r in range(2):
            sh = load_bc(S, "sh", MS[l, r:r + 1, 0:D])
            sc = load_bc(S, "sc", MS[l, r:r + 1, D:2 * D])
            stt(P, "dve", sc.t[:], sc.t[:], 1.0, ng.t[:], ALU.add, ALU.mult, [sc.tr, ng.tr], [sc.tr])
            gt = None
            if want_gate[r]:
                gt = load_bc(S, "gt", MS[l, r:r + 1, 2 * D:3 * D])
            res.append((sc, sh, gt))
        return res

    def make_z(xt, Gb, SHb, junk, ssb, tmpb, zb):
        act(P, junk.t[:], xt.t[:], AF.Square, [xt.tr], [junk.tr, ssb.tr], accum_out=ssb.t[:, 0:1])
        rstd_from_ss(P, ssb, tmpb, D)
        stt(P, "dve", xt.t[:], xt.t[:], ssb.t[:, 0:1], Gb.t[:], ALU.mult, ALU.mult, [xt.tr, ssb.tr, Gb.tr], [xt.tr])
        tt(P, "pool", zb.t[:], xt.t[:], SHb.t[:], ALU.add, [xt.tr, SHb.tr], [zb.tr])

    def transpose_to(psT, src, ncols, dstT, evac_eng):
        nb = ncols // 128
        for j in range(nb):
            tp(P, psT.t[:, j * 128:(j + 1) * 128], src.t[:, j * 128:(j + 1) * 128], identb.t[:],
               [src.tr, identb.tr], [psT.tr])
        cp(P, evac_eng, dstT.t[:].rearrange("p k t -> p (k t)"), psT.t[:, 0:ncols], [psT.tr], [dstT.tr])

    def load_w_bf16(dst, dcol, src, scol, ncols, nk=8):
        for kk in range(nk):
            for c0 in range(0, ncols, 1024):
                c1 = min(ncols, c0 + 1024)
                dma(P, "pool", dst.t[:, kk, dcol + c0:dcol + c1],
                    src[kk * 128:(kk + 1) * 128, scol + c0:scol + c1], [], [dst.tr])

    def zipper(gens):
        gens = list(gens)
        while gens:
            nxt = []
            for g in gens:
                try:
                    next(g)
                    nxt.append(g)
                except StopIteration:
                    pass
            gens = nxt

    def hgrn_pass(dn):
        with ExitStack() as S:
            ncols = 3 * D
            wres = P.buf(S, "wres", [128, 8, ncols], BF16)
            load_w_bf16(wres, 0, w0, 0, D)
            load_w_bf16(wres, D, w0, (2 * D if dn else D), D)
            load_w_bf16(wres, 2 * D, w0, 3 * D, D)
            d_idx = 1 if dn else 0
            e3 = P.buf(S, "e3", [128, 3, D], F32)
            omlb = P.buf(S, "omlb", [128, D], F32)
            for j in range(3):
                dma(P, "sp", e3.t[:, j, :], lbraw[d_idx, j:j + 1, :].partition_broadcast(128), [], [e3.tr])
            act(P, e3.t[:], e3.t[:], AF.Exp, [e3.tr], [e3.tr])
            tt(P, "dve", omlb.t[:], e3.t[:, 1, :], e3.t[:, 2, :], ALU.add, [e3.tr], [omlb.tr])
            tt(P, "dve", e3.t[:, 2, :], omlb.t[:], e3.t[:, 0, :], ALU.add, [omlb.tr, e3.tr], [e3.tr])
            P.op("dve", lambda e: e.reciprocal(out=e3.t[:, 1, :], in_=e3.t[:, 2, :]), [e3.tr], [e3.tr])
            tt(P, "dve", omlb.t[:], omlb.t[:], e3.t[:, 1, :], ALU.mult, [omlb.tr, e3.tr], [omlb.tr])
            if dn:
                mods = mod_tiles(S, 0, (False, False))
            xt = [P.buf(S, "xt", [128, D], F32) for _ in range(2)]
            junk = P.buf(S, "junk", [128, D], BF16)
            ssz = P.buf(S, "ssz", [128, 8], F32)
            tmpz = P.buf(S, "tmpz", [128, 8], F32)
            zb = P.buf(S, "zb", [128, D], BF16)
            zT = [P.buf(S, "zT", [128, 8, 128], BF16) for _ in range(2)]
            HW = 512

            class HB:
                pass
            hb = []
            for hf in range(2):
                b = HB()
                b.St = P.buf(S, "St", [128, 4, 128], F32)
                b.Sbf = P.buf(S, "Sbf", [128, 4, 128], BF16)
                P.op("pool", lambda e, b=b: e.memset(b.St.t[:], 0.0), [], [b.St.tr])
                P.op("pool", lambda e, b=b: e.memset(b.Sbf.t[:], 0.0), [], [b.Sbf.tr])
                b.Qf = P.buf(S, "Qf", [128, HW], F32)
                b.S2 = P.buf(S, "S2", [128, HW], F32)
                b.Kf = P.buf(S, "Kf", [128, HW], F32)
                b.Gl = P.buf(S, "Gl", [128, HW], F32)
                b.V = P.buf(S, "V", [128, HW], BF16)
                b.E = [P.buf(S, "E", [128, HW], F32) for _ in range(2)]
                b.Qe = P.buf(S, "Qe", [128, HW], BF16)
                b.Ke = P.buf(S, "Ke", [128, HW], BF16)
                b.Qb = P.buf(S, "Qb", [128, HW], BF16)
                b.Kend = P.buf(S, "Kend", [128, HW], BF16)
                b.QeT = P.buf(S, "QeT", [128, 4, 128], BF16)
                b.KeT = P.buf(S, "KeT", [128, 4, 128], BF16)
                b.QbT = P.buf(S, "QbT", [128, 4, 128], BF16)
                b.Asb = P.buf(S, "Asb", [128, 4, 64], BF16)
                b.eb = P.buf(S, "eb", [128, 4, 2], F32)
                b.osb = [P.buf(S, "osb", [128, HW], F32) for _ in range(2)]
                b.ssb = P.buf(S, "ssb", [128, 4], F32)
                b.tmpb = P.buf(S, "tmpb", [128, 4], F32)
                b.psA = P.pbuf(S, "psA", [128, HW])
                b.psB = P.pbuf(S, "psB", [128, HW])
                b.psC = P.pbuf(S, "psC", [128, HW])
                b.psT = P.pbuf(S, "psT", [128, 2 * HW], BF16)
                hb.append(b)
            tb = 3 if dn else 0
            order = [1, 0] + list(range(NS - 1, 1, -1)) if dn else list(range(NS))
            corder = (1, 0) if dn else (0, 1)
            if lim is not None:
                order = order[:lim]

            def tr4(psT, src, dstT, evac_eng):
                for j in range(4):
                    tp(P, psT.t[:, j * 128:(j + 1) * 128], src.t[:, j * 128:(j + 1) * 128], identb.t[:],
                       [src.tr, identb.tr], [psT.tr])
                cp(P, evac_eng, dstT.t[:].rearrange("p k t -> p (k t)"), psT.t[:, 0:HW], [psT.tr], [dstT.tr])

            def half_job(it, s, hf, z_):
                b = hb[hf]
                c0 = hf * HW
                psA, psB, psC, psT = b.psA, b.psB, b.psC, b.psT
                for ps, col in ((psA, 0), (psB, D), (psC, 2 * D)):
                    for kk in range(8):
                        mm(P, ps.t[:], z_.t[:, kk, :], wres.t[:, kk, col + c0:col + c0 + HW], kk == 0, kk == 7,
                           [z_.tr, wres.tr], [ps.tr])
                yield
                act(P, b.Qf.t[:], psA.t[:], AF.Silu, [psA.tr], [b.Qf.tr])
                act(P, b.S2.t[:], psB.t[:], AF.Sigmoid, [psB.tr], [b.S2.tr], scale=-1.0)
                cp(P, "dve", b.V.t[:], psC.t[:], [psC.tr], [b.V.tr])
                yield
                tt(P, "dve", b.Kf.t[:], b.S2.t[:], omlb.t[:, c0:c0 + HW], ALU.mult, [b.S2.tr, omlb.tr], [b.Kf.tr])
                yield
                act(P, b.Gl.t[:], b.Kf.t[:], AF.Ln, [b.Kf.tr], [b.Gl.tr], scale=-1.0, bias=1.0)
                yield
                for ps, ti in ((psA, tb + 1), (psB, tb + 0), (psC, tb + 2)):
                    mm(P, ps.t[:], tri.t[:, ti, :], b.Gl.t[:], True, True, [tri.tr, b.Gl.tr], [ps.tr])
                yield
                act(P, b.E[0].t[:], psA.t[:], AF.Exp, [psA.tr], [b.E[0].tr])
                act(P, b.E[1].t[:], psA.t[:], AF.Exp, [psA.tr], [b.E[1].tr], scale=-1.0)
                yield
                tt(P, "dve", b.Qe.t[:], b.Qf.t[:], b.E[0].t[:], ALU.mult, [b.Qf.tr, b.E[0].tr], [b.Qe.tr])
                tt(P, "pool", b.Ke.t[:], b.Kf.t[:], b.E[1].t[:], ALU.mult, [b.Kf.tr, b.E[1].tr], [b.Ke.tr])
                yield
                act(P, b.E[0].t[:], psB.t[:], AF.Exp, [psB.tr], [b.E[0].tr])
                act(P, b.E[1].t[:], psC.t[:], AF.Exp, [psC.tr], [b.E[1].tr])
                yield
                tt(P, "dve", b.Qb.t[:], b.Qf.t[:], b.E[0].t[:], ALU.mult, [b.Qf.tr, b.E[0].tr], [b.Qb.tr])
                tt(P, "pool", b.Kend.t[:], b.Kf.t[:], b.E[1].t[:], ALU.mult, [b.Kf.tr, b.E[1].tr], [b.Kend.tr])
                yield
                tr4(psT, b.Qe, b.QeT, "act")
                yield
                tr4(psT, b.Ke, b.KeT, "dve")
                yield
                tr4(psT, b.Qb, b.QbT, "act")
                yield
                for h in range(4):
                    mm(P, psA.t[:, 256 + 2 * h:258 + 2 * h], b.Gl.t[:, h * 128:(h + 1) * 128], cib.t[:], True, True,
                       [b.Gl.tr, cib.tr], [psA.tr])
                for h in range(4):
                    for c in range(2):
                        mm(P, psA.t[c * 64:(c + 1) * 64, h * 64:(h + 1) * 64], b.KeT.t[:, h, c * 64:(c + 1) * 64],
                           b.QeT.t[:, h, c * 64:(c + 1) * 64], True, True, [b.KeT.tr, b.QeT.tr], [psA.tr])
                yield
                act(P, b.eb.t[:].rearrange("p h c -> p (h c)"), psA.t[:, 256:264], AF.Exp, [psA.tr], [b.eb.tr])
                tt(P, "dve", b.Asb.t[:], psA.t[:, 0:256].rearrange("p (h t) -> p h t", h=4),
                   maskb.t[:, (1 if dn else 0), :].unsqueeze(1).to_broadcast([128, 4, 64]), ALU.mult,
                   [psA.tr, maskb.tr], [b.Asb.tr])
                yield
                for c in corder:
                    r0, r1 = c * 64, (c + 1) * 64
                    for h in range(4):
                        hs = slice(h * 128, (h + 1) * 128)
                        mm(P, psB.t[r0:r1, hs], b.Asb.t[r0:r1, h, :], b.V.t[r0:r1, hs], True, False,
                           [b.Asb.tr, b.V.tr], [psB.tr])
                        mm(P, psB.t[r0:r1, hs], b.QbT.t[:, h, r0:r1], b.Sbf.t[:, h, :], False, True,
                           [b.QbT.tr, b.Sbf.tr], [psB.tr])
                    for h in range(4):
                        hs = slice(h * 128, (h + 1) * 128)
                        mm(P, psC.t[:, hs], b.Kend.t[r0:r1, hs], b.V.t[r0:r1, hs], True, True,
                           [b.Kend.tr, b.V.tr], [psC.tr])
                    tt(P, "dve", b.St.t[:], b.St.t[:], b.eb.t[:, :, c:c + 1].to_broadcast([128, 4, 128]), ALU.mult,
                       [b.St.tr, b.eb.tr], [b.St.tr])
                    yield
                    tt(P, "dve", b.St.t[:].rearrange("p h v -> p (h v)"), b.St.t[:].rearrange("p h v -> p (h v)"),
                       psC.t[:], ALU.add, [b.St.tr, psC.tr], [b.St.tr])
                    yield
                    cp(P, "act", b.Sbf.t[:], b.St.t[:], [b.St.tr], [b.Sbf.tr])
                    yield
                ob = b.osb[it % 2]
                cp(P, "act", ob.t[:], psB.t[:], [psB.tr], [ob.tr])
                yield
                if dn:
                    dma(P, "sp", ODN[s][:, c0:c0 + HW], ob.t[:], [ob.tr], [t_ODN[s]])
                else:
                    dma(P, "sp", OUP[s][:, c0:c0 + HW], ob.t[:], [ob.tr], [t_OUP[s]])

            for it, s in enumerate(order):
                z_ = zT[it % 2]
                if dn:
                    x_ = xt[it % 2]
                    r = 1 if s < 2 else 0
                    dma(P, "sp", x_.t[:], xs[s], [], [x_.tr])
                    make_z(x_, mods[r][0], mods[r][1], junk, ssz, tmpz, zb)
                    transpose_to(hb[0].psT, zb, D, z_, "act")
                    dma(P, "sp", ZT[s], z_.t[:], [z_.tr], [t_ZT[s]])
                else:
                    dma(P, "sp", z_.t[:], ZT[s], [t_ZT[s]], [z_.tr])
                zipper([half_job(it, s, 0, z_), half_job(it, s, 1, z_)])
            P.end_phase()

    hgrn_pass(True)
    if stop_after <= 1:
        return finish(k, nc, P)
    hgrn_pass(False)
    if stop_after <= 2:
        return finish(k, nc, P)

    with ExitStack() as S:
        wres = P.buf(S, "wres3", [128, 8, 3 * D], BF16)
        load_w_bf16(wres, 0, w0, 5 * D, 2 * D)
        load_w_bf16(wres, 2 * D, w0, 4 * D, D)
        hgb = load_bc(S, "hgb", hgn[0:1, :])
        SGA = [P.buf(S, "SGA", [128, D], F32) for _ in range(2)]
        oup = P.buf(S, "oup", [128, D], F32)
        odn = P.buf(S, "odn", [128, D], F32)
        ssb = P.buf(S, "ssb3", [128, 8], F32)
        tmpb = P.buf(S, "tmpb3", [128, 8], F32)
        ow = P.buf(S, "ow0", [128, 16, D], BF16)
        load_w_bf16(ow, 0, out_w[0], 0, D, nk=16)
        pw = P.buf(S, "pw", [128, 4, 2, 256], BF16)
        for g in range(4):
            for j in range(2):
                dma(P, "pool", pw.t[:, g, j, :], pool_w[g, j * 128:(j + 1) * 128, :], [], [pw.tr])
        bands = P.buf(S, "bands", [128, 20, 128], BF16)
        for j in range(20):
            dma(P, "pool", bands.t[:, j, :], bands_d[j], [], [bands.tr])
        psc = load_bc(S, "psc", pool_scale[0:1, :])
        gts = []
        for r in range(2):
            gts.append(load_bc(S, "gt0", MS[0, r:r + 1, 2 * D:3 * D]))
        zT = [P.buf(S, "zT3", [128, 8, 128], BF16) for _ in range(2)]
        U = [P.buf(S, "U", [128, D], BF16) for _ in range(4)]
        SGB = [P.buf(S, "SGB", [128, D], F32) for _ in range(3)]
        SGA3 = [P.buf(S, "SGA3", [128, D], F32) for _ in range(3)]
        ppT = P.buf(S, "ppT", [128, 8, 128], BF16)
        yb = P.buf(S, "yb", [128, 2 * D], BF16)
        yT = P.buf(S, "yT", [128, 16, 128], BF16)
        bt = P.buf(S, "bt", [128, D], F32)
        bt2 = P.buf(S, "bt2", [128, D], F32)
        xt = [P.buf(S, "xt3", [128, D], F32) for _ in range(2)]
        psM = [P.pbuf(S, "ps3M", [128, 512]) for _ in range(2)]
        psC = P.pbuf(S, "ps3C", [128, D])
        psD = P.pbuf(S, "ps3D", [128, D])
        psT = P.pbuf(S, "ps3T", [128, D], BF16)

        def seq_info(s):
            return (0, 1) if s < 2 else (2, NS - 1)

        def main_gen(s):
            z_ = zT[s % 2]
            dma(P, "sp", z_.t[:], ZT[s], [t_ZT[s]], [z_.tr])
            gi = 0
            for col, kind in ((0, "u"), (D, "gb"), (2 * D, "ga")):
                for n in range(2):
                    ps = psM[gi % 2]
                    gi += 1
                    for kk in range(8):
                        mm(P, ps.t[:], z_.t[:, kk, :], wres.t[:, kk, col + n * 512:col + (n + 1) * 512],
                           kk == 0, kk == 7, [z_.tr, wres.tr], [ps.tr])
                    yield
                    cs = slice(n * 512, (n + 1) * 512)
                    if kind == "u":
                        cp(P, "dve", U[s % 4].t[:, cs], ps.t[:], [ps.tr], [U[s % 4].tr])
                    elif kind == "gb":
                        act(P, SGB[s % 3].t[:, cs], ps.t[:], AF.Silu, [ps.tr], [SGB[s % 3].tr])
                    else:
                        act(P, SGA3[s % 3].t[:, cs], ps.t[:], AF.Silu, [ps.tr], [SGA3[s % 3].tr])
                    yield

        def post_gen(s, itp):
            lo, hi = seq_info(s)
            var = 0 if s == lo else (2 if s == hi else 1)
            dma(P, "act", oup.t[:], OUP[s], [t_OUP[s]], [oup.tr])
            dma(P, "act", odn.t[:], ODN[s], [t_ODN[s]], [odn.tr])
            x_ = xt[itp % 2]
            dma(P, "sp", x_.t[:], xs[s], [], [x_.tr])
            for cb in range(8):
                g = cb // 2
                terms = []
                if s > lo:
                    terms.append((U[(s - 1) % 4], g * 5 + 3))
                terms.append((U[s % 4], g * 5 + var))
                if s < hi:
                    terms.append((U[(s + 1) % 4], g * 5 + 4))
                for i, (ub, bi) in enumerate(terms):
                    mm(P, psC.t[:, cb * 128:(cb + 1) * 128], ub.t[:, cb * 128:(cb + 1) * 128], bands.t[:, bi, :],
                       i == 0, i == len(terms) - 1, [ub.tr, bands.tr], [psC.tr])
            yield
            cp(P, "act", ppT.t[:].rearrange("p k t -> p (k t)"), psC.t[:], [psC.tr], [ppT.tr])
            tt(P, "dve", oup.t[:], oup.t[:], odn.t[:], ALU.add, [oup.tr, odn.tr], [oup.tr])
            yield
            for g in range(4):
                for j in range(2):
                    mm(P, psC.t[:, g * 256:(g + 1) * 256], ppT.t[:, 2 * g + j, :], pw.t[:, g, j, :], j == 0, j == 1,
                       [ppT.tr, pw.tr], [psC.tr])
            tt(P, "pool", odn.t[:], oup.t[:], oup.t[:], ALU.mult, [oup.tr], [odn.tr])
            yield
            tt(P, "dve", bt.t[:], psC.t[:], psc.t[:], ALU.mult, [psC.tr, psc.tr], [bt.tr])
            P.op("dve", lambda e: e.tensor_reduce(out=ssb.t[:], in_=odn.t[:].rearrange("p (h v) -> p h v", h=8),
                                                 axis=AX.X, op=ALU.add), [odn.tr], [ssb.tr])
            yield
            tt(P, "pool", yb.t[:, D:2 * D], bt.t[:], SGB[s % 3].t[:], ALU.mult, [bt.tr, SGB[s % 3].tr], [yb.tr])
            ts(P, "dve", tmpb.t[:], ssb.t[:], 1.0 / 128, EPS, ALU.mult, ALU.add, [ssb.tr], [tmpb.tr])
            yield
            act(P, tmpb.t[:], tmpb.t[:], AF.Sqrt, [tmpb.tr], [tmpb.tr])
            yield
            P.op("dve", lambda e: e.reciprocal(out=ssb.t[:], in_=tmpb.t[:]), [tmpb.tr], [ssb.tr])
            yield
            tt(P, "dve", oup.t[:].rearrange("p (h v) -> p h v", h=8), oup.t[:].rearrange("p (h v) -> p h v", h=8),
               ssb.t[:].unsqueeze(2).to_broadcast([128, 8, 128]), ALU.mult, [oup.tr, ssb.tr], [oup.tr])
            yield
            tt(P, "pool", oup.t[:], oup.t[:], hgb.t[:], ALU.mult, [oup.tr, hgb.tr], [oup.tr])
            yield
            tt(P, "dve", yb.t[:, 0:D], oup.t[:], SGA3[s % 3].t[:], ALU.mult, [oup.tr, SGA3[s % 3].tr], [yb.tr])
            yield
            for rnd in range(2):
                for j in range(8):
                    jj = rnd * 8 + j
                    tp(P, psT.t[:, j * 128:(j + 1) * 128], yb.t[:, jj * 128:(jj + 1) * 128], identb.t[:],
                       [yb.tr, identb.tr], [psT.tr])
                yield
                cp(P, "act" if rnd == 0 else "dve", yT.t[:, rnd * 8:(rnd + 1) * 8, :].rearrange("p k t -> p (k t)"),
                   psT.t[:], [psT.tr], [yT.tr])
                yield
            for n in range(2):
                for kk in range(16):
                    mm(P, psD.t[:, n * 512:(n + 1) * 512], yT.t[:, kk, :], ow.t[:, kk, n * 512:(n + 1) * 512],
                       kk == 0, kk == 15, [yT.tr, ow.tr], [psD.tr])
            yield
            gt = gts[1 if s < 2 else 0]
            tt(P, "dve", bt2.t[:], psD.t[:], gt.t[:], ALU.mult, [psD.tr, gt.tr], [bt2.tr])
            yield
            tt(P, "pool", x_.t[:], x_.t[:], bt2.t[:], ALU.add, [x_.tr, bt2.tr], [x_.tr])
            yield
            dma(P, "sp", H1[s], x_.t[:], [x_.tr], [t_H1[s]])

        itp = 0
        for s in range(NS + 2):
            gens = []
            if s < NS:
                gens.append(main_gen(s))
            if s >= 2:
                gens.append(post_gen(s - 2, itp))
                itp += 1
            zipper(gens)
        P.end_phase()
    if stop_after <= 3:
        return finish(k, nc, P)

    cqT = P.buf(G, "cqT", [128, 4, NQ], BF16)
    ckvT = P.buf(G, "ckvT", [128, 2, NK], BF16)
    krT = P.buf(G, "krT2", [128, NK // 2], BF16)
    t_cq = [Tr() for _ in range(8)]
    t_kv = [Tr() for _ in range(17)]

    with ExitStack() as S:
        wres = P.buf(S, "wres4", [128, 8, 2944], BF16)
        load_w_bf16(wres, 0, w1, 0, 2944)
        mods = mod_tiles(S, 1, (False, False))
        qag = load_bc(S, "qag", qa_g[0:1, :])
        kvag = load_bc(S, "kvag", kva_g[0:1, :])
        xt = [P.buf(S, "xt4", [128, D], F32) for _ in range(2)]
        junk = P.buf(S, "junk4", [128, D], BF16)
        ssb = P.buf(S, "ssb4", [128, 8], F32)
        tmpb = P.buf(S, "tmpb4", [128, 8], F32)
        zb = P.buf(S, "zb4", [128, D], BF16)
        zT = [P.buf(S, "zT4", [128, 8, 128], BF16) for _ in range(2)]
        kvb = P.buf(S, "kvb", [128, 256], BF16)
        rc = [P.buf(S, "rc", [128, 64], F32) for _ in range(2)]
        rs = [P.buf(S, "rs", [128, 64], F32) for _ in range(2)]
        krf = P.buf(S, "krf", [128, 64], F32)
        krf2 = P.buf(S, "krf2", [128, 64], F32)
        krb = P.buf(S, "krb", [128, 64], BF16)
        cqb = P.buf(S, "cqb", [128, 512], BF16)
        sgf = P.buf(S, "sgf", [128, 2 * D], F32)
        sgT = [P.buf(S, "sgT", [128, 16, 128], F32) for _ in range(1)]
        psK = P.pbuf(S, "ps4K", [128, 512])
        psQ = P.pbuf(S, "ps4Q", [128, 512])
        psG = [P.pbuf(S, "ps4G", [128, D]) for _ in range(2)]
        psT = P.pbuf(S, "ps4T", [128, D], BF16)
        psF = P.pbuf(S, "ps4F", [128, 512])
        for s in range(NS):
            x_ = xt[s % 2]
            z_ = zT[s % 2]
            own = 2 <= s < 2 + NOWN
            r = 1 if s < 2 else 0
            dma(P, "sp", x_.t[:], H1[s], [t_H1[s]], [x_.tr])
            dma(P, "act", rc[s % 2].t[:], ropeK_cos[s], [], [rc[s % 2].tr])
            dma(P, "act", rs[s % 2].t[:], ropeK_sin[s], [], [rs[s % 2].tr])
            make_z(x_, mods[r][0], mods[r][1], junk, ssb, tmpb, zb)
            transpose_to(psT, zb, D, z_, "act")
            for kk in range(8):
                mm(P, psK.t[:, 0:384], z_.t[:, kk, :], wres.t[:, kk, 512:896], kk == 0, kk == 7,
                   [z_.tr, wres.tr], [psK.tr])
            act(P, junk.t[:, 0:256], psK.t[:, 0:256], AF.Square, [psK.tr], [junk.tr, ssb.tr], accum_out=ssb.t[:, 0:1])
            rstd_from_ss(P, ssb, tmpb, 256)
            stt(P, "dve", kvb.t[:], psK.t[:, 0:256], ssb.t[:, 0:1], kvag.t[:], ALU.mult, ALU.mult,
                [psK.tr, ssb.tr, kvag.tr], [kvb.tr])
            tt(P, "dve", krf.t[:], psK.t[:, 256:320], rc[s % 2].t[:], ALU.mult, [psK.tr, rc[s % 2].tr], [krf.tr])
            tt(P, "dve", krf2.t[:], psK.t[:, 320:384], rs[s % 2].t[:], ALU.mult, [psK.tr, rs[s % 2].tr], [krf2.tr])
            tt(P, "pool", krb.t[:], krf.t[:], krf2.t[:], ALU.add, [krf.tr, krf2.tr], [krb.tr])
            kc_tr = t_kv[s // 4]
            for j in range(2):
                tp(P, psT.t[:, j * 128:(j + 1) * 128], kvb.t[:, j * 128:(j + 1) * 128], identb.t[:],
                   [kvb.tr, identb.tr], [psT.tr])
            hp = s % 2
            tp(P, psT.t[hp * 64:(hp + 1) * 64, 256:384], krb.t[:], identb.t[:], [krb.tr, identb.tr], [psT.tr])
            for j in range(2):
                cp(P, "act", ckvT.t[:, j, s * 128:(s + 1) * 128], psT.t[:, j * 128:(j + 1) * 128], [psT.tr], [kc_tr])
            cp(P, "dve", krT.t[hp * 64:(hp + 1) * 64, (s // 2) * 128:(s // 2 + 1) * 128],
               psT.t[hp * 64:(hp + 1) * 64, 256:384], [psT.tr], [kc_tr])
            if own:
                q0 = (s - 2) * 128
                for kk in range(8):
                    mm(P, psQ.t[:], z_.t[:, kk, :], wres.t[:, kk, 0:512], kk == 0, kk == 7, [z_.tr, wres.tr], [psQ.tr])
                act(P, junk.t[:, 0:512], psQ.t[:], AF.Square, [psQ.tr], [junk.tr, ssb.tr], accum_out=ssb.t[:, 1:2])
                ts(P, "dve", tmpb.t[:, 1:2], ssb.t[:, 1:2], 1.0 / 512, EPS, ALU.mult, ALU.add, [ssb.tr], [tmpb.tr])
                act(P, tmpb.t[:, 1:2], tmpb.t[:, 1:2], AF.Sqrt, [tmpb.tr], [tmpb.tr])
                P.op("dve", lambda e: e.reciprocal(out=ssb.t[:, 1:2], in_=tmpb.t[:, 1:2]), [tmpb.tr], [ssb.tr])
                stt(P, "dve", cqb.t[:], psQ.t[:], ssb.t[:, 1:2], qag.t[:], ALU.mult, ALU.mult,
                    [psQ.tr, ssb.tr, qag.tr], [cqb.tr])
                for j in range(4):
                    tp(P, psT.t[:, 512 + j * 128:512 + (j + 1) * 128], cqb.t[:, j * 128:(j + 1) * 128], identb.t[:],
                       [cqb.tr, identb.tr], [psT.tr])
                for j in range(4):
                    cp(P, "act" if j % 2 == 0 else "dve", cqT.t[:, j, q0:q0 + 128],
                       psT.t[:, 512 + j * 128:512 + (j + 1) * 128], [psT.tr], [t_cq[(s - 2) // 4]])
                for half in range(2):
                    pg = psG[half]
                    for n in range(2):
                        c0 = 896 + half * D + n * 512
                        for kk in range(8):
                            mm(P, pg.t[:, n * 512:(n + 1) * 512], z_.t[:, kk, :], wres.t[:, kk, c0:c0 + 512],
                               kk == 0, kk == 7, [z_.tr, wres.tr], [pg.tr])
                    act(P, sgf.t[:, half * D:(half + 1) * D], pg.t[:], AF.Silu, [pg.tr], [sgf.tr])
                st_ = sgT[0]
                for hq in range(4):
                    for j in range(4):
                        h = hq * 4 + j
                        tp(P, psF.t[:, j * 128:(j + 1) * 128], sgf.t[:, h * 128:(h + 1) * 128], identf.t[:],
                           [sgf.tr, identf.tr], [psF.tr])
                    cp(P, "dve" if hq % 2 == 0 else "act", st_.t[:, hq * 4:(hq + 1) * 4, :].rearrange("p h t -> p (h t)"),
                       psF.t[:], [psF.tr], [st_.tr])
                dma(P, "sp", SGT[:, :, q0:q0 + 128].rearrange("h v q -> v h q"), st_.t[:], [st_.tr], [t_SGT])
        P.end_phase()
    if stop_after <= 4:
        return finish(k, nc, P)

    with ExitStack() as S:
        qw = [P.buf(S, "qw", [128, 4, 256], BF16) for _ in range(2)]
        kw = [P.buf(S, "kw", [128, 2, 256], BF16) for _ in range(2)]
        KnT = P.buf(S, "KnT", [128, NK], BF16)
        Vh = P.buf(S, "Vh", [128, NS, 128], BF16)
        QnT = P.buf(S, "QnT", [128, NQ], BF16)
        QrT = P.buf(S, "QrT2", [128, NQ], BF16)
        sgt = P.buf(S, "sgt", [128, NQ], F32)
        rq = [[P.buf(S, "rq", [128, 512], F32) for _ in range(2)] for _ in range(2)]
        q1 = P.buf(S, "q1", [128, 512], F32)
        q2 = P.buf(S, "q2", [128, 512], F32)
        PT = [P.buf(S, "PT", [128, 1024], BF16) for _ in range(3)]
        laccA = P.buf(S, "laccA", [128, 1024], F32)
        laccB = P.buf(S, "laccB", [128, 1024], F32)
        onesf = P.buf(S, "onesf", [128, 128], F32)
        rl = P.buf(S, "rl", [128, 512], F32)
        yo = [P.buf(S, "yo", [128, 512], BF16) for _ in range(2)]
        yf = P.buf(S, "yf", [128, 512], F32)
        P.op("pool", lambda e: e.memset(onesf.t[:], 1.0), [], [onesf.tr])
        psS = [P.pbuf(S, "ps5S", [128, 1024]) for _ in range(3)]
        psO = P.pbuf(S, "ps5O", [128, 512])
        psL = P.pbuf(S, "ps5L", [128, 512])

        class _PX:
            def __init__(self, b, c0):
                self.tr = b.tr
                self.t = b.t[:, c0:c0 + 512]
        psX = [_PX(psS[i], j * 512) for j in range(2) for i in range(3)]
        t_kn = [Tr() for _ in range(17)]
        t_v = [Tr() for _ in range(17)]
        t_qn = [Tr() for _ in range(8)]
        ix = 0
        ip = 0
        for h in range(HEADS):
            qw_, kw_ = qw[h % 2], kw[h % 2]
            for kk in range(4):
                dma(P, "pool", qw_.t[:, kk, :], qbw[kk * 128:(kk + 1) * 128, h * 256:(h + 1) * 256], [], [qw_.tr])
            for kk in range(2):
                dma(P, "pool", kw_.t[:, kk, :], kvbw[kk * 128:(kk + 1) * 128, h * 256:(h + 1) * 256], [], [kw_.tr])
            dma(P, "sp", sgt.t[:], SGT[h], [t_SGT], [sgt.tr])
            for kc in range(17):
                c0 = kc * 512
                cn = min(512, NK - c0)
                px = psX[ix % 6]
                ix += 1
                for kk in range(2):
                    mm(P, px.t[:, 0:cn], kw_.t[:, kk, 0:128], ckvT.t[:, kk, c0:c0 + cn], kk == 0, kk == 1,
                       [kw_.tr, t_kv[kc]], [px.tr])
                cp(P, "dve" if kc % 2 == 0 else "pool" if False else "dve", KnT.t[:, c0:c0 + cn], px.t[:, 0:cn],
                   [px.tr], [t_kn[kc]])
                px = psX[ix % 6]
                ix += 1
                nt = cn // 128
                for j in range(nt):
                    s = kc * 4 + j
                    for kk in range(2):
                        mm(P, px.t[:, j * 128:(j + 1) * 128], ckvT.t[:, kk, s * 128:(s + 1) * 128], kw_.t[:, kk, 128:256],
                           kk == 0, kk == 1, [kw_.tr, t_kv[kc]], [px.tr])
                cp(P, "act", Vh.t[:, kc * 4:kc * 4 + nt, :].rearrange("p s v -> p (s v)"), px.t[:, 0:cn],
                   [px.tr], [t_v[kc]])
            for qc in range(8):
                c0 = qc * 512
                px = psX[ix % 6]
                ix += 1
                for kk in range(4):
                    mm(P, px.t[:], qw_.t[:, kk, 0:128], cqT.t[:, kk, c0:c0 + 512], kk == 0, kk == 3,
                       [qw_.tr, t_cq[qc]], [px.tr])
                cp(P, "dve", QnT.t[:, c0:c0 + 512], px.t[:], [px.tr], [t_qn[qc]])
                rq_ = rq[qc % 2]
                for hp in range(2):
                    dma(P, "sp", rq_[0].t[hp * 64:(hp + 1) * 64, :], ropeQ_cosT[:, c0:c0 + 512], [], [rq_[0].tr])
                    dma(P, "sp", rq_[1].t[hp * 64:(hp + 1) * 64, :], ropeQ_sinT[:, c0:c0 + 512], [], [rq_[1].tr])
                px = psX[ix % 6]
                ix += 1
                for hp in range(2):
                    for kk in range(4):
                        mm(P, px.t[hp * 64:(hp + 1) * 64, :], qw_.t[:, kk, 128:192], cqT.t[:, kk, c0:c0 + 512], kk == 0, kk == 3,
                           [qw_.tr, t_cq[qc]], [px.tr])
                tt(P, "dve", q1.t[:], px.t[:], rq_[0].t[:], ALU.mult, [px.tr, rq_[0].tr], [q1.tr])
                px = psX[ix % 6]
                ix += 1
                for hp in range(2):
                    for kk in range(4):
                        mm(P, px.t[hp * 64:(hp + 1) * 64, :], qw_.t[:, kk, 192:256], cqT.t[:, kk, c0:c0 + 512], kk == 0, kk == 3,
                           [qw_.tr, t_cq[qc]], [px.tr])
                tt(P, "dve", q2.t[:], px.t[:], rq_[1].t[:], ALU.mult, [px.tr, rq_[1].tr], [q2.tr])
                tt(P, "pool", QrT.t[:, c0:c0 + 512], q1.t[:], q2.t[:], ALU.add, [q1.tr, q2.tr], [t_qn[qc]])
            for qc in range(8):
                c0 = qc * 512
                NP = NS // 2
                for p in range(NP + 2):
                    if p < NP:
                        ps = psS[p % 3]
                        pt = PT[p % 3]
                        for j in range(2):
                            s = 2 * p + j
                            kc = s // 4
                            mm(P, ps.t[:, j * 512:(j + 1) * 512], KnT.t[:, s * 128:(s + 1) * 128], QnT.t[:, c0:c0 + 512],
                               True, False, [t_kn[kc], t_qn[qc]], [ps.tr])
                        for j in range(2):
                            s = 2 * p + j
                            kc = s // 4
                            mm(P, ps.t[:, j * 512:(j + 1) * 512], krT.t[j * 64:(j + 1) * 64, p * 128:(p + 1) * 128],
                               QrT.t[j * 64:(j + 1) * 64, c0:c0 + 512], False, True, [t_kv[kc], t_qn[qc]], [ps.tr])
                        act(P, pt.t[:], ps.t[:], AF.Exp, [ps.tr], [pt.tr], scale=MLA_SCALE)
                    if p >= 2:
                        q = p - 2
                        pt0 = PT[q % 3]
                        for j in range(2):
                            s0 = 2 * q + j
                            mm(P, psO.t[:], Vh.t[:, s0, :], pt0.t[:, j * 512:(j + 1) * 512], s0 == 0, s0 == NS - 1,
                               [t_v[s0 // 4], pt0.tr], [psO.tr])
                        la, eng = laccA, "dve"
                        if q == 0:
                            cp(P, eng, la.t[:], pt0.t[:], [pt0.tr], [la.tr])
                        else:
                            tt(P, eng, la.t[:], la.t[:], pt0.t[:], ALU.add, [la.tr, pt0.tr], [la.tr])
                for j, (la, c1) in enumerate(((laccA, 0), (laccA, 512))):
                    mm(P, psL.t[:], onesf.t[:], la.t[:, c1:c1 + 512], j == 0, j == 1, [onesf.tr, la.tr], [psL.tr])
                P.op("dve", lambda e: e.reciprocal(out=rl.t[:], in_=psL.t[:]), [psL.tr], [rl.tr])
                tt(P, "dve", yf.t[:], psO.t[:], rl.t[:], ALU.mult, [psO.tr, rl.tr], [yf.tr])
                yo_ = yo[qc % 2]
                tt(P, "pool", yo_.t[:], yf.t[:], sgt.t[:, c0:c0 + 512], ALU.mult, [yf.tr, sgt.tr], [yo_.tr])
                dma(P, "sp", YT[h, :, c0:c0 + 512], yo_.t[:], [yo_.tr], [t_YT[h]])
        P.end_phase()
    if stop_after <= 5:
        return finish(k, nc, P)

    with ExitStack() as S:
        ow = P.buf(S, "ow1", [128, 16, D], BF16)
        load_w_bf16(ow, 0, out_w[1], 0, D, nk=16)
        gt1 = load_bc(S, "gt1", MS[1, 0:1, 2 * D:3 * D])
        fgb = load_bc(S, "fgb", final_g[0:1, :])
        yT = [P.buf(S, "yT6", [128, 16, 128], BF16) for _ in range(2)]
        xt = [P.buf(S, "xt6", [128, D], F32) for _ in range(2)]
        bt = P.buf(S, "bt6", [128, D], F32)
        junk = P.buf(S, "junk6", [128, D], BF16)
        ssb = P.buf(S, "ssb6", [128, 8], F32)
        tmpb = P.buf(S, "tmpb6", [128, 8], F32)
        psB = [P.pbuf(S, "ps6B", [128, D]) for _ in range(2)]
        for i in range(NOWN):
            s = i + 2
            q0 = i * 128
            y_ = yT[i % 2]
            x_ = xt[i % 2]
            pb = psB[i % 2]
            dma(P, "sp", y_.t[:], YT[:, :, q0:q0 + 128].rearrange("h v q -> v h q"), t_YT, [y_.tr])
            dma(P, "act", x_.t[:], H1[s], [t_H1[s]], [x_.tr])
            for n in range(2):
                for kk in range(16):
                    mm(P, pb.t[:, n * 512:(n + 1) * 512], y_.t[:, kk, :], ow.t[:, kk, n * 512:(n + 1) * 512],
                       kk == 0, kk == 15, [y_.tr, ow.tr], [pb.tr])
            tt(P, "dve", bt.t[:], pb.t[:], gt1.t[:], ALU.mult, [pb.tr, gt1.tr], [bt.tr])
            tt(P, "pool", x_.t[:], x_.t[:], bt.t[:], ALU.add, [x_.tr, bt.tr], [x_.tr])
            act(P, junk.t[:], x_.t[:], AF.Square, [x_.tr], [junk.tr, ssb.tr], accum_out=ssb.t[:, 0:1])
            rstd_from_ss(P, ssb, tmpb, D)
            stt(P, "dve", x_.t[:], x_.t[:], ssb.t[:, 0:1], fgb.t[:], ALU.mult, ALU.mult, [x_.tr, ssb.tr, fgb.tr], [x_.tr])
            dma(P, "sp", out_d[i], x_.t[:], [x_.tr], [])
        P.end_phase()
    return finish(k, nc, P)


def finish(k, nc, P):
    if any(len(v) for v in P.ops.values()):
        P.end_phase()
    P.close()
    return nc


def _consts(mir):
    ident = np.eye(128, dtype=np.float32)
    s = np.arange(128)[:, None]
    t = np.arange(128)[None, :]
    same = (s // 64) == (t // 64)
    tb_up = (same & (s <= t)).astype(np.float32)
    tb_dn = (same & (s >= t)).astype(np.float32)
    mid = (t // 64) * 64 + 32
    td_up = tb_up - tb_up[:, mid[0]]
    td_dn = tb_dn - tb_dn[:, mid[0]]
    tr_up = (same & (s > t)).astype(np.float32)
    tr_dn = (same & (s < t)).astype(np.float32)
    tris = np.stack([tb_up, td_up, tr_up, tb_dn, td_dn, tr_dn]).astype(np.float32)
    ci = np.zeros((128, 2), np.float32)
    ci[:64, 0] = 1
    ci[64:, 1] = 1
    p = np.arange(128)[:, None] % 64
    j = np.arange(64)[None, :]
    masks = np.stack([(p <= j), (p >= j)]).astype(np.float32)
    L = 384
    bands = np.zeros((20, 128, 128), np.float32)
    for g, w in enumerate((2, 4, 8, 16)):
        Bm = np.zeros((L, L), np.float64)
        for jl in range(L):
            to = (L - 1 - jl) if mir else jl
            lo = max(0, to - w // 2)
            hi = min(L, to + w // 2)
            cnt = hi - lo
            for so in range(lo, hi):
                sl = (L - 1 - so) if mir else so
                Bm[sl, jl] += 1.0 / cnt
            Bm[jl, jl] -= 1.0
        bands[g * 5 + 0] = Bm[0:128, 0:128]
        bands[g * 5 + 1] = Bm[128:256, 128:256]
        bands[g * 5 + 2] = Bm[256:384, 256:384]
        bands[g * 5 + 3] = Bm[0:128, 128:256]
        bands[g * 5 + 4] = Bm[256:384, 128:256]
    return ident, tris, ci, masks, bands


def _rope_tables(mir):
    inv = (10000.0 ** (-2.0 * np.arange(16, dtype=np.float32) / 32.0)).astype(np.float32)
    jl = np.arange(T)
    to = (T - 1 - jl) if mir else jl
    pr = (to // 64).astype(np.float32)
    pc = (to % 64).astype(np.float32)
    ang_r = pr[:, None] * inv[None, :]
    ang_c = pc[:, None] * inv[None, :]
    cr, sr = np.cos(ang_r).astype(np.float32), np.sin(ang_r).astype(np.float32)
    cc, sc = np.cos(ang_c).astype(np.float32), np.sin(ang_c).astype(np.float32)
    cos = np.concatenate([cr, cr, cc, cc], axis=1)
    sin = np.concatenate([-sr, sr, -sc, sc], axis=1)
    cosK = np.concatenate([np.ones((TC, 64), np.float32), cos], axis=0).reshape(NS, 128, 64)
    sinK = np.concatenate([np.zeros((TC, 64), np.float32), sin], axis=0).reshape(NS, 128, 64)
    cosQ = np.ascontiguousarray(cos[:NQ].T)
    sinQ = np.ascontiguousarray(sin[:NQ].T)
    return cosK, sinK, cosQ, sinQ


_PERM64 = np.concatenate([np.arange(16, 32), np.arange(0, 16), np.arange(48, 64), np.arange(32, 48)])

_NC_CACHE = {}


def _prep_inputs(x, c, ctx, c_ctx, ada_w, ada_b, norm_g, out_w, ev_in_w, hg_lb, hg_norm_g, pool_w,
                 pool_scale, od_in_w, qa_norm_g, qb_w, kva_norm_g, kvb_w, final_norm_g):
    f = lambda a: np.ascontiguousarray(np.asarray(a, dtype=np.float32))
    x, c, ctx, c_ctx = f(x), f(c), f(ctx), f(c_ctx)
    ev = f(ev_in_w)[0]
    od = f(od_in_w)[0]
    qb = f(qb_w)[0].reshape(512, HEADS, 192)
    qb_ext = np.concatenate([qb, qb[:, :, 128:][:, :, _PERM64]], axis=2).reshape(512, HEADS * 256)
    kr = od[:, 768:832]
    w1 = np.concatenate([od[:, 0:768], kr, kr[:, _PERM64], od[:, 832:]], axis=1)
    lb = f(hg_lb)
    shared = dict(ada_w=f(ada_w), ada_b=f(ada_b), norm_g=f(norm_g), final_g=f(final_norm_g)[None, :],
                  hgn=f(hg_norm_g), pool_w=f(pool_w)[0], pool_scale=f(pool_scale), out_w=f(out_w),
                  w1=np.ascontiguousarray(w1), qa_g=f(qa_norm_g), kva_g=f(kva_norm_g),
                  qbw=np.ascontiguousarray(qb_ext), kvbw=f(kvb_w)[0])
    per_mir = {}
    for mir in (False, True):
        ident, tris, ci, masks, bands = _consts(mir)
        cosK, sinK, cosQ, sinQ = _rope_tables(mir)
        q_, ff, fb, i_, ga, u_, gb = [ev[:, j * D:(j + 1) * D] for j in range(7)]
        w0 = np.concatenate([q_, fb, ff, i_, ga, u_, gb] if mir else [q_, ff, fb, i_, ga, u_, gb], axis=1)
        lbr = lb[::-1] if mir else lb
        per_mir[mir] = dict(ident=ident, tris=tris, ci=ci, masks=masks, bands=bands, ropeK_cos=cosK, ropeK_sin=sinK,
                            ropeQ_cosT=cosQ, ropeQ_sinT=sinQ, w0=np.ascontiguousarray(w0),
                            lbraw=np.ascontiguousarray(lbr))
    in_maps = []
    for core in range(8):
        b, half = core // 2, core % 2
        mir = half == 1
        xl = x[b][::-1] if mir else x[b]
        cl = ctx[b][::-1] if mir else ctx[b]
        xs = np.concatenate([cl, xl], axis=0).reshape(NS, 128, D)
        cvec = np.stack([c[b], c_ctx], axis=0)
        cvT = np.ascontiguousarray(cvec.reshape(2, 8, 128).transpose(2, 1, 0))
        m = dict(xs=np.ascontiguousarray(xs), cvT=cvT)
        m.update(shared)
        m.update(per_mir[mir])
        in_maps.append(m)
    return in_maps


def kernel(x, c, ctx, c_ctx, ada_w, ada_b, norm_g, out_w, ev_in_w, hg_lb, hg_norm_g, pool_w,
           pool_scale, od_in_w, qa_norm_g, qb_w, kva_norm_g, kvb_w, final_norm_g):
    in_maps = _prep_inputs(x, c, ctx, c_ctx, ada_w, ada_b, norm_g, out_w, ev_in_w, hg_lb, hg_norm_g, pool_w,
                           pool_scale, od_in_w, qa_norm_g, qb_w, kva_norm_g, kvb_w, final_norm_g)
    if "nc" not in _NC_CACHE:
        _NC_CACHE["nc"] = build_program()
    nc = _NC_CACHE["nc"]
    res = run_bass_kernel_spmd(nc, in_maps, core_ids=list(range(8)))
    out = np.zeros((4, T, D), np.float32)
    for core in range(8):
        b, half = core // 2, core % 2
        o = np.asarray(res.results[core]["out"]).reshape(NQ, D)
        if half == 1:
            out[b, NQ:] = o[::-1]
        else:
            out[b, :NQ] = o
    return out
```
